# Optimizing a Trainium2 kernel written in Bass

```python
import jax
import jax.numpy as jnp
from jax import lax
import numpy as np

D_MODEL = 1024
BATCH = 32
SEQ = 2048
DEPTH = 2
DEC_BATCH = 8
DEC_SEQ = 16
PAST_LEN = 4096

CHUNK = 64
N_MEM = 256
EPS = 1e-6

A_HEADS = 8
A_HEAD_DIM = 64
A_WIDTH = A_HEADS * A_HEAD_DIM
A_PAST_CHUNKS = 8
A_WINDOW = A_PAST_CHUNKS * CHUNK
A_BAND = A_WINDOW + CHUNK
A_MAX_REL = 128
A_N_REL = 2 * A_MAX_REL + 1

B_HEADS = 4
B_HEAD_DIM = 128
B_WIDTH = B_HEADS * B_HEAD_DIM
B_CONV = 4

C_HEADS = 16
C_HEAD_DIM = 64
C_INNER = C_HEADS * C_HEAD_DIM
C_GROUPS = 2
C_STATE = 128
C_XBC = C_INNER + 2 * C_GROUPS * C_STATE
C_CONV = 4

X_HEADS = 4
X_HEAD_DIM = D_MODEL // X_HEADS

D_FF = 2816
F_CONV = 3

IN_SPLITS = (3 * A_WIDTH, 3 * B_WIDTH, B_HEADS, B_HEADS, B_WIDTH, C_INNER, C_XBC, C_HEADS, 3 * D_MODEL)
N_IN = 3 * A_WIDTH + 3 * B_WIDTH + 2 * B_HEADS + B_WIDTH + C_INNER + C_XBC + C_HEADS + 3 * D_MODEL

kernel_name = 'hybrid_chunk_stream_encoder_step'


def rmsnorm(x, g):
    xf = x.astype(jnp.float32)
    y = xf * lax.rsqrt(jnp.mean(xf * xf, axis=-1, keepdims=True) + EPS)
    return (y * g.astype(jnp.float32)).astype(x.dtype)


def l2norm(x):
    xf = x.astype(jnp.float32)
    return xf * lax.rsqrt(jnp.sum(xf * xf, axis=-1, keepdims=True) + EPS)


def split_cols(z, sizes):
    idx = np.cumsum(np.array(sizes))[:-1].tolist()
    return jnp.split(z, idx, axis=-1)


def causal_dwconv(x, buf, w, b=None):
    k_w = w.shape[0]
    L = x.shape[1]
    xp = jnp.concatenate([buf.astype(x.dtype), x], axis=1)
    y = xp[:, 0:L] * w[0]
    for i in range(1, k_w):
        y = y + xp[:, i:i + L] * w[i]
    if b is not None:
        y = y + b
    return y, xp[:, xp.shape[1] - (k_w - 1):]


def scan_chunks(step, state, xs):
    L = xs[0].shape[1]
    cl = min(CHUNK, L)
    nc = L // cl

    def to_chunks(t):
        return jnp.swapaxes(t.reshape((t.shape[0], nc, cl) + t.shape[2:]), 0, 1)

    state, ys = lax.scan(lambda s, c: step(s, *c), state, tuple(to_chunks(t) for t in xs))
    ys = jnp.swapaxes(ys, 0, 1)
    return state, ys.reshape((ys.shape[0], L) + ys.shape[3:])


def rel_bias_lookup(table, rel):
    idx = jnp.clip(rel, -A_MAX_REL, A_MAX_REL) + A_MAX_REL
    return jnp.transpose(table[idx], (2, 0, 1)).astype(jnp.float32)


def band_attn_prompt(q, k, v, rel_table):
    bsz, L, H, Dh = q.shape
    nc = L // CHUNK
    kp = jnp.pad(k, ((0, 0), (A_WINDOW, 0), (0, 0), (0, 0)))
    vp = jnp.pad(v, ((0, 0), (A_WINDOW, 0), (0, 0), (0, 0)))
    qc = q.reshape(bsz, nc, CHUNK, H, Dh)
    rel = jnp.arange(CHUNK)[:, None] + A_WINDOW - jnp.arange(A_BAND)[None, :]
    bias = rel_bias_lookup(rel_table, rel)
    scale = A_HEAD_DIM ** -0.5
    neg = jnp.finfo(jnp.float32).min

    def one_chunk(c):
        start = c * CHUNK
        qb = lax.dynamic_index_in_dim(qc, c, axis=1, keepdims=False)
        kb = lax.dynamic_slice_in_dim(kp, start, A_BAND, axis=1)
        vb = lax.dynamic_slice_in_dim(vp, start, A_BAND, axis=1)
        valid = (start - A_WINDOW + jnp.arange(A_BAND)) >= 0
        s = jnp.einsum('bqhd,bkhd->bhqk', qb, kb).astype(jnp.float32) * scale + bias
        s = jnp.where(valid, s, neg)
        p = jax.nn.softmax(s, axis=-1).astype(vb.dtype)
        return jnp.einsum('bhqk,bkhd->bqhd', p, vb)

    out = lax.map(one_chunk, jnp.arange(nc))
    return jnp.swapaxes(out, 0, 1).reshape(bsz, L, H * Dh)


def band_attn_sample(q, k, v, k_cache, v_cache, rel_table):
    bsz, L, H, Dh = q.shape
    lc = k_cache.shape[1]
    kk = jnp.concatenate([k_cache.astype(k.dtype), k], axis=1)
    vv = jnp.concatenate([v_cache.astype(v.dtype), v], axis=1)
    kpos = jnp.concatenate([jnp.arange(lc) - lc, jnp.arange(L)])
    rel = jnp.arange(L)[:, None] - kpos[None, :]
    bias = rel_bias_lookup(rel_table, rel)
    s = jnp.einsum('bqhd,bkhd->bhqk', q, kk).astype(jnp.float32) * (A_HEAD_DIM ** -0.5) + bias
    p = jax.nn.softmax(s, axis=-1).astype(vv.dtype)
    return jnp.einsum('bhqk,bkhd->bqhd', p, vv).reshape(bsz, L, H * Dh)


def gdn_chunk(s0, q, k, v, beta, g):
    L = q.shape[1]
    incl = jnp.tril(jnp.ones((L, L), dtype=bool))
    strict = jnp.tril(jnp.ones((L, L), dtype=bool), -1)
    gc = jnp.cumsum(g, axis=1).transpose(0, 2, 1)
    decay = jnp.exp(jnp.where(incl, gc[..., :, None] - gc[..., None, :], -jnp.inf))
    qh = q.transpose(0, 2, 1, 3)
    kh = k.transpose(0, 2, 1, 3)
    vh = v.transpose(0, 2, 1, 3)
    bh = beta.transpose(0, 2, 1)[..., None]
    kb = kh * bh
    lmat = jnp.where(strict, jnp.einsum('bhik,bhjk->bhij', kb, kh) * decay, 0.0)
    tmat = lmat + jnp.eye(L, dtype=lmat.dtype)
    rhs = jnp.concatenate([vh * bh, kb * jnp.exp(gc)[..., None]], axis=-1)
    sol = lax.linalg.triangular_solve(tmat, rhs, left_side=True, lower=True, unit_diagonal=True)
    w_v, w_k = sol[..., :B_HEAD_DIM], sol[..., B_HEAD_DIM:]
    u = w_v - jnp.einsum('bhlk,bhkv->bhlv', w_k, s0)
    qk = jnp.einsum('bhik,bhjk->bhij', qh, kh) * decay
    o = jnp.einsum('bhlk,bhkv->bhlv', qh * jnp.exp(gc)[..., None], s0) + jnp.einsum('bhij,bhjv->bhiv', qk, u)
    g_last = gc[..., -1:]
    s_new = s0 * jnp.exp(g_last)[..., None] + jnp.einsum('bhlk,bhlv->bhkv', kh * jnp.exp(g_last - gc)[..., None], u)
    return s_new, o.transpose(0, 2, 1, 3)


def ssd_chunk(h0, x, dt, bm, cm, a_neg):
    bsz, L = x.shape[0], x.shape[1]
    R = C_HEADS // C_GROUPS
    incl = jnp.tril(jnp.ones((L, L), dtype=bool))
    ac = jnp.cumsum(dt * a_neg, axis=1).transpose(0, 2, 1).reshape(bsz, C_GROUPS, R, L)
    decay = jnp.exp(jnp.where(incl, ac[..., :, None] - ac[..., None, :], -jnp.inf))
    xdt = (x * dt[..., None]).reshape(bsz, L, C_GROUPS, R, C_HEAD_DIM)
    cb = jnp.einsum('bign,bjgn->bgij', cm, bm)
    y_diag = jnp.einsum('bgrij,bjgrp->bigrp', cb[:, :, None] * decay, xdt)
    h0g = h0.reshape(bsz, C_GROUPS, R, C_HEAD_DIM, C_STATE)
    y_off = jnp.einsum('bign,bgrpn->bigrp', cm, h0g) * jnp.exp(ac).transpose(0, 3, 1, 2)[..., None]
    a_last = ac[..., -1:]
    h_new = h0g * jnp.exp(a_last)[..., None] + jnp.einsum('bjgn,bgrj,bjgrp->bgrpn', bm, jnp.exp(a_last - ac), xdt)
    y = (y_diag + y_off).reshape(bsz, L, C_HEADS, C_HEAD_DIM)
    return h_new.reshape(bsz, C_HEADS, C_HEAD_DIM, C_STATE), y


def memory_kv(mem, g, wk, wv):
    m = rmsnorm(mem, g)
    bsz = m.shape[0]
    k = (m @ wk).reshape(bsz, N_MEM, X_HEADS, X_HEAD_DIM)
    v = (m @ wv).reshape(bsz, N_MEM, X_HEADS, X_HEAD_DIM)
    return k, v


def memory_attn(h, mk, mv, wq, wo):
    bsz, L, _ = h.shape
    q = (h @ wq).reshape(bsz, L, X_HEADS, X_HEAD_DIM)
    s = jnp.einsum('bqhd,bkhd->bhqk', q, mk.astype(q.dtype)).astype(jnp.float32) * (X_HEAD_DIM ** -0.5)
    p = jax.nn.softmax(s, axis=-1).astype(h.dtype)
    o = jnp.einsum('bhqk,bkhd->bqhd', p, mv.astype(h.dtype)).reshape(bsz, L, D_MODEL)
    return o @ wo


def encoder_layer(x, lw, a_past, b_conv, b_rec, c_conv, c_ssm, f_conv, mem_k, mem_v):
    f32 = jnp.float32
    bsz, L, _ = x.shape
    h = rmsnorm(x, lw['norm_mix'])
    a_qkv, b_qkv, b_beta, b_dec, b_gate, c_z, c_xbc, c_dt, gates = split_cols(h @ lw['w_in'], IN_SPLITS)

    aq, ak, av = [t.reshape(bsz, L, A_HEADS, A_HEAD_DIM) for t in jnp.split(a_qkv, 3, axis=-1)]
    if a_past is None:
        ya = band_attn_prompt(aq, ak, av, lw['a_rel_bias'])
        keep = min(A_WINDOW, L)
        a_k_new, a_v_new = ak[:, L - keep:], av[:, L - keep:]
    else:
        ya = band_attn_sample(aq, ak, av, a_past[0], a_past[1], lw['a_rel_bias'])
        a_k_new, a_v_new = ak, av

    bqkv, b_conv_new = causal_dwconv(b_qkv, b_conv, lw['b_conv_w'])
    bqkv = jax.nn.silu(bqkv)
    bq, bk, bv = [t.reshape(bsz, L, B_HEADS, B_HEAD_DIM) for t in jnp.split(bqkv, 3, axis=-1)]
    bq = l2norm(bq) * (B_HEAD_DIM ** -0.5)
    bk = l2norm(bk)
    beta = jax.nn.sigmoid(b_beta.astype(f32))
    gdec = -jnp.exp(lw['b_a_log'].astype(f32)) * jax.nn.softplus(b_dec.astype(f32) + lw['b_dt_bias'].astype(f32))
    b_rec_new, ob = scan_chunks(gdn_chunk, b_rec.astype(f32), (bq, bk, bv.astype(f32), beta, gdec))
    ob = rmsnorm(ob.astype(x.dtype), lw['b_norm'])
    yb = ob.reshape(bsz, L, B_WIDTH) * jax.nn.silu(b_gate)

    xbc, c_conv_new = causal_dwconv(c_xbc, c_conv, lw['c_conv_w'], lw['c_conv_b'])
    xbc = jax.nn.silu(xbc)
    cx, cbm, ccm = split_cols(xbc, (C_INNER, C_GROUPS * C_STATE, C_GROUPS * C_STATE))
    cx = cx.reshape(bsz, L, C_HEADS, C_HEAD_DIM).astype(f32)
    cbm = cbm.reshape(bsz, L, C_GROUPS, C_STATE).astype(f32)
    ccm = ccm.reshape(bsz, L, C_GROUPS, C_STATE).astype(f32)
    dt = jax.nn.softplus(c_dt.astype(f32) + lw['c_dt_bias'].astype(f32))
    a_neg = -jnp.exp(lw['c_a_log'].astype(f32))
    c_ssm_new, yc = scan_chunks(lambda s, xx, dd, bb, cc: ssd_chunk(s, xx, dd, bb, cc, a_neg), c_ssm.astype(f32), (cx, dt, cbm, ccm))
    yc = yc + lw['c_d'].astype(f32)[:, None] * cx
    yc = rmsnorm(yc.astype(x.dtype).reshape(bsz, L, C_INNER) * jax.nn.silu(c_z), lw['c_norm'])

    ga, gb, gc = jnp.split(jax.nn.sigmoid(gates), 3, axis=-1)
    m = ga * (ya @ lw['w_br_a']) + gb * (yb @ lw['w_br_b']) + gc * (yc @ lw['w_br_c'])
    x = x + m @ lw['w_out']

    x = x + memory_attn(rmsnorm(x, lw['norm_x']), mem_k, mem_v, lw['wx_q'], lw['wx_o'])

    hf = rmsnorm(x, lw['norm_ffn'])
    u, gf = jnp.split(hf @ lw['w_up'], 2, axis=-1)
    gf, f_conv_new = causal_dwconv(gf, f_conv, lw['f_conv_w'], lw['f_conv_b'])
    x = x + (u * jax.nn.silu(gf)) @ lw['w_down']
    return x, (a_k_new, a_v_new, b_conv_new, b_rec_new, c_conv_new, c_ssm_new, f_conv_new)


def setup_inputs(seed: int = 0) -> dict:
    key = jax.random.key(seed)
    keys = jax.random.split(key, 64)
    cnt = [0]

    def nk():
        k = keys[cnt[0]]
        cnt[0] += 1
        return k

    def nrm(shape, scale):
        return jax.random.normal(nk(), shape, jnp.float32) * scale

    def gain(shape):
        return 1.0 + nrm(shape, 0.02)

    def dt_bias(n):
        u = jax.random.uniform(nk(), (DEPTH, n), jnp.float32)
        lo, hi = jnp.log(jnp.float32(0.001)), jnp.log(jnp.float32(0.1))
        dtv = jnp.exp(u * (hi - lo) + lo)
        return dtv + jnp.log(-jnp.expm1(-dtv))

    def a_log(n):
        return jnp.log(jax.random.uniform(nk(), (DEPTH, n), jnp.float32, 1.0, 16.0))

    ac = min(A_WINDOW, PAST_LEN)
    d = D_MODEL
    return {
        'x_prompt': nrm((BATCH, SEQ, d), 1.0),
        'x_sample': nrm((DEC_BATCH, DEC_SEQ, d), 1.0),
        'cache_attn_k': nrm((DEPTH, DEC_BATCH, ac, A_HEADS, A_HEAD_DIM), 1.0),
        'cache_attn_v': nrm((DEPTH, DEC_BATCH, ac, A_HEADS, A_HEAD_DIM), 1.0),
        'state_b_conv': nrm((DEPTH, DEC_BATCH, B_CONV - 1, 3 * B_WIDTH), 1.0),
        'state_b_rec': nrm((DEPTH, DEC_BATCH, B_HEADS, B_HEAD_DIM, B_HEAD_DIM), 0.1),
        'state_c_conv': nrm((DEPTH, DEC_BATCH, C_CONV - 1, C_XBC), 1.0),
        'state_c_ssm': nrm((DEPTH, DEC_BATCH, C_HEADS, C_HEAD_DIM, C_STATE), 0.1),
        'state_ffn_conv': nrm((DEPTH, DEC_BATCH, F_CONV - 1, D_FF), 1.0),
        'cache_mem_k': nrm((DEPTH, DEC_BATCH, N_MEM, X_HEADS, X_HEAD_DIM), 1.0),
        'cache_mem_v': nrm((DEPTH, DEC_BATCH, N_MEM, X_HEADS, X_HEAD_DIM), 1.0),
        'mem_prompt': nrm((BATCH, N_MEM, d), 1.0),
        'norm_mix': gain((DEPTH, d)),
        'w_in': nrm((DEPTH, d, N_IN), d ** -0.5),
        'a_rel_bias': nrm((DEPTH, A_N_REL, A_HEADS), 0.5),
        'b_conv_w': nrm((DEPTH, B_CONV, 3 * B_WIDTH), 0.5),
        'b_a_log': a_log(B_HEADS),
        'b_dt_bias': dt_bias(B_HEADS),
        'b_norm': gain((DEPTH, B_HEAD_DIM)),
        'c_conv_w': nrm((DEPTH, C_CONV, C_XBC), 0.5),
        'c_conv_b': nrm((DEPTH, C_XBC), 0.02),
        'c_dt_bias': dt_bias(C_HEADS),
        'c_a_log': a_log(C_HEADS),
        'c_d': gain((DEPTH, C_HEADS)),
        'c_norm': gain((DEPTH, C_INNER)),
        'w_br_a': nrm((DEPTH, A_WIDTH, d), A_WIDTH ** -0.5),
        'w_br_b': nrm((DEPTH, B_WIDTH, d), B_WIDTH ** -0.5),
        'w_br_c': nrm((DEPTH, C_INNER, d), C_INNER ** -0.5),
        'w_out': nrm((DEPTH, d, d), d ** -0.5),
        'norm_x': gain((DEPTH, d)),
        'norm_mem': gain((DEPTH, d)),
        'wx_q': nrm((DEPTH, d, d), d ** -0.5),
        'wx_k': nrm((DEPTH, d, d), d ** -0.5),
        'wx_v': nrm((DEPTH, d, d), d ** -0.5),
        'wx_o': nrm((DEPTH, d, d), d ** -0.5),
        'norm_ffn': gain((DEPTH, d)),
        'w_up': nrm((DEPTH, d, 2 * D_FF), d ** -0.5),
        'f_conv_w': nrm((DEPTH, F_CONV, D_FF), F_CONV ** -0.5),
        'f_conv_b': nrm((DEPTH, D_FF), 0.02),
        'w_down': nrm((DEPTH, D_FF, d), D_FF ** -0.5),
        'norm_final': gain((d,)),
    }


def reference(x_prompt, x_sample, cache_attn_k, cache_attn_v, state_b_conv, state_b_rec, state_c_conv, state_c_ssm, state_ffn_conv, cache_mem_k, cache_mem_v, mem_prompt, norm_mix, w_in, a_rel_bias, b_conv_w, b_a_log, b_dt_bias, b_norm, c_conv_w, c_conv_b, c_dt_bias, c_a_log, c_d, c_norm, w_br_a, w_br_b, w_br_c, w_out, norm_x, norm_mem, wx_q, wx_k, wx_v, wx_o, norm_ffn, w_up, f_conv_w, f_conv_b, w_down, norm_final):
    dtype = x_prompt.dtype
    nb = x_prompt.shape[0]
    xp, xs = x_prompt, x_sample
    p_states, s_states, p_mk, p_mv = [], [], [], []
    for l in range(DEPTH):
        lw = {
            'norm_mix': norm_mix[l], 'w_in': w_in[l], 'a_rel_bias': a_rel_bias[l],
            'b_conv_w': b_conv_w[l], 'b_a_log': b_a_log[l], 'b_dt_bias': b_dt_bias[l], 'b_norm': b_norm[l],
            'c_conv_w': c_conv_w[l], 'c_conv_b': c_conv_b[l], 'c_dt_bias': c_dt_bias[l], 'c_a_log': c_a_log[l],
            'c_d': c_d[l], 'c_norm': c_norm[l],
            'w_br_a': w_br_a[l], 'w_br_b': w_br_b[l], 'w_br_c': w_br_c[l], 'w_out': w_out[l],
            'norm_x': norm_x[l], 'wx_q': wx_q[l], 'wx_o': wx_o[l],
            'norm_ffn': norm_ffn[l], 'w_up': w_up[l], 'f_conv_w': f_conv_w[l], 'f_conv_b': f_conv_b[l], 'w_down': w_down[l],
        }
        mk, mv = memory_kv(mem_prompt, norm_mem[l], wx_k[l], wx_v[l])
        xp, st_p = encoder_layer(
            xp, lw, None,
            jnp.zeros((nb, B_CONV - 1, 3 * B_WIDTH), dtype),
            jnp.zeros((nb, B_HEADS, B_HEAD_DIM, B_HEAD_DIM), jnp.float32),
            jnp.zeros((nb, C_CONV - 1, C_XBC), dtype),
            jnp.zeros((nb, C_HEADS, C_HEAD_DIM, C_STATE), jnp.float32),
            jnp.zeros((nb, F_CONV - 1, D_FF), dtype),
            mk, mv)
        p_states.append(st_p)
        p_mk.append(mk)
        p_mv.append(mv)
        xs, st_s = encoder_layer(
            xs, lw, (cache_attn_k[l], cache_attn_v[l]),
            state_b_conv[l], state_b_rec[l], state_c_conv[l], state_c_ssm[l], state_ffn_conv[l],
            cache_mem_k[l], cache_mem_v[l])
        s_states.append(st_s)

    y_prompt = rmsnorm(xp, norm_final)
    y_sample = rmsnorm(xs, norm_final)
    pst = [jnp.stack([s[i] for s in p_states]).astype(dtype) for i in range(7)]
    sst = [jnp.stack([s[i] for s in s_states]).astype(dtype) for i in range(7)]
    attn_k_prompt, attn_v_prompt, b_conv_prompt, b_rec_prompt, c_conv_prompt, c_ssm_prompt, ffn_conv_prompt = pst
    attn_k_sample, attn_v_sample, b_conv_sample, b_rec_sample, c_conv_sample, c_ssm_sample, ffn_conv_sample = sst
    mem_k_prompt = jnp.stack(p_mk).astype(dtype)
    mem_v_prompt = jnp.stack(p_mv).astype(dtype)
    return (y_prompt, y_sample, attn_k_prompt, attn_v_prompt, b_conv_prompt, b_rec_prompt, c_conv_prompt, c_ssm_prompt, ffn_conv_prompt, mem_k_prompt, mem_v_prompt, attn_k_sample, attn_v_sample, b_conv_sample, b_rec_sample, c_conv_sample, c_ssm_sample, ffn_conv_sample)
```

```python
import contextlib
import numpy as np
import concourse.bass as bass
import concourse.mybir as mybir
from concourse.bass_utils import run_bass_kernel_spmd

F32 = mybir.dt.float32
BF16 = mybir.dt.bfloat16
ALU = mybir.AluOpType
AF = mybir.ActivationFunctionType

D = 1024
KD = 8
EPS = 1e-6
CHUNK = 64
N_MEM = 256
A_H, A_DH, A_W = 8, 64, 512
A_WIN = 512
B_H, B_DH, B_W = 4, 128, 512
C_H, C_P, C_IN, C_N, C_XBC = 16, 64, 1024, 128, 1536
X_H, X_DH = 4, 256
D_FF = 2816
KFF = 22
N_IN = 9240
O_AQ, O_AK, O_AV = 0, 512, 1024
O_BQKV = 1536
O_BBETA = 3072
O_BDEC = 3076
O_BGATE = 3080
O_CZ = 3592
O_CXBC = 4616
O_CDT = 6152
O_GATES = 6168

ENGINES = ("tensor", "vector", "scalar", "gpsimd", "sync")
SEM_LIMIT = 30000
N_DMA_SEMS = 24


class Sched:
    def __init__(self, nc):
        self.nc = nc
        self.stack = contextlib.ExitStack()
        self.ops = {e: [] for e in ENGINES}
        self.eng_sem = {}
        self.eng_cnt = {e: 0 for e in ENGINES}
        self.eng_epoch = {e: 0 for e in ENGINES}
        for e in ENGINES:
            self.eng_sem[e] = self._new_sem(f"s_{e}_0")
        self.dma_sems = [self._new_sem(f"s_dma_{i}") for i in range(N_DMA_SEMS)]
        self.dma_cnt = [0] * N_DMA_SEMS
        self.dma_rr = 0
        self.last_w = {}
        self.readers = {}
        self.waited = {e: {} for e in ENGINES}
        self.dry = False

    def _new_sem(self, name):
        return self.stack.enter_context(self.nc.semaphore(name))

    def _deps_for(self, eng, reads, writes):
        deps = []
        for k in reads:
            w = self.last_w.get(k)
            if w is not None:
                deps.append(w)
            if isinstance(k, tuple) and k[0] == "ps":
                rd = self.readers.get(k)
                if rd:
                    deps.extend(t for rk, t in rd.items() if rk != eng)
        for k in writes:
            w = self.last_w.get(k)
            if w is not None:
                deps.append(w)
            rd = self.readers.get(k)
            if rd:
                deps.extend(rd.values())
        best = {}
        for (s, v) in deps:
            i = id(s)
            if i not in best or best[i][1] < v:
                best[i] = (s, v)
        out = []
        wd = self.waited[eng]
        for i, (s, v) in best.items():
            if wd.get(i, 0) >= v:
                continue
            wd[i] = v
            out.append((s, v))
        return out

    def _commit(self, tok, rkey, reads, writes):
        for k in writes:
            self.last_w[k] = tok
            self.readers[k] = {}
        for k in reads:
            self.readers.setdefault(k, {})[rkey] = tok

    def op(self, eng, fn, reads=(), writes=()):
        if self.dry:
            return
        waits = self._deps_for(eng, reads, writes)
        if self.eng_cnt[eng] >= SEM_LIMIT:
            self.eng_epoch[eng] += 1
            self.eng_sem[eng] = self._new_sem(f"s_{eng}_{self.eng_epoch[eng]}")
            self.eng_cnt[eng] = 0
        self.eng_cnt[eng] += 1
        s = self.eng_sem[eng]
        v = self.eng_cnt[eng]
        if eng == "tensor":
            self.waited[eng][id(s)] = v
        self.ops[eng].append((fn, waits, (s, 1)))
        self._commit((s, v), eng, reads, writes)

    def dma(self, eng, fn, reads=(), writes=()):
        if self.dry:
            return
        i = self.dma_rr
        self.dma_rr = (self.dma_rr + 1) % N_DMA_SEMS
        s = self.dma_sems[i]
        waits = self._deps_for(eng, reads, writes)
        prev = self.dma_cnt[i]
        if prev > 0 and self.waited[eng].get(id(s), 0) < prev:
            waits.append((s, prev))
            self.waited[eng][id(s)] = prev
        self.dma_cnt[i] += 16
        v = self.dma_cnt[i]
        self.ops[eng].append((fn, waits, (s, 16)))
        self._commit((s, v), ("dma", i, v), reads, writes)

    def wait_all(self, eng):
        waits = []
        for e in ENGINES:
            if self.eng_cnt[e] > 0:
                waits.append((self.eng_sem[e], self.eng_cnt[e]))
        for i, s in enumerate(self.dma_sems):
            if self.dma_cnt[i] > 0:
                waits.append((s, self.dma_cnt[i]))
        self.ops[eng].append((None, waits, None))

    def emit(self):
        nc = self.nc
        ops = self.ops

        def run(e):
            def body(engobj):
                for fn, waits, inc in ops[e]:
                    for (s, v) in waits:
                        engobj.wait_ge(s, v)
                    if fn is not None:
                        fn(engobj).then_inc(inc[0], inc[1])
            return body

        with nc.Block() as block:
            block.tensor(run("tensor"))
            block.vector(run("vector"))
            block.scalar(run("scalar"))
            block.gpsimd(run("gpsimd"))
            block.sync(run("sync"))
        self.stack.close()
        return {e: len(ops[e]) for e in ENGINES}


WEIGHT_SHAPES = {
    "w_in": (D, N_IN), "w_br_a": (A_W, D), "w_br_b": (B_W, D), "w_br_c": (C_IN, D), "w_out": (D, D),
    "wx_q": (D, D), "wx_k": (D, D), "wx_v": (D, D), "wx_o": (D, D), "w_up": (D, 2 * D_FF), "w_down": (D_FF, D),
}
SMALL_SHAPES = {
    "norm_mix": (2, D), "a_rel_bias": (2, 257, 8), "b_conv_w": (2, 4, 1536), "b_a_log": (2, 4), "b_dt_bias": (2, 4),
    "b_norm": (2, 128), "c_conv_w": (2, 4, 1536), "c_conv_b": (2, 1536), "c_dt_bias": (2, 16), "c_a_log": (2, 16),
    "c_d": (2, 16), "c_norm": (2, D), "norm_x": (2, D), "norm_mem": (2, D), "norm_ffn": (2, D),
    "f_conv_w": (2, 3, D_FF), "f_conv_b": (2, D_FF), "norm_final": (D,),
}


NL_MAX = 7


def make_consts():
    c = {}
    eye = np.eye(128, dtype=np.float32)
    p = np.arange(128)[:, None]
    f = np.arange(128)[None, :]
    c["ident"] = eye
    c["anti"] = eye[::-1].copy()
    c["triI"] = (f >= p).astype(np.float32)
    c["triSL"] = (f < p).astype(np.float32)
    c["negU"] = np.where(f >= p, 0.0, -30000.0).astype(np.float32)
    mm = np.zeros((128, NL_MAX, 2, 128), np.float32)
    for l in range(NL_MAX):
        b = 1 << l
        same = (p // (2 * b)) == (f // (2 * b))
        M = same & ((p % (2 * b)) >= b) & ((f % (2 * b)) < b)
        mm[:, l, 0, :] = M
        mm[:, l, 1, :] = M.T
    c["mm"] = mm.reshape(128, NL_MAX * 2 * 128)
    c["ii"] = np.concatenate([eye, eye], axis=1)
    return c


class Builder:
    def __init__(self, NP, SEQ, T=256, sample=True, dbg=None):
        self.NP, self.SEQ, self.T, self.sample = NP, SEQ, T, sample
        self.dbg = dbg or {}
        self.KEEP = min(A_WIN, SEQ)
        self.NSLOT = 4 + T // 128
        nc = self.nc = bass.Bass("TRN2", target_bir_lowering=False)
        self.S = Sched(nc)
        self.inp = {}
        self.out = {}
        self._ptmp = 0
        self._wrr = 0
        self._rr = {}
        self._uid = 0
        self._nb = 8
        self._declare_io()

    def din(self, name, shape):
        self.inp[name] = self.nc.dram_tensor(name, list(shape), F32, kind="ExternalInput").ap()
        return self.inp[name]

    def dout(self, name, shape):
        self.out[name] = self.nc.dram_tensor(name, list(shape), F32, kind="ExternalOutput").ap()
        return self.out[name]

    def _declare_io(self):
        NP, SEQ, KEEP = self.NP, self.SEQ, self.KEEP
        self.din("x_prompt", (NP, SEQ, D))
        self.din("mem_prompt", (NP, N_MEM, D))
        if self.sample:
            self.din("x_sample", (16, D))
            self.din("cache_attn_k", (2, 512, 512))
            self.din("cache_attn_v", (2, 512, 512))
            self.din("state_b_conv", (2, 3, 1536))
            self.din("state_b_rec", (2, 4, 128, 128))
            self.din("state_c_conv", (2, 3, 1536))
            self.din("state_c_ssm", (2, 16, 64, 128))
            self.din("state_ffn_conv", (2, 2, D_FF))
            self.din("cache_mem_k", (2, N_MEM, D))
            self.din("cache_mem_v", (2, N_MEM, D))
        for k, shp in WEIGHT_SHAPES.items():
            self.din(k, (2,) + shp)
        for k, shp in SMALL_SHAPES.items():
            self.din(k, shp)
        for k, v in make_consts().items():
            self.din("c_" + k, v.shape)
        self.dout("y_prompt", (NP, SEQ, D))
        self.dout("attn_k_prompt", (2, NP, KEEP, 512))
        self.dout("attn_v_prompt", (2, NP, KEEP, 512))
        self.dout("b_conv_prompt", (2, NP, 3, 1536))
        self.dout("b_rec_prompt", (2, NP, 4, 128, 128))
        self.dout("c_conv_prompt", (2, NP, 3, 1536))
        self.dout("c_ssm_prompt", (2, NP, 16, 64, 128))
        self.dout("ffn_conv_prompt", (2, NP, 2, D_FF))
        self.dout("mem_k_prompt", (2, NP, N_MEM, D))
        self.dout("mem_v_prompt", (2, NP, N_MEM, D))
        if self.sample:
            self.dout("y_sample", (16, D))
            self.dout("attn_k_sample", (2, 16, 512))
            self.dout("attn_v_sample", (2, 16, 512))
            self.dout("b_conv_sample", (2, 3, 1536))
            self.dout("b_rec_sample", (2, 4, 128, 128))
            self.dout("c_conv_sample", (2, 3, 1536))
            self.dout("c_ssm_sample", (2, 16, 64, 128))
            self.dout("ffn_conv_sample", (2, 2, D_FF))
        self.wscr = {}
        for k, shp in WEIGHT_SHAPES.items():
            self.wscr[k] = self.nc.dram_tensor("scr_" + k, [2, shp[0], shp[1]], BF16).ap()
        self.ext = self.nc.dram_tensor("scr_ext", [2, 8, 384], F32).ap()

    def sb(self, name, shape, dt=F32):
        return self.nc.alloc_sbuf_tensor(name, list(shape), dt)

    def mm(self, out, lhsT, rhs, start=True, stop=True, r=(), w=()):
        self.S.op("tensor", lambda e: e.matmul(out, lhsT=lhsT, rhs=rhs, start=start, stop=stop), reads=r, writes=w)

    def tr(self, out, in_, r=(), w=()):
        k = in_.shape[0]
        ident = self.ident[0:k, 0:k]
        self.S.op("tensor", lambda e: e.transpose(out, in_, ident), reads=list(r) + ["consts"], writes=w)

    def act(self, out, in_, func, r=(), w=(), scale=1.0, bias=0.0):
        self.S.op("scalar", lambda e: e.activation(out=out, in_=in_, func=func, bias=bias, scale=scale), reads=r, writes=w)

    def tt(self, eng, out, in0, in1, op, r=(), w=()):
        self.S.op(eng, lambda e: e.tensor_tensor(out=out, in0=in0, in1=in1, op=op), reads=r, writes=w)

    def ts(self, eng, out, in0, s1, s2, op0, op1=None, r=(), w=()):
        if op1 is None:
            self.S.op(eng, lambda e: e.tensor_scalar(out=out, in0=in0, scalar1=s1, scalar2=None, op0=op0), reads=r, writes=w)
        else:
            self.S.op(eng, lambda e: e.tensor_scalar(out=out, in0=in0, scalar1=s1, scalar2=s2, op0=op0, op1=op1), reads=r, writes=w)

    def stt(self, out, in0, scalar, in1, op0, op1, r=(), w=()):
        self.S.op("vector", lambda e: e.scalar_tensor_tensor(out=out, in0=in0, scalar=scalar, in1=in1, op0=op0, op1=op1), reads=r, writes=w)

    def cp(self, eng, out, in_, r=(), w=()):
        if eng == "scalar":
            self.S.op(eng, lambda e: e.activation(out=out, in_=in_, func=AF.Copy), reads=r, writes=w)
        else:
            self.S.op(eng, lambda e: e.tensor_copy(out=out, in_=in_), reads=r, writes=w)

    def memset(self, eng, ap, val, w=()):
        self.S.op(eng, lambda e: e.memset(ap, val), writes=w)

    def dma(self, q, out, in_, r=(), w=(), slow=False):
        if slow:
            self.S.dma(q, lambda e: e.dma_start(out=out, in_=in_, allow_slow_non_contiguous=True), reads=r, writes=w)
        else:
            self.S.dma(q, lambda e: e.dma_start(out=out, in_=in_), reads=r, writes=w)

    def ptmp(self):
        b = self._ptmp % self._nb
        self._ptmp = (b + 1) % self._nb
        return b

    def rot(self, name, n):
        i = self._rr.get(name, 0)
        self._rr[name] = (i + 1) % n
        return i

    @contextlib.contextmanager
    def scope(self):
        st = contextlib.ExitStack()

        def alloc(name, shape, dt=F32):
            self._uid += 1
            return st.enter_context(self.nc.sbuf_tensor(f"{name}_{self._uid}", list(shape), dt))
        try:
            yield alloc
        finally:
            self.barrier()
            st.close()

    def barrier(self, full=False):
        S = self.S
        if S.dry:
            return
        for e in ENGINES:
            if e == "sync" and not full:
                continue
            waits = []
            for e2 in ENGINES:
                if e2 == "sync" and not full:
                    continue
                if S.eng_cnt[e2] > 0:
                    s, v = S.eng_sem[e2], S.eng_cnt[e2]
                    if S.waited[e].get(id(s), 0) < v:
                        S.waited[e][id(s)] = v
                        waits.append((s, v))
            for i, s in enumerate(S.dma_sems):
                v = S.dma_cnt[i]
                if v > 0 and S.waited[e].get(id(s), 0) < v:
                    S.waited[e][id(s)] = v
                    waits.append((s, v))
            S.ops[e].append((None, waits, None))

    def weights_used(self):
        return self.dbg.get("weights", list(WEIGHT_SHAPES.keys()))

    def prepass(self):
        inp = self.inp
        CW = 2048
        NB = 4
        with contextlib.ExitStack() as st:
            cvf = [st.enter_context(self.nc.sbuf_tensor(f"cvf{i}", [128, CW], F32)) for i in range(NB)]
            cvb = [st.enter_context(self.nc.sbuf_tensor(f"cvb{i}", [128, CW], BF16)) for i in range(NB)]
            engs = ("gpsimd", "scalar", "vector", "gpsimd")
            jobs = []
            for name in self.weights_used():
                rows, cols = WEIGHT_SHAPES[name]
                for l in range(2):
                    for rc in range(rows // 128):
                        for c0 in range(0, cols, CW):
                            jobs.append((name, l, rc, c0, min(CW, cols - c0)))
            LAG = NB - 1
            for i in range(len(jobs) + LAG):
                if i < len(jobs):
                    name, l, rc, c0, n = jobs[i]
                    sl = i % NB
                    self.dma("sync", cvf[sl][:, 0:n], inp[name][l, rc * 128:(rc + 1) * 128, c0:c0 + n], w=[("cvf", sl)])
                    self.cp(engs[sl], cvb[sl][:, 0:n], cvf[sl][:, 0:n], r=[("cvf", sl)], w=[("cvb", sl)])
                if i - LAG >= 0:
                    name, l, rc, c0, n = jobs[i - LAG]
                    sl = (i - LAG) % NB
                    self.dma("sync", self.wscr[name][l, rc * 128:(rc + 1) * 128, c0:c0 + n], cvb[sl][:, 0:n], r=[("cvb", sl)])
            self.barrier(full=True)
        self.barrier(full=True)

    def wload(self, name, l, segs):
        rows = WEIGHT_SHAPES[name][0]
        KC = rows // 128
        ncols = sum(n for _, n in segs)
        assert KC * ncols <= self.WBE, (name, KC, ncols)
        b = self._wrr
        self._wrr = (b + 1) % self.NWB
        view = self.wbuf[b][:, 0:KC * ncols].rearrange("p (c n) -> p c n", c=KC)
        o = 0
        src = self.wscr[name][l].rearrange("(c p) n -> p c n", p=128)
        for (c0, n) in segs:
            self.dma("sync", view[:, :, o:o + n], src[:, :, c0:c0 + n], w=[("wbuf", b)], slow=(n * 2 < 512))
            o += n
        return view, ("wbuf", b)
    def alloc(self):
        T = self.T
        sb = self.sb
        self.ps = self.nc.alloc_psum_tensor("ps", [128, 8, 512], F32)
        self.ident = sb("ident", [128, 128])
        self.anti = sb("anti", [128, 128])
        self.triI = sb("triI", [128, 128])
        self.triSL = sb("triSL", [128, 128])
        self.negU = sb("negU", [128, 128])
        self.zeros = sb("zeros", [128, 128])
        self.identb = sb("identb", [128, 128], BF16)
        self.negUb = sb("negUb", [128, 128], BF16)
        self.mmask = sb("mmask", [128, NL_MAX, 2, 128])
        self.ii = sb("ii", [128, 2, 128])
        self.ones_f = sb("ones_f", [128, 128])
        self.ones_b = sb("ones_b", [128, 128], BF16)
        self.eps_col = sb("eps_col", [128, 1])
        self.g_mix = sb("g_mix", [128, 2, KD])
        self.g_x = sb("g_x", [128, 2, KD])
        self.g_mem = sb("g_mem", [128, 2, KD])
        self.g_ffn = sb("g_ffn", [128, 2, KD])
        self.g_cn = sb("g_cn", [128, 2, KD])
        self.g_fin = sb("g_fin", [128, KD])
        self.g_bn = sb("g_bn", [128, 2])
        self.fcw = sb("fcw", [128, 2, 3, KFF])
        self.fcb = sb("fcb", [128, 2, KFF])
        self.bcw = sb("bcw", [128, 2, 4, 12])
        self.ccw = sb("ccw", [128, 2, 4, 12])
        self.ccb = sb("ccb", [128, 2, 12])
        self.b_dtb = sb("b_dtb", [128, 2, 4])
        self.b_nA = sb("b_nA", [128, 2, 4])
        self.c_dtb = sb("c_dtb", [128, 2, 16])
        self.c_nA = sb("c_nA", [128, 2, 16])
        self.c_dsk = sb("c_dsk", [128, 2, 16])
        self.a_bc = sb("a_bc", [128, 2, 8])
        self.Etab = sb("Etab", [128, 2, 8, 2, 128], BF16)
        NS = self.NSLOT
        self.KT = [sb(f"KT{l}", [64, 8, NS * 128], BF16) for l in range(2)]
        self.VH = [sb(f"VH{l}", [128, NS, 8, 65], BF16) for l in range(2)]
        self.Sst = [sb(f"Sst{l}", [128, 4, 128]) for l in range(2)]
        self.Hst = [sb(f"Hst{l}", [128, 16, 64]) for l in range(2)]
        self.bhist = sb("bhist", [128, 2, 12, 3])
        self.chist = sb("chist", [128, 2, 12, 3])
        self.fhist = sb("fhist", [128, 2, KFF, 2])
        self.MKT = [sb(f"MKT{l}", [128, 8, N_MEM], BF16) for l in range(2)]
        self.MV = [sb(f"MV{l}", [128, 2, D], BF16) for l in range(2)]
        self.xT = sb("xT", [128, KD, T])
        self.hT = sb("hT", [128, KD, T], BF16)
        self.mT = sb("mT", [128, KD, T])
        self.sq = sb("sq", [128, KD, T])
        self.rstd = sb("rstd", [128, T])
        self.sqb = sb("sqb", [128, KD, T], BF16)
        self.NWB = 3
        self.WBE = 4096
        self.wbuf = [sb(f"wbuf{i}", [128, self.WBE], BF16) for i in range(self.NWB)]
        self.gsig = [sb(f"gsig{i}", [128, T]) for i in range(2)]
        self.brT = sb("brT", [128, KD, T], BF16)
        self.io = [sb(f"io{i}", [128, D]) for i in range(2)]
        self.HG = 2
        self.bd = sb("bd", [128, 2, 8])
        self.bet = sb("bet", [128, 2, 4])
        self.nbet = sb("nbet", [128, 2, 4])
        self.gg = sb("gg", [128, 2, 4])
        self.dtr = sb("dtr", [128, 2, 16])
        self.dtt = sb("dtt", [128, 2, 16])
        self.aa = sb("aa", [128, 2, 16])

    def setup(self):
        inp = self.inp
        q = "sync"
        P = ["params"]
        self.dma(q, self.ident[:], inp["c_ident"], w=["consts"])
        self.dma(q, self.anti[:], inp["c_anti"], w=["consts"])
        self.dma(q, self.triI[:], inp["c_triI"], w=["consts"])
        self.dma(q, self.triSL[:], inp["c_triSL"], w=["consts"])
        self.dma(q, self.negU[:], inp["c_negU"], w=["consts"])
        self.memset("vector", self.zeros[:], 0.0, w=["ones"])
        self.cp("vector", self.identb[:], self.ident[:], r=["consts"], w=["constsb"])
        self.cp("vector", self.negUb[:], self.negU[:], r=["consts"], w=["constsb"])
        self.dma(q, self.mmask[:].rearrange("p a b c -> p (a b c)"), inp["c_mm"], w=["consts"])
        self.dma(q, self.ii[:].rearrange("p a b -> p (a b)"), inp["c_ii"], w=["consts"])
        self.memset("vector", self.ones_f[:], 1.0, w=["ones"])
        self.memset("vector", self.ones_b[:], 1.0, w=["ones"])
        self.memset("vector", self.eps_col[:], EPS, w=["ones"])
        for nm, t in (("norm_mix", self.g_mix), ("norm_x", self.g_x), ("norm_mem", self.g_mem),
                      ("norm_ffn", self.g_ffn), ("c_norm", self.g_cn)):
            self.dma(q, t[:], inp[nm].rearrange("l (c p) -> p l c", p=128), w=P, slow=True)
        self.dma(q, self.g_fin[:], inp["norm_final"].rearrange("(c p) -> p c", p=128), w=P, slow=True)
        self.dma(q, self.g_bn[:], inp["b_norm"].rearrange("l p -> p l"), w=P, slow=True)
        for l in range(2):
            for k in range(3):
                self.dma(q, self.fcw[:, l, k, :], inp["f_conv_w"][l, k].rearrange("(c p) -> p c", p=128), w=P, slow=True)
            for k in range(4):
                self.dma(q, self.bcw[:, l, k, :], inp["b_conv_w"][l, k].rearrange("(c p) -> p c", p=128), w=P, slow=True)
                self.dma(q, self.ccw[:, l, k, :], inp["c_conv_w"][l, k].rearrange("(c p) -> p c", p=128), w=P, slow=True)
        self.dma(q, self.fcb[:], inp["f_conv_b"].rearrange("l (c p) -> p l c", p=128), w=P, slow=True)
        self.dma(q, self.ccb[:], inp["c_conv_b"].rearrange("l (c p) -> p l c", p=128), w=P, slow=True)

        def bc(dst, src2d):
            (s0, n0), (s1, n1) = src2d.ap
            src = bass.AP(src2d.tensor, src2d.offset, [[0, 128], [s0, n0], [s1, n1]])
            self.dma(q, dst[:], src, w=P, slow=True)
        if self.dbg.get("nobc"):
            return
        bc(self.b_dtb, inp["b_dt_bias"])
        bc(self.b_nA, inp["b_a_log"])
        bc(self.c_dtb, inp["c_dt_bias"])
        bc(self.c_nA, inp["c_a_log"])
        bc(self.c_dsk, inp["c_d"])
        bc(self.a_bc, inp["a_rel_bias"][:, 256, :])
        for t in (self.b_nA, self.c_nA):
            self.act(t[:], t[:], AF.Exp, r=P, w=P)
            self.ts("vector", t[:], t[:], -1.0, None, ALU.mult, r=P, w=P)
        if self.dbg.get("noE"):
            return
        for l in range(2):
            i = self.rot("io", 2)
            xi, xk = self.io[i], ("io", i)
            self.dma(q, xi[:, 0:8], inp["a_rel_bias"][l, 1:129, :], w=[xk])
            self.dma(q, xi[:, 8:16], inp["a_rel_bias"][l, 129:257, :], w=[xk])
            b = self.ptmp()
            self.tr(self.ps[0:8, b, 0:128], xi[:, 0:8], r=[xk], w=[("ps", b)])
            self.tr(self.ps[0:8, b, 128:256], xi[:, 8:16], r=[xk], w=[("ps", b)])
            self.cp("vector", xi[0:8, 512:768], self.ps[0:8, b, 0:256], r=[("ps", b)], w=[xk])
            self.cp("vector", xi[0:8, 768:896], xi[0:8, 767:768].broadcast_to([8, 128]), r=[xk], w=[xk])
            self.dma(q, self.ext[l], xi[0:8, 512:896], r=[xk], w=[("ext", l)])
            hk = self.sq[:, :, :].rearrange("p c t -> p (c t)")[:, 0:2048].rearrange("p (h j q) -> p h j q", h=8, j=2)
            for jj in range(2):
                off = 128 if jj == 0 else 0
                e = self.ext[l]
                src = bass.AP(e.tensor, e.offset + off, [[1, 128], [384, 8], [1, 128]])
                self.dma(q, hk[:, :, jj, :], src, r=[("ext", l)], w=["sq"])
            hkf = self.sq[:, :, :].rearrange("p c t -> p (c t)")
            ef = self.Etab[:, l].rearrange("p h j q -> p (h j q)")
            for i in range(4):
                b = self.ptmp()
                self.mm(self.ps[:, b, :], self.anti[:], hkf[:, i * 512:(i + 1) * 512], r=["consts", "sq"], w=[("ps", b)])
                self.act(ef[:, i * 512:(i + 1) * 512], self.ps[:, b, :], AF.Exp, r=[("ps", b)], w=[("E", l)])
            self.memset("gpsimd", self.Etab[64:128, l, :, 1, 0:64], 0.0, w=[("E", l)])

    def rmsnorm(self, src, skey, gain, L, out, okey, nk=KD, dim=D):
        rk = [(skey, c) for c in range(nk)]
        hh = nk // 2 if nk >= 4 else nk
        self.act(self.sqb[:, 0:hh, 0:L], src[:, 0:hh, 0:L], AF.Square, r=rk[0:hh], w=["sqb0"])
        if hh < nk:
            self.act(self.sqb[:, hh:nk, 0:L], src[:, hh:nk, 0:L], AF.Square, r=rk[hh:nk], w=["sqb1"])
        b = self.ptmp()
        pst = self.ps[:, b, 0:L]
        for c in range(nk):
            self.mm(pst, self.ones_b[:], self.sqb[:, c, 0:L], start=(c == 0), stop=(c == nk - 1),
                    r=["ones", "sqb0" if c < hh else "sqb1"], w=[("ps", b)])
        self.act(self.rstd[:, 0:L], pst, AF.Ln, r=[("ps", b), "ones"], w=["rstd"], scale=1.0 / dim, bias=self.eps_col[:, 0:1])
        self.act(self.rstd[:, 0:L], self.rstd[:, 0:L], AF.Exp, r=["rstd"], w=["rstd"], scale=-0.5)
        for c in range(nk):
            self.stt(out[:, c, 0:L], src[:, c, 0:L], gain[:, c:c + 1], self.rstd[:, 0:L],
                     ALU.mult, ALU.mult, r=[(skey, c), "rstd", "params"], w=[(okey, c)])

    def hkeys(self):
        return [("hT", c) for c in range(KD)]

    def proj_fm(self, wv, wk, col0, M, L, src=None, skeys=None, KC=KD):
        src = self.hT if src is None else src
        skeys = self.hkeys() if skeys is None else skeys
        b = self.ptmp()
        for c in range(KC):
            self.mm(self.ps[0:M, b, 0:L], wv[:, c, col0:col0 + M], src[:, c, 0:L], start=(c == 0), stop=(c == KC - 1),
                    r=[wk, skeys[c]], w=[("ps", b)])
        return b

    def proj_tm(self, wv, wk, col0, N, t0, n, b=None):
        b = self.ptmp() if b is None else b
        for c in range(KD):
            self.mm(self.ps[0:n, b, 0:N], self.hT[:, c, t0:t0 + n], wv[:, c, col0:col0 + N], start=(c == 0), stop=(c == KD - 1),
                    r=[wk, ("hT", c)], w=[("ps", b)])
        return b

    def merge_branch(self, l, L, wname, KC, gate_off, first):
        bkeys = [("brT", c) for c in range(KC)]
        wb, wbk = self.wload(wname, l, [(0, D)]) if KC == 4 else (None, None)
        for n in range(KD):
            if n % 4 == 0:
                wg, wgk = self.wload("w_in", l, [(gate_off + n * 128, 512)])
                if KC == 8:
                    wb, wbk = self.wload(wname, l, [(n * 128, 512)])
            bg = self.proj_fm(wg, wgk, (n % 4) * 128, 128, L)
            i = self.rot("gsig", 2)
            gs = self.gsig[i]
            self.act(gs[:, 0:L], self.ps[:, bg, 0:L], AF.Sigmoid, r=[("ps", bg)], w=[("gsig", i)])
            coff = n * 128 if KC == 4 else (n % 4) * 128
            bp = self.proj_fm(wb, wbk, coff, 128, L, src=self.brT, skeys=bkeys, KC=KC)
            if first:
                self.tt("vector", self.mT[:, n, 0:L], self.ps[:, bp, 0:L], gs[:, 0:L], ALU.mult,
                        r=[("ps", bp), ("gsig", i)], w=[("mT", n)])
            else:
                self.tt("vector", gs[:, 0:L], self.ps[:, bp, 0:L], gs[:, 0:L], ALU.mult,
                        r=[("ps", bp), ("gsig", i)], w=[("gsig", i)])
                self.tt("gpsimd", self.mT[:, n, 0:L], self.mT[:, n, 0:L], gs[:, 0:L], ALU.add,
                        r=[("mT", n), ("gsig", i)], w=[("mT", n)])

    def resid_proj(self, l, L, wname, src, skeys, KC, nblk):
        for n in range(KD):
            if KC == KFF:
                wv, wk = self.wload(wname, l, [(n * 128, 128)])
                coff = 0
            else:
                if n % 4 == 0:
                    wv, wk = self.wload(wname, l, [(n * 128, 512)])
                coff = (n % 4) * 128
            b = self.proj_fm(wv, wk, coff, 128, L, src=src, skeys=skeys, KC=KC)
            self.tt("vector", self.xT[:, n, 0:L], self.xT[:, n, 0:L], self.ps[:, b, 0:L], ALU.add,
                    r=[("xT", n), ("ps", b)], w=[("xT", n)])
    def load_x_tile(self, src, L):
        nsub = (L + 127) // 128
        for j in range(nsub):
            n = min(128, L - j * 128)
            i = self.rot("io", 2)
            xi, xk = self.io[i], ("io", i)
            self.dma("sync", xi[0:n, :], src[j * 128:j * 128 + n, :], w=[xk])
            for half in range(2):
                b = self.ptmp()
                for cc in range(4):
                    c = half * 4 + cc
                    self.tr(self.ps[:, b, cc * 128:cc * 128 + n], xi[0:n, c * 128:(c + 1) * 128], r=[xk], w=[("ps", b)])
                dst = self.xT[:, half * 4:half * 4 + 4, j * 128:j * 128 + n]
                srcp = self.ps[:, b, :].rearrange("p (c t) -> p c t", c=4)[:, :, 0:n]
                self.cp("vector" if half else "scalar", dst, srcp, r=[("ps", b)], w=[("xT", half * 4 + cc) for cc in range(4)])

    def store_tm(self, dst, src, skey, nch, L, t_lo=0):
        skeys = [(skey, c) for c in range(nch)]
        for j in range(t_lo // 128, (L + 127) // 128):
            n = min(128, L - j * 128)
            i = self.rot("io", 2)
            xi, xk = self.io[i], ("io", i)
            for g in range(nch // 4):
                b = self.ptmp()
                for cc in range(4):
                    c = g * 4 + cc
                    self.tr(self.ps[0:n, b, cc * 128:(cc + 1) * 128], src[:, c, j * 128:j * 128 + n], r=skeys, w=[("ps", b)])
                self.cp("vector" if g % 2 else "scalar", xi[0:n, g * 512:(g + 1) * 512], self.ps[0:n, b, :], r=[("ps", b)], w=[xk])
            self.dma("sync", dst[j * 128 - t_lo:j * 128 - t_lo + n, :], xi[0:n, 0:nch * 128], r=[xk])

    def init_stream(self, st):
        kind = st["kind"]
        inp = self.inp
        for l in range(2):
            self.memset("gpsimd", self.VH[l][:, :, :, 64:65], 1.0, w=[("VH", l)])
        if kind == "p":
            self.memset("gpsimd", self.bhist[:], 0.0, w=["bhist"])
            self.memset("gpsimd", self.chist[:], 0.0, w=["chist"])
            self.memset("gpsimd", self.fhist[:], 0.0, w=["fhist"])
            for l in range(2):
                self.memset("gpsimd", self.Sst[l][:], 0.0, w=[("S", l, h) for h in range(4)])
                self.memset("gpsimd", self.Hst[l][:], 0.0, w=[("H", l)])
            if not self.dbg.get("nomem"):
                self.memory_kv_prompt(st["idx"])
        else:
            for l in range(2):
                for c in range(12):
                    self.dma("sync", self.bhist[:, l, c, :], inp["state_b_conv"][l, :, c * 128:(c + 1) * 128].rearrange("k p -> p k"),
                             w=["bhist"], slow=True)
                    self.dma("sync", self.chist[:, l, c, :], inp["state_c_conv"][l, :, c * 128:(c + 1) * 128].rearrange("k p -> p k"),
                             w=["chist"], slow=True)
                for c in range(KFF):
                    self.dma("sync", self.fhist[:, l, c, :], inp["state_ffn_conv"][l, :, c * 128:(c + 1) * 128].rearrange("k p -> p k"),
                             w=["fhist"], slow=True)
                self.dma("sync", self.Sst[l][:], inp["state_b_rec"][l].rearrange("h k v -> k h v"), w=[("S", l, h) for h in range(4)])
                for g in range(2):
                    i = self.rot("io", 2)
                    xi, xk = self.io[i], ("io", i)
                    src = inp["state_c_ssm"][l].rearrange("h p n -> (h p) n").rearrange("(c q) n -> q c n", q=128)
                    self.dma("sync", xi[:, 0:512].rearrange("q (c n) -> q c n", c=4), src[:, g * 4:g * 4 + 4, :], w=[xk])
                    b = self.ptmp()
                    for cc in range(4):
                        self.tr(self.ps[:, b, cc * 128:(cc + 1) * 128], xi[:, cc * 128:(cc + 1) * 128], r=[xk], w=[("ps", b)])
                    self.cp("vector", self.Hst[l][:, g * 8:g * 8 + 8, :].rearrange("n h p -> n (h p)"), self.ps[:, b, :],
                            r=[("ps", b)], w=[("H", l)])
                for tt_ in range(4):
                    i = self.rot("io", 2)
                    xi, xk = self.io[i], ("io", i)
                    self.dma("sync", xi[:, 0:512], inp["cache_attn_k"][l, tt_ * 128:(tt_ + 1) * 128, :], w=[xk])
                    self.dma("sync", xi[:, 512:1024], inp["cache_attn_v"][l, tt_ * 128:(tt_ + 1) * 128, :], w=[xk])
                    for g in range(2):
                        b = self.ptmp()
                        for hh in range(4):
                            h = g * 4 + hh
                            self.tr(self.ps[0:64, b, hh * 128:(hh + 1) * 128], xi[:, h * 64:(h + 1) * 64], r=[xk], w=[("ps", b)])
                        self.cp("scalar", self.KT[l][:, g * 4:g * 4 + 4, tt_ * 128:(tt_ + 1) * 128],
                                self.ps[0:64, b, :].rearrange("p (h t) -> p h t", h=4), r=[("ps", b)], w=[("KT", l, tt_)])
                    self.cp("vector", self.VH[l][:, tt_, :, 0:64], xi[:, 512:1024].rearrange("p (h d) -> p h d", h=8),
                            r=[xk], w=[("VH", l)])
                for mt in range(2):
                    i = self.rot("io", 2)
                    xi, xk = self.io[i], ("io", i)
                    self.dma("sync", xi[:], inp["cache_mem_k"][l, mt * 128:(mt + 1) * 128, :], w=[xk])
                    for g in range(2):
                        b = self.ptmp()
                        for cc in range(4):
                            c = g * 4 + cc
                            self.tr(self.ps[:, b, cc * 128:(cc + 1) * 128], xi[:, c * 128:(c + 1) * 128], r=[xk], w=[("ps", b)])
                        self.cp("scalar", self.MKT[l][:, g * 4:g * 4 + 4, mt * 128:(mt + 1) * 128],
                                self.ps[:, b, :].rearrange("p (c t) -> p c t", c=4), r=[("ps", b)], w=[("MKT", l)])
                    i = self.rot("io", 2)
                    xi, xk = self.io[i], ("io", i)
                    self.dma("sync", xi[:], inp["cache_mem_v"][l, mt * 128:(mt + 1) * 128, :], w=[xk])
                    self.cp("vector", self.MV[l][:, mt, :], xi[:], r=[xk], w=[("MV", l)])

    def memory_kv_prompt(self, s):
        with self.scope() as sc:
            self.qxT = sc("qxT", [128, KD, self.T], BF16)
            self._memory_kv_prompt(s)

    def _memory_kv_prompt(self, s):
        src = self.inp["mem_prompt"][s]
        L = N_MEM
        for j in range(2):
            i = self.rot("io", 2)
            xi, xk = self.io[i], ("io", i)
            self.dma("sync", xi[:], src[j * 128:(j + 1) * 128, :], w=[xk])
            for half in range(2):
                b = self.ptmp()
                for cc in range(4):
                    c = half * 4 + cc
                    self.tr(self.ps[:, b, cc * 128:(cc + 1) * 128], xi[:, c * 128:(c + 1) * 128], r=[xk], w=[("ps", b)])
                self.cp("vector" if half else "scalar", self.mT[:, half * 4:half * 4 + 4, j * 128:(j + 1) * 128],
                        self.ps[:, b, :].rearrange("p (c t) -> p c t", c=4), r=[("ps", b)], w=[("mT", half * 4 + cc) for cc in range(4)])
        stage = self.dbg.get("memstage", 9)
        for l in range(2):
            if stage < 1:
                continue
            self.rmsnorm(self.mT, "mT", self.g_mem[:, l, :], L, self.qxT, "qxT")
            qk = [("qxT", c) for c in range(KD)]
            if stage < 2:
                continue
            for n in range(KD):
                if n % 4 == 0:
                    wv, wk = self.wload("wx_k", l, [(n * 128, 512)])
                b = self.proj_fm(wv, wk, (n % 4) * 128, 128, L, src=self.qxT, skeys=qk)
                self.cp("scalar" if n % 2 else "vector", self.MKT[l][:, n, :], self.ps[:, b, 0:L], r=[("ps", b)], w=[("MKT", l)])
            if stage < 3:
                continue
            for which, wname, oname in ((0, "wx_k", "mem_k_prompt"), (1, "wx_v", "mem_v_prompt")):
                if which == 1 and stage < 4:
                    continue
                for half in range(2):
                    wv, wk = self.wload(wname, l, [(half * 512, 512)])
                    for mt in range(2):
                        b = self.ptmp()
                        for c in range(KD):
                            self.mm(self.ps[:, b, :], self.qxT[:, c, mt * 128:(mt + 1) * 128], wv[:, c, :], start=(c == 0), stop=(c == KD - 1),
                                    r=[wk] + qk, w=[("ps", b)])
                        i = self.rot("io", 2)
                        ko, kk = self.io[i][:, 0:512], ("io", i)
                        self.cp("scalar", ko[:], self.ps[:, b, :], r=[("ps", b)], w=[kk])
                        if which == 1:
                            self.cp("vector", self.MV[l][:, mt, half * 512:(half + 1) * 512], ko[:], r=[kk], w=[("MV", l)])
                        self.dma("sync", self.out[oname][l, s, mt * 128:(mt + 1) * 128, half * 512:(half + 1) * 512], ko[:], r=[kk])

    def phase_A(self, l, st, t0, L):
        with self.scope() as sc:
            self.QT = sc("QT", [64, 8, self.T], BF16)
            self.PT = sc("PT", [128, 8, 5, 128], BF16)
            self.yA = sc("yA", [128, 512])
            self.rec8 = sc("rec8", [128, 8])
            self._phase_A(l, st, t0, L)

    def _phase_A(self, l, st, t0, L):
        NS = self.NSLOT
        kind = st["kind"]
        vpos0 = t0 if kind == "p" else 512
        nsub = (L + 127) // 128
        hk = self.hkeys()
        wq, wqk = self.wload("w_in", l, [(O_AQ, 512)])
        for h in range(8):
            b = self.proj_fm(wq, wqk, h * 64, 64, L)
            self.act(self.QT[:, h, 0:L], self.ps[0:64, b, 0:L], AF.Copy, r=[("ps", b)], w=[("QT", h)], scale=0.125)
        wkk, wkkk = self.wload("w_in", l, [(O_AK, 512)])
        slot0 = (vpos0 // 128) % NS
        kkeys = [("KT", l, (slot0 + j) % NS) for j in range(nsub)]
        for h in range(8):
            b = self.proj_fm(wkk, wkkk, h * 64, 64, L)
            self.cp("vector" if h % 2 else "scalar", self.KT[l][:, h, slot0 * 128:slot0 * 128 + L], self.ps[0:64, b, 0:L],
                    r=[("ps", b)], w=kkeys)
        keep_lo = st["keep_lo"]
        for j in range(nsub):
            n = min(128, L - j * 128)
            if t0 + j * 128 + n <= keep_lo:
                continue
            b = self.proj_tm(wkk, wkkk, 0, 512, j * 128, n)
            i = self.rot("io", 2)
            ko, kk = self.io[i][:, 0:512], ("io", i)
            self.cp("scalar", ko[0:n, :], self.ps[0:n, b, :], r=[("ps", b)], w=[kk])
            r0 = t0 + j * 128 - keep_lo
            self.dma("sync", st["ak_out"][l][r0:r0 + n, :], ko[0:n, :], r=[kk])
        wvv, wvk = self.wload("w_in", l, [(O_AV, 512)])
        for j in range(nsub):
            n = min(128, L - j * 128)
            slot = (slot0 + j) % NS
            b = self.proj_tm(wvv, wvk, 0, 512, j * 128, n)
            self.cp("vector", self.VH[l][0:n, slot, :, 0:64], self.ps[0:n, b, :].rearrange("p (h d) -> p h d", h=8),
                    r=[("ps", b)], w=[("VH", l)])
            if t0 + j * 128 + n > keep_lo:
                i = self.rot("io", 2)
                ko, kk = self.io[i][:, 0:512], ("io", i)
                self.cp("scalar", ko[0:n, :], self.ps[0:n, b, :], r=[("ps", b)], w=[kk])
                r0 = t0 + j * 128 - keep_lo
                self.dma("sync", st["av_out"][l][r0:r0 + n, :], ko[0:n, :], r=[kk])
        self._nb = 4
        for j in range(nsub):
            n = min(128, L - j * 128)
            q0 = j * 128
            vq = vpos0 + q0
            jmin = max(0, 4 - vq // 128)
            tiles = []
            for jt in range(jmin, 5):
                slot = ((vq - 512 + 128 * jt) // 128) % NS
                tiles.append((jt, slot, n if jt == 4 else 128))
            for hp in range(4):
                bn = self.ptmp()
                bf_ = [None, None]
                for hh in range(2):
                    h = hp * 2 + hh
                    far = [t for t in tiles if t[0] < 3]
                    if far:
                        bf_[hh] = self.ptmp()
                    for (jt, slot, nk) in tiles:
                        if jt < 3:
                            out = self.ps[0:nk, bf_[hh], jt * 128:jt * 128 + n]
                            wkey = ("ps", bf_[hh])
                        else:
                            col = (hh * 2 + (jt - 3)) * 128
                            out = self.ps[0:nk, bn, col:col + n]
                            wkey = ("ps", bn)
                        self.mm(out, self.KT[l][:, h, slot * 128:slot * 128 + nk], self.QT[:, h, q0:q0 + n],
                                r=[("KT", l, slot), ("QT", h)], w=[wkey])
                    if far:
                        j0 = far[0][0]
                        self.act(self.PT[:, h, j0:3, 0:n], self.ps[:, bf_[hh], :].rearrange("p (j q) -> p j q", j=4)[:, j0:3, 0:n],
                                 AF.Exp, r=[("ps", bf_[hh]), "params"], w=[("PT", h)], bias=self.a_bc[:, l, h:h + 1])
                        if j0 == 0 and n > 64:
                            self.memset("gpsimd", self.PT[0:64, h, 0, 64:n], 0.0, w=[("PT", h)])
                near = [t for t in tiles if t[0] >= 3]
                for (jt, slot, nk) in near:
                    jj = jt - 3
                    src = self.ps[0:nk, bn, :].rearrange("p (h j q) -> p h j q", h=2, j=2)[:, :, jj, 0:n]
                    dst = self.PT[0:nk, hp * 2:hp * 2 + 2, jt, 0:n]
                    self.act(dst, src, AF.Exp, r=[("ps", bn)], w=[("PT", hp * 2), ("PT", hp * 2 + 1)])
                    self.tt("vector", dst, dst, self.Etab[0:nk, l, hp * 2:hp * 2 + 2, jj, 0:n], ALU.mult,
                            r=[("PT", hp * 2), ("PT", hp * 2 + 1), ("E", l)], w=[("PT", hp * 2), ("PT", hp * 2 + 1)])
            for h in range(8):
                ob = 4 + h // 4
                oview = self.ps[0:n, ob, :].rearrange("p (h d) -> p h d", d=128)[:, h % 4, 0:65]
                for ti, (jt, slot, nk) in enumerate(tiles):
                    self.mm(oview, self.PT[0:nk, h, jt, 0:n], self.VH[l][0:nk, slot, h, :], start=(ti == 0), stop=(ti == len(tiles) - 1),
                            r=[("PT", h), ("VH", l)], w=[("ps", ob)])
            for g in range(2):
                ob = 4 + g
                ov = self.ps[0:n, ob, :].rearrange("p (h d) -> p h d", d=128)
                self.S.op("vector", lambda e, o=self.rec8[0:n, g * 4:g * 4 + 4], i_=ov[:, :, 64]: e.reciprocal(out=o, in_=i_),
                          reads=[("ps", ob)], writes=["rec8"])
                self.tt("vector", self.yA[0:n, g * 256:(g + 1) * 256].rearrange("p (h d) -> p h d", h=4), ov[:, :, 0:64],
                        self.rec8[0:n, g * 4:g * 4 + 4].unsqueeze(2).broadcast_to([n, 4, 64]), ALU.mult,
                        r=[("ps", ob), "rec8"], w=["yA"])
            b = self.ptmp()
            for c in range(4):
                self.tr(self.ps[:, b, c * 128:c * 128 + n], self.yA[0:n, c * 128:(c + 1) * 128], r=["yA"], w=[("ps", b)])
            self.cp("scalar", self.brT[:, 0:4, q0:q0 + n], self.ps[:, b, :].rearrange("p (c t) -> p c t", c=4)[:, :, 0:n],
                    r=[("ps", b)], w=[("brT", c) for c in range(4)])
        self._nb = 8
        self.merge_branch(l, L, "w_br_a", 4, O_GATES, first=True)
    def raw_proj_conv(self, l, L, col_off, hist, hkey, cw, cb):
        self.cp("gpsimd", self.RAW[:, :, 0:3], hist[:, l, :, :], r=[hkey], w=[("RAW", c) for c in range(12)])
        for c in range(12):
            if c % 4 == 0:
                wv, wk = self.wload("w_in", l, [(col_off + c * 128, 512)])
            b = self.proj_fm(wv, wk, (c % 4) * 128, 128, L)
            rk, ck = ("RAW", c), ("CO", c)
            self.cp("scalar" if c % 2 else "vector", self.RAW[:, c, 3:3 + L], self.ps[:, b, 0:L], r=[("ps", b)], w=[rk])
            if cb is None:
                self.act(self.CO[:, c, 0:L], self.RAW[:, c, 3:3 + L], AF.Copy, r=[rk, "params"], w=[ck], scale=cw[:, l, 3, c:c + 1])
            else:
                self.act(self.CO[:, c, 0:L], self.RAW[:, c, 3:3 + L], AF.Identity, r=[rk, "params"], w=[ck],
                         scale=cw[:, l, 3, c:c + 1], bias=cb[:, l, c:c + 1])
            for k in range(3):
                self.stt(self.CO[:, c, 0:L], self.RAW[:, c, k:k + L], cw[:, l, k, c:c + 1], self.CO[:, c, 0:L], ALU.mult, ALU.add,
                         r=[rk, ck, "params"], w=[ck])
            if c > 0:
                self.act(self.CO[:, c - 1, 0:L], self.CO[:, c - 1, 0:L], AF.Silu, r=[("CO", c - 1)], w=[("CO", c - 1)])
        self.act(self.CO[:, 11, 0:L], self.CO[:, 11, 0:L], AF.Silu, r=[("CO", 11)], w=[("CO", 11)])
        self.cp("gpsimd", hist[:, l, :, :], self.RAW[:, :, L:L + 3], r=[("RAW", c) for c in range(12)], w=[hkey])

    GDN_TMPS = (("KVt", [128, 2, 2, 128]), ("DD", [128, 2, 2, 128]), ("EG", [128, 2, 128]),
                ("QKD", [128, 2, 128]), ("LL", [128, 2, 2, 128]), ("XX", [128, 2, 2, 128]), ("YY", [128, 2, 2, 128]),
                ("Rp", [128, 2, 128]), ("UU", [128, 2, 128]), ("qg", [128, 2, 128]), ("kd", [128, 2, 128]),
                ("gct", [128, 2]), ("egc", [128, 2]), ("wdec", [128, 2]))

    def phase_B(self, l, st, L):
        T = self.T
        with self.scope() as sc:
            self.CO = sc("CO", [128, 12, T])
            self.OB = sc("OB", [128, 4, T])
            with self.scope() as sc1:
                self.RAW = sc1("RAW", [128, 12, T + 3])
                self._phase_B1(l, st, L)
            with self.scope() as sc2:
                self.G = []
                for g in range(2):
                    d = {nm: sc2(f"{nm}{g}", shp) for nm, shp in self.GDN_TMPS}
                    d["tag"] = g
                    self.G.append(d)
                self._phase_B2(l, st, L)

    def _phase_B1(self, l, st, L):
        P = ["params"]
        nsub = (L + 127) // 128
        self.raw_proj_conv(l, L, O_BQKV, self.bhist, "bhist", self.bcw, None)
        wd, wdk = self.wload("w_in", l, [(O_BBETA, 8)])
        for j in range(nsub):
            n = min(128, L - j * 128)
            b = self.proj_tm(wd, wdk, 0, 8, j * 128, n)
            self.cp("vector", self.bd[0:n, j, :], self.ps[0:n, b, 0:8], r=[("ps", b)], w=["bd"])
        nfull = min(128, L)
        J = slice(0, nsub)
        self.act(self.bet[0:nfull, J, :], self.bd[0:nfull, J, 0:4], AF.Exp, r=["bd"], w=["bet"], scale=-1.0)
        self.ts("vector", self.bet[0:nfull, J, :], self.bet[0:nfull, J, :], 1.0, None, ALU.add, r=["bet"], w=["bet"])
        self.S.op("vector", lambda e, o=self.bet[0:nfull, J, :]: e.reciprocal(out=o, in_=o), reads=["bet"], writes=["bet"])
        self.ts("vector", self.nbet[0:nfull, J, :], self.bet[0:nfull, J, :], -1.0, None, ALU.mult, r=["bet"], w=["nbet"])
        self.tt("vector", self.gg[0:nfull, J, :], self.bd[0:nfull, J, 4:8],
                self.b_dtb[0:nfull, l:l + 1, :].broadcast_to([nfull, nsub, 4]), ALU.add, r=["bd"] + P, w=["gg"])
        self.act(self.gg[0:nfull, J, :], self.gg[0:nfull, J, :], AF.Exp, r=["gg"], w=["gg"])
        self.act(self.gg[0:nfull, J, :], self.gg[0:nfull, J, :], AF.Ln, r=["gg"], w=["gg"], bias=1.0)
        self.tt("vector", self.gg[0:nfull, J, :], self.gg[0:nfull, J, :],
                self.b_nA[0:nfull, l:l + 1, :].broadcast_to([nfull, nsub, 4]), ALU.mult, r=["gg"] + P, w=["gg"])
        cok = [("CO", c) for c in range(8)]
        rk8 = [("RAW", c) for c in range(8)]
        self.act(self.sqb[:, 0:8, 0:L], self.CO[:, 0:8, 0:L], AF.Square, r=cok, w=["sqb0", "sqb1"])
        for c2 in range(4):
            b = self.ptmp()
            pv = self.ps[:, b, :].rearrange("p (c t) -> p c t", c=2)
            for cc in range(2):
                self.mm(pv[:, cc, 0:L], self.ones_b[:], self.sqb[:, c2 * 2 + cc, 0:L], r=["ones", "sqb0", "sqb1"], w=[("ps", b)])
            rv = self.RAW[:, c2 * 2:c2 * 2 + 2, 0:L]
            self.act(rv, pv[:, :, 0:L], AF.Ln, r=[("ps", b), "ones"], w=rk8[c2 * 2:c2 * 2 + 2], bias=self.eps_col[:, 0:1])
            self.act(rv, rv, AF.Exp, r=rk8[c2 * 2:c2 * 2 + 2], w=rk8[c2 * 2:c2 * 2 + 2], scale=-0.5)
        self.stt(self.CO[:, 0:4, 0:L], self.CO[:, 0:4, 0:L], float(B_DH ** -0.5), self.RAW[:, 0:4, 0:L], ALU.mult, ALU.mult,
                 r=cok[0:4] + rk8[0:4], w=cok[0:4])
        self.tt("vector", self.CO[:, 4:8, 0:L], self.CO[:, 4:8, 0:L], self.RAW[:, 4:8, 0:L], ALU.mult, r=cok[4:8] + rk8[4:8], w=cok[4:8])

    def _phase_B2(self, l, st, L):
        P = ["params"]
        nsub = (L + 127) // 128
        self._nb = 4
        for j in range(nsub):
            C = min(128, L - j * 128)
            t0 = j * 128
            nl = max(1, (C - 1).bit_length())
            gens = [self.gdn_chunk(l, j, t0, C, nl, [2 * g, 2 * g + 1], self.G[g]) for g in range(2)]
            live = list(gens)
            while live:
                for g in list(live):
                    try:
                        next(g)
                    except StopIteration:
                        live.remove(g)
        self._nb = 8
        obk = [("OB", h) for h in range(4)]
        self.act(self.sqb[:, 0:4, 0:L], self.OB[:, :, 0:L], AF.Square, r=obk, w=["sqb0", "sqb1"])
        for h2 in range(2):
            b = self.ptmp()
            pv = self.ps[:, b, :].rearrange("p (c t) -> p c t", c=2)
            for cc in range(2):
                self.mm(pv[:, cc, 0:L], self.ones_b[:], self.sqb[:, h2 * 2 + cc, 0:L], r=["ones", "sqb0", "sqb1"], w=[("ps", b)])
            rv = self.sq[:, 4 + h2 * 2:6 + h2 * 2, 0:L]
            self.act(rv, pv[:, :, 0:L], AF.Ln, r=[("ps", b), "ones"], w=["sq2"], scale=1.0 / B_DH, bias=self.eps_col[:, 0:1])
            self.act(rv, rv, AF.Exp, r=["sq2"], w=["sq2"], scale=-0.5)
        for h in range(4):
            self.stt(self.OB[:, h, 0:L], self.OB[:, h, 0:L], self.g_bn[:, l:l + 1], self.sq[:, 4 + h, 0:L], ALU.mult, ALU.mult,
                     r=[("OB", h), "sq2"] + P, w=[("OB", h)])
        wg, wgk = self.wload("w_in", l, [(O_BGATE, 512)])
        for h in range(4):
            bg = self.proj_fm(wg, wgk, h * 128, 128, L)
            i2 = self.rot("gsig", 2)
            g2 = self.gsig[i2]
            self.act(g2[:, 0:L], self.ps[:, bg, 0:L], AF.Silu, r=[("ps", bg)], w=[("gsig", i2)])
            self.tt("vector", self.brT[:, h, 0:L], self.OB[:, h, 0:L], g2[:, 0:L], ALU.mult,
                    r=[("OB", h), ("gsig", i2)], w=[("brT", h)])
        self.merge_branch(l, L, "w_br_b", 4, O_GATES + D, first=False)

    def gdn_chunk(self, l, j, t0, C, nl, hs, G):
        HG = len(hs)
        h0 = hs[0]
        tg = G["tag"]
        K = lambda nm: (nm, tg)
        CON = ["consts"]
        KVt, DD, EG, QKD, LL, XX, YY = G["KVt"], G["DD"], G["EG"], G["QKD"], G["LL"], G["XX"], G["YY"]
        Rp, UU, qg, kd, gct, egc, wdec = G["Rp"], G["UU"], G["qg"], G["kd"], G["gct"], G["egc"], G["wdec"]
        b = self.ptmp()
        for i, h in enumerate(hs):
            self.tr(self.ps[0:C, b, i * 128:(i + 1) * 128], self.CO[:, 4 + h, t0:t0 + C], r=[("CO", 4 + h)], w=[("ps", b)])
            self.tr(self.ps[0:C, b, (HG + i) * 128:(HG + i + 1) * 128], self.CO[:, 8 + h, t0:t0 + C], r=[("CO", 8 + h)], w=[("ps", b)])
        self.cp("scalar", KVt[0:C].rearrange("p a h d -> p (a h d)"), self.ps[0:C, b, 0:2 * HG * 128], r=[("ps", b)], w=[K("KVt")])
        gsl = self.gg[0:C, j, h0:h0 + HG]
        yield
        bg = self.ptmp()
        gcv = self.ps[:, bg, 0:HG * 128].rearrange("p (h i) -> p h i", h=HG)
        for i in range(HG):
            self.mm(gcv[:, i, 0:C], gsl[:, i:i + 1].broadcast_to([C, 128]), self.triI[0:C, 0:C], r=["gg"] + CON, w=[("ps", bg)])
        self.mm(self.ps[0:C, bg, 384:384 + HG], self.triI[0:C, 0:C], gsl, r=["gg"] + CON, w=[("ps", bg)])
        self.cp("vector", gct[0:C, :], self.ps[0:C, bg, 384:384 + HG], r=[("ps", bg)], w=[K("gct")])
        for i in range(HG):
            self.stt(DD[0:C, 0, i, 0:C], gcv[0:C, i, 0:C], gct[0:C, i:i + 1], self.negU[0:C, 0:C], ALU.subtract, ALU.add,
                     r=[("ps", bg), K("gct")] + CON, w=[K("DD")])
            self.stt(DD[0:C, 1, i, 0:C], gcv[0:C, i, 0:C], gct[0:C, i:i + 1], self.zeros[0:C, 0:C], ALU.subtract, ALU.max,
                     r=[("ps", bg), K("gct"), "ones"], w=[K("DD")])
        self.tt("vector", wdec[0:C, :], gcv[0:C, :, C - 1], gct[0:C, :], ALU.subtract, r=[("ps", bg), K("gct")], w=[K("wdec")])
        self.act(EG[:, :, 0:C], gcv[:, :, 0:C], AF.Exp, r=[("ps", bg)], w=[K("EG")])
        self.act(egc[0:C, :], gct[0:C, :], AF.Exp, r=[K("gct")], w=[K("egc")])
        self.act(wdec[0:C, :], wdec[0:C, :], AF.Exp, r=[K("wdec")], w=[K("wdec")])
        self.act(DD[0:C, 0, :, 0:C], DD[0:C, 0, :, 0:C], AF.Exp, r=[K("DD")], w=[K("DD")])
        self.act(DD[0:C, 1, :, 0:C], DD[0:C, 1, :, 0:C], AF.Exp, r=[K("DD")], w=[K("DD")], scale=-1.0)
        self.tt("vector", DD[0:C, 1, :, 0:C], DD[0:C, 1, :, 0:C], self.triSL[0:C, 0:C].unsqueeze(1).broadcast_to([C, HG, C]),
                ALU.mult, r=[K("DD")] + CON, w=[K("DD")])
        yield
        bk = self.ptmp()
        kv = self.ps[:, bk, :].rearrange("p (a h i) -> p a h i", a=2, h=2)
        for i, h in enumerate(hs):
            kT = self.CO[:, 4 + h, t0:t0 + C]
            self.mm(kv[0:C, 0, i, 0:C], kT, kT, r=[("CO", 4 + h)], w=[("ps", bk)])
            self.mm(kv[0:C, 1, i, 0:C], kT, self.CO[:, h, t0:t0 + C], r=[("CO", 4 + h), ("CO", h)], w=[("ps", bk)])
        for i, h in enumerate(hs):
            self.stt(LL[0:C, i, 0, 0:C], kv[0:C, 0, i, 0:C], self.bet[0:C, j, h:h + 1], DD[0:C, 1, i, 0:C], ALU.mult, ALU.mult,
                     r=[("ps", bk), "bet", K("DD")], w=[K("LL")])
        self.tt("vector", QKD[0:C, :, 0:C], kv[0:C, 1, 0:HG, 0:C], DD[0:C, 0, :, 0:C], ALU.mult, r=[("ps", bk), K("DD")], w=[K("QKD")])
        yield
        bt = self.ptmp()
        for i in range(HG):
            self.tr(self.ps[0:C, bt, i * 128:i * 128 + C], LL[0:C, i, 0, 0:C], r=[K("LL")], w=[("ps", bt)])
        self.cp("scalar", LL[0:C, :, 1, 0:C], self.ps[0:C, bt, 0:HG * 128].rearrange("p (h i) -> p h i", h=HG)[:, :, 0:C],
                r=[("ps", bt)], w=[K("LL")])
        mm0 = self.mmask[0:C, 0, :, 0:C].unsqueeze(1).broadcast_to([C, HG, 2, C])
        self.tt("vector", XX[0:C, :, :, 0:C], LL[0:C, :, :, 0:C], mm0, ALU.mult, r=[K("LL")] + CON, w=[K("XX")])
        self.tt("vector", XX[0:C, :, :, 0:C], self.ii[0:C, :, 0:C].unsqueeze(1).broadcast_to([C, HG, 2, C]), XX[0:C, :, :, 0:C],
                ALU.subtract, r=[K("XX")] + CON, w=[K("XX")])
        yield
        for lev in range(1, nl):
            by = self.ptmp()
            yv = self.ps[:, by, :].rearrange("p (h a i) -> p h a i", h=2, a=2)
            last = (lev == nl - 1)
            for i in range(HG):
                if not last:
                    self.mm(yv[0:C, i, 0, 0:C], LL[0:C, i, 1, 0:C], XX[0:C, i, 0, 0:C], r=[K("LL"), K("XX")], w=[("ps", by)])
                self.mm(yv[0:C, i, 1, 0:C], LL[0:C, i, 0, 0:C], XX[0:C, i, 1, 0:C], r=[K("LL"), K("XX")], w=[("ps", by)])
            if last:
                mml = self.mmask[0:C, lev, 1, 0:C].unsqueeze(1).broadcast_to([C, HG, C])
                self.tt("vector", YY[0:C, :, 1, 0:C], yv[0:C, 0:HG, 1, 0:C], mml, ALU.mult, r=[("ps", by)] + CON, w=[K("YY")])
            else:
                mml = self.mmask[0:C, lev, :, 0:C].unsqueeze(1).broadcast_to([C, HG, 2, C])
                self.tt("vector", YY[0:C, :, :, 0:C], yv[0:C, 0:HG, :, 0:C], mml, ALU.mult, r=[("ps", by)] + CON, w=[K("YY")])
            yield
            bz = self.ptmp()
            zv = self.ps[:, bz, :].rearrange("p (h a i) -> p h a i", h=2, a=2)
            for i in range(HG):
                if not last:
                    self.mm(zv[0:C, i, 0, 0:C], XX[0:C, i, 1, 0:C], YY[0:C, i, 0, 0:C], r=[K("XX"), K("YY")], w=[("ps", bz)])
                self.mm(zv[0:C, i, 1, 0:C], XX[0:C, i, 0, 0:C], YY[0:C, i, 1, 0:C], r=[K("XX"), K("YY")], w=[("ps", bz)])
            if last:
                self.tt("vector", XX[0:C, :, 1, 0:C], XX[0:C, :, 1, 0:C], zv[0:C, 0:HG, 1, 0:C], ALU.subtract,
                        r=[K("XX"), ("ps", bz)], w=[K("XX")])
            else:
                self.tt("vector", XX[0:C, :, :, 0:C], XX[0:C, :, :, 0:C], zv[0:C, 0:HG, :, 0:C], ALU.subtract,
                        r=[K("XX"), ("ps", bz)], w=[K("XX")])
            yield
        bs = self.ptmp()
        sv = self.ps[:, bs, :].rearrange("p (h d) -> p h d", h=4)
        for i, h in enumerate(hs):
            self.mm(sv[0:C, i, :], self.CO[:, 4 + h, t0:t0 + C], self.Sst[l][:, h, :], r=[("CO", 4 + h), ("S", l, h)], w=[("ps", bs)])
        for i, h in enumerate(hs):
            self.stt(Rp[0:C, i, :], sv[0:C, i, :], egc[0:C, i:i + 1], KVt[0:C, 1, i, :], ALU.mult, ALU.subtract,
                     r=[("ps", bs), K("egc"), K("KVt")], w=[K("Rp")])
            self.act(Rp[0:C, i, :], Rp[0:C, i, :], AF.Copy, r=[K("Rp"), "nbet"], w=[K("Rp")], scale=self.nbet[0:C, j, h:h + 1])
        yield
        bu = self.ptmp()
        uv = self.ps[:, bu, :].rearrange("p (h d) -> p h d", h=4)
        for i in range(HG):
            self.mm(uv[0:C, i, :], XX[0:C, i, 1, 0:C], Rp[0:C, i, :], r=[K("XX"), K("Rp")], w=[("ps", bu)])
        self.cp("scalar", UU[0:C, :, :], uv[0:C, 0:HG, :], r=[("ps", bu)], w=[K("UU")])
        self.tt("gpsimd", qg[:, :, 0:C], self.CO[:, h0:h0 + HG, t0:t0 + C], EG[:, :, 0:C], ALU.mult,
                r=[("CO", h) for h in hs] + [K("EG")], w=[K("qg")])
        for i in range(HG):
            self.act(kd[0:C, i, :], KVt[0:C, 0, i, :], AF.Copy, r=[K("KVt"), K("wdec")], w=[K("kd")], scale=wdec[0:C, i:i + 1])
        yield
        bo = self.ptmp()
        ov = self.ps[:, bo, :].rearrange("p (h i) -> p h i", h=4)
        for i, h in enumerate(hs):
            self.mm(ov[:, i, 0:C], self.Sst[l][:, h, :], qg[:, i, 0:C], start=True, stop=False, r=[("S", l, h), K("qg")], w=[("ps", bo)])
            self.mm(ov[:, i, 0:C], UU[0:C, i, :], QKD[0:C, i, 0:C], start=False, stop=True, r=[K("UU"), K("QKD")], w=[("ps", bo)])
        self.cp("scalar", self.OB[:, h0:h0 + HG, t0:t0 + C], ov[:, 0:HG, 0:C], r=[("ps", bo)], w=[("OB", h) for h in hs])
        bn = self.ptmp()
        nv = self.ps[:, bn, :].rearrange("p (h d) -> p h d", h=4)
        for i in range(HG):
            self.mm(nv[:, i, :], kd[0:C, i, :], UU[0:C, i, :], r=[K("kd"), K("UU")], w=[("ps", bn)])
        for i, h in enumerate(hs):
            self.stt(self.Sst[l][:, h, :], self.Sst[l][:, h, :], EG[:, i, C - 1:C], nv[:, i, :], ALU.mult, ALU.add,
                     r=[("S", l, h), K("EG"), ("ps", bn)], w=[("S", l, h)])
        yield
    def phase_C(self, l, st, L):
        T = self.T
        P = ["params"]
        nsub = (L + 127) // 128
        with self.scope() as sc:
            self.CO = sc("CO", [128, 12, T])
            with self.scope() as sc1:
                self.RAW = sc1("RAW", [128, 12, T + 3])
                self.raw_proj_conv(l, L, O_CXBC, self.chist, "chist", self.ccw, self.ccb)
                wd, wdk = self.wload("w_in", l, [(O_CDT, 16)])
                for j in range(nsub):
                    n = min(128, L - j * 128)
                    b = self.proj_tm(wd, wdk, 0, 16, j * 128, n)
                    self.tt("vector", self.dtr[0:n, j, :], self.ps[0:n, b, 0:16], self.c_dtb[0:n, l, :], ALU.add, r=[("ps", b)] + P, w=["dtr"])
                    self.act(self.dtr[0:n, j, :], self.dtr[0:n, j, :], AF.Exp, r=["dtr"], w=["dtr"])
                    self.act(self.dtt[0:n, j, :], self.dtr[0:n, j, :], AF.Ln, r=["dtr"], w=["dtt"], bias=1.0)
                    self.tt("vector", self.aa[0:n, j, :], self.dtt[0:n, j, :], self.c_nA[0:n, l, :], ALU.mult, r=["dtt"] + P, w=["aa"])
            with self.scope() as sc2:
                for nm, shp in (("xtok", [128, 16, 64]), ("xdt", [128, 16, 64]), ("Btok", [128, 2, 128]), ("CBm", [128, 2, 128]),
                                ("act_", [128, 16]), ("nact", [128, 16]), ("eact", [128, 16]), ("AL", [128, 16]), ("EAL", [128, 16]), ("wst", [128, 16]),
                                ("Yt", [128, 16, 64]), ("xsk", [128, 16, 64]), ("YZ", [128, KD, T])):
                    setattr(self, nm, sc2(nm, shp))
                self.MTm = [sc2(f"MTm{i}", [128, 4, 128], BF16) for i in range(4)]
                self.xdtb = sc2("xdtb", [128, 16, 64], BF16)
                for j in range(nsub):
                    C = min(128, L - j * 128)
                    self.ssd_chunk(l, j, j * 128, C)
                for n in range(KD):
                    if n % 4 == 0:
                        wz, wzk = self.wload("w_in", l, [(O_CZ + n * 128, 512)])
                    b = self.proj_fm(wz, wzk, (n % 4) * 128, 128, L)
                    i = self.rot("gsig", 2)
                    gs = self.gsig[i]
                    self.act(gs[:, 0:L], self.ps[:, b, 0:L], AF.Silu, r=[("ps", b)], w=[("gsig", i)])
                    self.tt("gpsimd", self.YZ[:, n, 0:L], self.YZ[:, n, 0:L], gs[:, 0:L], ALU.mult, r=[("YZ", n), ("gsig", i)], w=[("YZ", n)])
                self.rmsnorm(self.YZ, "YZ", self.g_cn[:, l, :], L, self.brT, "brT")
        self.merge_branch(l, L, "w_br_c", 8, O_GATES + 2 * D, first=False)

    def ssd_chunk(self, l, j, t0, C):
        self._nb = 4
        self._ssd_chunk(l, j, t0, C)
        self._nb = 8

    def _ssd_chunk(self, l, j, t0, C):
        CON = ["consts"]
        xk = [("CO", c) for c in range(8)]
        for g in range(2):
            b = self.ptmp()
            for cc in range(4):
                self.tr(self.ps[0:C, b, cc * 128:(cc + 1) * 128], self.CO[:, g * 4 + cc, t0:t0 + C], r=[("CO", g * 4 + cc)], w=[("ps", b)])
            self.cp("scalar" if g else "vector", self.xtok[0:C, g * 8:g * 8 + 8, :].rearrange("p h d -> p (h d)"), self.ps[0:C, b, :],
                    r=[("ps", b)], w=["xtok"])
        b = self.ptmp()
        for g in range(2):
            self.tr(self.ps[0:C, b, g * 128:(g + 1) * 128], self.CO[:, 8 + g, t0:t0 + C], r=[("CO", 8 + g)], w=[("ps", b)])
        for g in range(2):
            self.mm(self.ps[0:C, b, 256 + g * 128:256 + g * 128 + C], self.CO[:, 8 + g, t0:t0 + C], self.CO[:, 10 + g, t0:t0 + C],
                    r=[("CO", 8 + g), ("CO", 10 + g)], w=[("ps", b)])
        self.cp("scalar", self.Btok[0:C, :, :].rearrange("p g n -> p (g n)"), self.ps[0:C, b, 0:256], r=[("ps", b)], w=["Btok"])
        self.tt("vector", self.CBm[0:C, :, 0:C], self.ps[0:C, b, 256:512].rearrange("p (g i) -> p g i", g=2)[:, :, 0:C],
                self.triI[0:C, 0:C].unsqueeze(1).broadcast_to([C, 2, C]), ALU.mult, r=[("ps", b)] + CON, w=["CBm"])
        asl = self.aa[0:C, j, :]
        ba = self.ptmp()
        self.mm(self.ps[0:C, ba, 0:16], self.triI[0:C, 0:C], asl, r=["aa"] + CON, w=[("ps", ba)])
        self.cp("vector", self.act_[0:C, :], self.ps[0:C, ba, 0:16], r=[("ps", ba)], w=["act_"])
        self.act(self.eact[0:C, :], self.ps[0:C, ba, 0:16], AF.Exp, r=[("ps", ba)], w=["eact"])
        self.tt("gpsimd", self.xsk[0:C], self.xtok[0:C], self.c_dsk[0:C, l, :].unsqueeze(2).broadcast_to([C, 16, 64]), ALU.mult,
                r=["xtok", "params"], w=["xsk"])
        self.tt("vector", self.xdtb[0:C], self.xtok[0:C], self.dtt[0:C, j, :].unsqueeze(2).broadcast_to([C, 16, 64]), ALU.mult,
                r=["xtok", "dtt"], w=["xdtb"])
        for g in range(2):
            self.mm(self.ps[0:C, 6 + g, :], self.CO[:, 10 + g, t0:t0 + C], self.Hst[l][:, g * 8:g * 8 + 8, :].rearrange("n h p -> n (h p)"),
                    r=[("CO", 10 + g), ("H", l)], w=[("ps", 6 + g)])
        self.ts("vector", self.nact[0:C, :], self.act_[0:C, :], -1.0, None, ALU.mult, r=["act_"], w=["nact"])

        def stage1(hq):
            i = self.rot("MTm", 4)
            mt, mk = self.MTm[i], ("MTm", i)
            bb = self.ptmp()
            av = self.ps[:, bb, :].rearrange("p (h i) -> p h i", h=4)
            for hh in range(4):
                h = hq * 4 + hh
                self.mm(av[0:C, hh, 0:C], asl[:, h:h + 1].broadcast_to([C, C]), self.triI[0:C, 0:C], start=True, stop=False,
                        r=["aa"] + CON, w=[("ps", bb)])
                self.mm(av[0:C, hh, 0:C], self.identb[0:C, 0:C], self.negUb[0:C, 0:C], start=False, stop=True, r=["constsb"], w=[("ps", bb)])
            self.cp("scalar", self.AL[0:C, hq * 4:hq * 4 + 4], av[0:C, :, C - 1], r=[("ps", bb)], w=["AL"])
            for hh in range(4):
                h = hq * 4 + hh
                self.act(mt[0:C, hh, 0:C], av[0:C, hh, 0:C], AF.Exp, r=[("ps", bb), "nact"], w=[mk], bias=self.nact[0:C, h:h + 1])
            g = hq // 2
            self.tt("vector", mt[0:C, :, 0:C], mt[0:C, :, 0:C], self.CBm[0:C, g:g + 1, 0:C].broadcast_to([C, 4, C]), ALU.mult,
                    r=[mk, "CBm"], w=[mk])
            return mt, mk

        def stage2(hq, mt, mk):
            for hh in range(4):
                h = hq * 4 + hh
                ob = 4 + h // 8
                self.mm(self.ps[0:C, ob, (h % 8) * 64:(h % 8) * 64 + 64], mt[0:C, hh, 0:C], self.xdtb[0:C, h, :], r=[mk, "xdtb"], w=[("ps", ob)])

        prev = None
        for hq in range(4):
            cur = stage1(hq)
            if prev is not None:
                stage2(hq - 1, *prev)
            prev = cur
        stage2(3, *prev)
        for g in range(2):
            yv = self.Yt[0:C, g * 8:g * 8 + 8, :]
            self.tt("vector", yv, self.ps[0:C, 6 + g, :].rearrange("p (h d) -> p h d", h=8),
                    self.eact[0:C, g * 8:g * 8 + 8].unsqueeze(2).broadcast_to([C, 8, 64]), ALU.mult, r=[("ps", 6 + g), "eact"], w=[("Yt", g)])
            self.tt("vector", yv, yv, self.ps[0:C, 4 + g, :].rearrange("p (h d) -> p h d", h=8), ALU.add, r=[("Yt", g), ("ps", 4 + g)], w=[("Yt", g)])
            self.tt("vector", yv, yv, self.xsk[0:C, g * 8:g * 8 + 8, :], ALU.add, r=[("Yt", g), "xsk"], w=[("Yt", g)])
        self.tt("vector", self.wst[0:C, :], self.AL[0:C, :], self.act_[0:C, :], ALU.subtract, r=["AL", "act_"], w=["wst"])
        self.act(self.wst[0:C, :], self.wst[0:C, :], AF.Exp, r=["wst"], w=["wst"])
        self.tt("vector", self.wst[0:C, :], self.wst[0:C, :], self.dtt[0:C, j, :], ALU.mult, r=["wst", "dtt"], w=["wst"])
        self.tt("vector", self.xdt[0:C], self.xtok[0:C], self.wst[0:C, :].unsqueeze(2).broadcast_to([C, 16, 64]), ALU.mult,
                r=["xtok", "wst", "xdt"], w=["xdt"])
        bl = self.ptmp()
        self.mm(self.ps[:, bl, 0:16], self.ones_f[0:C, :], asl, r=["ones", "aa"], w=[("ps", bl)])
        self.act(self.EAL[:, :], self.ps[:, bl, 0:16], AF.Exp, r=[("ps", bl)], w=["EAL"])
        for g in range(2):
            bh = self.ptmp()
            self.mm(self.ps[:, bh, :], self.Btok[0:C, g, :], self.xdt[0:C, g * 8:g * 8 + 8, :].rearrange("p h d -> p (h d)"),
                    r=["Btok", "xdt"], w=[("ps", bh)])
            hv = self.Hst[l][:, g * 8:g * 8 + 8, :]
            self.tt("vector", hv, hv, self.EAL[:, g * 8:g * 8 + 8].unsqueeze(2).broadcast_to([128, 8, 64]), ALU.mult,
                    r=[("H", l), "EAL"], w=[("H", l)])
            self.tt("vector", hv, hv, self.ps[:, bh, :].rearrange("p (h d) -> p h d", h=8), ALU.add, r=[("H", l), ("ps", bh)], w=[("H", l)])
        for g in range(2):
            b = self.ptmp()
            for cc in range(4):
                c = g * 4 + cc
                self.tr(self.ps[:, b, cc * 128:cc * 128 + C], self.Yt[0:C, c * 2:c * 2 + 2, :].rearrange("p h d -> p (h d)"),
                        r=[("Yt", g)], w=[("ps", b)])
            self.cp("vector", self.YZ[:, g * 4:g * 4 + 4, t0:t0 + C], self.ps[:, b, :].rearrange("p (c t) -> p c t", c=4)[:, :, 0:C],
                    r=[("ps", b)], w=[("YZ", g * 4 + cc) for cc in range(4)])

    def phase_X(self, l, st, L):
        with self.scope() as sc:
            self.qxT = sc("qxT", [128, KD, self.T], BF16)
            self.PX = sc("PX", [128, 8, self.T], BF16)
            self.recx = [sc(f"recx{i}", [128, self.T]) for i in range(2)]
            self._phase_X(l, st, L)

    def _phase_X(self, l, st, L):
        for c in range(KD):
            self.cp("scalar" if c % 2 else "vector", self.qxT[:, c, 0:L], self.mT[:, c, 0:L], r=[("mT", c)], w=[("qxT", c)])
        self.resid_proj(l, L, "w_out", self.qxT, [("qxT", c) for c in range(KD)], KD, 2)
        self.rmsnorm(self.xT, "xT", self.g_x[:, l, :], L, self.hT, "hT")
        for n in range(KD):
            if n % 4 == 0:
                wv, wk = self.wload("wx_q", l, [(n * 128, 512)])
            b = self.proj_fm(wv, wk, (n % 4) * 128, 128, L)
            self.act(self.qxT[:, n, 0:L], self.ps[:, b, 0:L], AF.Copy, r=[("ps", b)], w=[("qxT", n)], scale=float(X_DH ** -0.5))
        for h in range(4):
            for mt in range(2):
                b = self.ptmp()
                for dc in range(2):
                    self.mm(self.ps[:, b, 0:L], self.MKT[l][:, h * 2 + dc, mt * 128:(mt + 1) * 128], self.qxT[:, h * 2 + dc, 0:L],
                            start=(dc == 0), stop=(dc == 1), r=[("MKT", l), ("qxT", h * 2 + dc)], w=[("ps", b)])
                self.act(self.PX[:, h * 2 + mt, 0:L], self.ps[:, b, 0:L], AF.Exp, r=[("ps", b)], w=[("PX", h * 2 + mt)])
            bd = self.ptmp()
            for mt in range(2):
                self.mm(self.ps[:, bd, 0:L], self.ones_b[:], self.PX[:, h * 2 + mt, 0:L], start=(mt == 0), stop=(mt == 1),
                        r=["ones", ("PX", h * 2 + mt)], w=[("ps", bd)])
            i = self.rot("recx", 2)
            rx = self.recx[i]
            self.act(rx[:, 0:L], self.ps[:, bd, 0:L], AF.Ln, r=[("ps", bd)], w=[("recx", i)])
            self.act(rx[:, 0:L], rx[:, 0:L], AF.Exp, r=[("recx", i)], w=[("recx", i)], scale=-1.0)
            for dc in range(2):
                b = self.ptmp()
                for mt in range(2):
                    self.mm(self.ps[:, b, 0:L], self.MV[l][:, mt, (h * 2 + dc) * 128:(h * 2 + dc + 1) * 128], self.PX[:, h * 2 + mt, 0:L],
                            start=(mt == 0), stop=(mt == 1), r=[("MV", l), ("PX", h * 2 + mt)], w=[("ps", b)])
                self.tt("vector", self.brT[:, h * 2 + dc, 0:L], self.ps[:, b, 0:L], rx[:, 0:L], ALU.mult,
                        r=[("ps", b), ("recx", i)], w=[("brT", h * 2 + dc)])
        self.resid_proj(l, L, "wx_o", self.brT, [("brT", c) for c in range(KD)], KD, 2)

    def phase_F(self, l, st, L):
        with self.scope() as sc:
            self.gT = sc("gT", [128, KFF, self.T], BF16)
            self.vtmp = [sc(f"vtmp{i}", [128, self.T + 2]) for i in range(3)]
            self.ctmp = [sc(f"ctmp{i}", [128, self.T]) for i in range(3)]
            self._phase_F(l, st, L)

    def _phase_F(self, l, st, L):
        self.rmsnorm(self.xT, "xT", self.g_ffn[:, l, :], L, self.hT, "hT")
        P = ["params"]
        pend = None
        for blk in range(KFF // 2):
            wv, wk = self.wload("w_up", l, [(blk * 256, 256), (D_FF + blk * 256, 256)])
            for kk in range(2):
                k = blk * 2 + kk
                bu = self.proj_fm(wv, wk, kk * 128, 128, L)
                bv = self.proj_fm(wv, wk, 256 + kk * 128, 128, L)
                i = self.rot("vtmp", 3)
                vt, ct = self.vtmp[i], self.ctmp[i]
                vk, ck = ("vtmp", i), ("ctmp", i)
                self.cp("gpsimd", vt[:, 0:2], self.fhist[:, l, k, :], r=["fhist"], w=[vk])
                self.cp("scalar", vt[:, 2:2 + L], self.ps[:, bv, 0:L], r=[("ps", bv)], w=[vk])
                self.cp("gpsimd", self.fhist[:, l, k, :], vt[:, L:L + 2], r=[vk], w=["fhist"])
                self.act(ct[:, 0:L], self.ps[:, bv, 0:L], AF.Identity, r=[("ps", bv)] + P, w=[ck],
                         scale=self.fcw[:, l, 2, k:k + 1], bias=self.fcb[:, l, k:k + 1])
                self.stt(ct[:, 0:L], vt[:, 1:1 + L], self.fcw[:, l, 1, k:k + 1], ct[:, 0:L], ALU.mult, ALU.add, r=[vk, ck] + P, w=[ck])
                self.stt(ct[:, 0:L], vt[:, 0:L], self.fcw[:, l, 0, k:k + 1], ct[:, 0:L], ALU.mult, ALU.add, r=[vk, ck] + P, w=[ck])
                if pend is not None:
                    self.ffn_tail(*pend, L)
                pend = (k, bu, ct, ck)
        self.ffn_tail(*pend, L)
        self.resid_proj(l, L, "w_down", self.gT, [("gT", k) for k in range(KFF)], KFF, 8)

    def ffn_tail(self, k, bu, ct, ck, L):
        self.act(ct[:, 0:L], ct[:, 0:L], AF.Silu, r=[ck], w=[ck])
        self.tt("vector", self.gT[:, k, 0:L], self.ps[:, bu, 0:L], ct[:, 0:L], ALU.mult, r=[("ps", bu), ck], w=[("gT", k)])

    def store_hist(self, dst, hist, hkey, nch, k):
        for c0 in range(0, nch, 8):
            nc8 = min(8, nch - c0)
            i = self.rot("io", 2)
            xi, xk = self.io[i], ("io", i)
            for g0 in range(0, nc8, 4):
                ng = min(4, nc8 - g0)
                b = self.ptmp()
                for cc in range(ng):
                    self.tr(self.ps[0:k, b, cc * 128:(cc + 1) * 128], hist[:, c0 + g0 + cc, :], r=[hkey], w=[("ps", b)])
                self.cp("vector", xi[0:k, g0 * 128:(g0 + ng) * 128], self.ps[0:k, b, 0:ng * 128], r=[("ps", b)], w=[xk])
            self.dma("sync", dst[:, c0 * 128:(c0 + nc8) * 128], xi[0:k, 0:nc8 * 128], r=[xk])

    def store_states(self, st):
        o = st["outs"]
        for l in range(2):
            self.store_hist(o["b_conv"][l], self.bhist[:, l], "bhist", 12, 3)
            self.store_hist(o["c_conv"][l], self.chist[:, l], "chist", 12, 3)
            self.store_hist(o["ffn_conv"][l], self.fhist[:, l], "fhist", KFF, 2)
            self.dma("sync", o["b_rec"][l].rearrange("h k v -> k h v"), self.Sst[l][:], r=[("S", l, h) for h in range(4)])
            dst = o["c_ssm"][l].rearrange("h p n -> (h p) n").rearrange("(c q) n -> q c n", q=128)
            for g in range(2):
                b = self.ptmp()
                for cc in range(4):
                    c = g * 4 + cc
                    self.tr(self.ps[:, b, cc * 128:(cc + 1) * 128], self.Hst[l][:, c * 2:c * 2 + 2, :].rearrange("n h p -> n (h p)"),
                            r=[("H", l)], w=[("ps", b)])
                i = self.rot("io", 2)
                xi, xk = self.io[i], ("io", i)
                self.cp("vector", xi[:, 0:512], self.ps[:, b, :], r=[("ps", b)], w=[xk])
                self.dma("sync", dst[:, g * 4:g * 4 + 4, :], xi[:, 0:512].rearrange("q (c n) -> q c n", c=4), r=[xk])

    def run_stream(self, st):
        T = self.T
        self.init_stream(st)
        Ls = st["L"]
        for t0 in range(0, Ls, T):
            L = min(T, Ls - t0)
            self.load_x_tile(st["x"][t0:t0 + L, :], L)
            ph = self.dbg.get("phases", "ABCXF")
            for l in range(2):
                self.rmsnorm(self.xT, "xT", self.g_mix[:, l, :], L, self.hT, "hT")
                if "A" in ph:
                    self.phase_A(l, st, t0, L)
                else:
                    self.memset("vector", self.mT[:], 0.0, w=[("mT", c) for c in range(KD)])
                if "B" in ph:
                    self.phase_B(l, st, L)
                if "C" in ph:
                    self.phase_C(l, st, L)
                if "X" in ph:
                    self.phase_X(l, st, L)
                if "F" in ph:
                    self.phase_F(l, st, L)
            self.rmsnorm(self.xT, "xT", self.g_fin, L, self.mT, "mT")
            self.store_tm(st["y"][t0:t0 + L, :], self.mT, "mT", KD, L)
        if self.dbg.get("states", True):
            self.store_states(st)

    def build(self):
        self.prepass()
        self.alloc()
        self.setup()
        out, inp = self.out, self.inp
        for s in range(self.NP):
            st = {"kind": "p", "idx": s, "L": self.SEQ, "x": inp["x_prompt"][s], "y": out["y_prompt"][s],
                  "keep_lo": self.SEQ - self.KEEP,
                  "ak_out": [out["attn_k_prompt"][l, s] for l in range(2)], "av_out": [out["attn_v_prompt"][l, s] for l in range(2)],
                  "outs": {k: [out[k + "_prompt"][l, s] for l in range(2)] for k in ("b_conv", "b_rec", "c_conv", "c_ssm", "ffn_conv")}}
            self.run_stream(st)
        if self.sample:
            st = {"kind": "s", "idx": 0, "L": 16, "x": inp["x_sample"], "y": out["y_sample"], "keep_lo": 0,
                  "ak_out": [out["attn_k_sample"][l] for l in range(2)], "av_out": [out["attn_v_sample"][l] for l in range(2)],
                  "outs": {k: [out[k + "_sample"][l] for l in range(2)] for k in ("b_conv", "b_rec", "c_conv", "c_ssm", "ffn_conv")}}
            self.run_stream(st)
        self.S.wait_all("sync")
        return self.S.emit()


_CACHE = {}

N_CORES = 8
NP_CORE = 4
SEQ_FULL = 2048


def _get_builder():
    if "b" not in _CACHE:
        b = Builder(NP_CORE, SEQ_FULL, T=256, sample=True)
        b.build()
        _CACHE["b"] = b
    return _CACHE["b"]


def kernel(**inputs):
    b = _get_builder()
    f32 = lambda a: np.ascontiguousarray(np.asarray(a, dtype=np.float32))
    consts = {"c_" + k: v for k, v in make_consts().items()}
    shared = {}
    for k in list(WEIGHT_SHAPES.keys()) + list(SMALL_SHAPES.keys()):
        shared[k] = f32(inputs[k])
    in_maps = []
    for i in range(N_CORES):
        m = dict(shared)
        m.update(consts)
        m["x_prompt"] = f32(inputs["x_prompt"][i * NP_CORE:(i + 1) * NP_CORE])
        m["mem_prompt"] = f32(inputs["mem_prompt"][i * NP_CORE:(i + 1) * NP_CORE])
        m["x_sample"] = f32(inputs["x_sample"][i])
        m["cache_attn_k"] = f32(np.asarray(inputs["cache_attn_k"])[:, i]).reshape(2, 512, 512)
        m["cache_attn_v"] = f32(np.asarray(inputs["cache_attn_v"])[:, i]).reshape(2, 512, 512)
        m["state_b_conv"] = f32(np.asarray(inputs["state_b_conv"])[:, i])
        m["state_b_rec"] = f32(np.asarray(inputs["state_b_rec"])[:, i])
        m["state_c_conv"] = f32(np.asarray(inputs["state_c_conv"])[:, i])
        m["state_c_ssm"] = f32(np.asarray(inputs["state_c_ssm"])[:, i])
        m["state_ffn_conv"] = f32(np.asarray(inputs["state_ffn_conv"])[:, i])
        m["cache_mem_k"] = f32(np.asarray(inputs["cache_mem_k"])[:, i]).reshape(2, N_MEM, D)
        m["cache_mem_v"] = f32(np.asarray(inputs["cache_mem_v"])[:, i]).reshape(2, N_MEM, D)
        in_maps.append({k: v for k, v in m.items() if k in b.inp})
    res = run_bass_kernel_spmd(b.nc, in_maps, core_ids=list(range(N_CORES))).results
    B = N_CORES * NP_CORE
    cat0 = lambda nm: np.concatenate([r[nm] for r in res], axis=0)
    cat1 = lambda nm: np.concatenate([r[nm] for r in res], axis=1)
    stk0 = lambda nm: np.stack([r[nm] for r in res], axis=0)
    stk1 = lambda nm: np.stack([r[nm] for r in res], axis=1)
    outs = (
        cat0("y_prompt"),
        stk0("y_sample"),
        cat1("attn_k_prompt").reshape(2, B, 512, A_H, A_DH),
        cat1("attn_v_prompt").reshape(2, B, 512, A_H, A_DH),
        cat1("b_conv_prompt"),
        cat1("b_rec_prompt"),
        cat1("c_conv_prompt"),
        cat1("c_ssm_prompt"),
        cat1("ffn_conv_prompt"),
        cat1("mem_k_prompt").reshape(2, B, N_MEM, X_H, X_DH),
        cat1("mem_v_prompt").reshape(2, B, N_MEM, X_H, X_DH),
        stk1("attn_k_sample").reshape(2, N_CORES, 16, A_H, A_DH),
        stk1("attn_v_sample").reshape(2, N_CORES, 16, A_H, A_DH),
        stk1("b_conv_sample"),
        stk1("b_rec_sample"),
        stk1("c_conv_sample"),
        stk1("c_ssm_sample"),
        stk1("ffn_conv_sample"),
    )
    return tuple(np.ascontiguousarray(o, dtype=np.float32) for o in outs)
```

```python
import contextlib
import numpy as np
import concourse.bass as bass
import concourse.mybir as mybir
from concourse.bass_utils import run_bass_kernel_spmd

F32 = mybir.dt.float32
BF16 = mybir.dt.bfloat16
ALU = mybir.AluOpType
AF = mybir.ActivationFunctionType

D = 1024
KD = 8
EPS = 1e-6
CHUNK = 64
N_MEM = 256
A_H, A_DH, A_W = 8, 64, 512
A_WIN = 512
B_H, B_DH, B_W = 4, 128, 512
C_H, C_P, C_IN, C_N, C_XBC = 16, 64, 1024, 128, 1536
X_H, X_DH = 4, 256
D_FF = 2816
KFF = 22
N_IN = 9240
O_AQ, O_AK, O_AV = 0, 512, 1024
O_BQKV = 1536
O_BBETA = 3072
O_BDEC = 3076
O_BGATE = 3080
O_CZ = 3592
O_CXBC = 4616
O_CDT = 6152
O_GATES = 6168

ENGINES = ("tensor", "vector", "scalar", "gpsimd", "sync")
SEM_LIMIT = 30000
N_DMA_SEMS = 24


class Sched:
    def __init__(self, nc):
        self.nc = nc
        self.stack = contextlib.ExitStack()
        self.ops = {e: [] for e in ENGINES}
        self.eng_sem = {}
        self.eng_cnt = {e: 0 for e in ENGINES}
        self.eng_epoch = {e: 0 for e in ENGINES}
        for e in ENGINES:
            self.eng_sem[e] = self._new_sem(f"s_{e}_0")
        self.dma_sems = [self._new_sem(f"s_dma_{i}") for i in range(N_DMA_SEMS)]
        self.dma_cnt = [0] * N_DMA_SEMS
        self.dma_rr = 0
        self.last_w = {}
        self.readers = {}
        self.waited = {e: {} for e in ENGINES}
        self.dry = False

    def _new_sem(self, name):
        return self.stack.enter_context(self.nc.semaphore(name))

    def _deps_for(self, eng, reads, writes):
        deps = []
        for k in reads:
            w = self.last_w.get(k)
            if w is not None:
                deps.append(w)
            if isinstance(k, tuple) and k[0] == "ps":
                rd = self.readers.get(k)
                if rd:
                    deps.extend(t for rk, t in rd.items() if rk != eng)
        for k in writes:
            w = self.last_w.get(k)
            if w is not None:
                deps.append(w)
            rd = self.readers.get(k)
            if rd:
                deps.extend(rd.values())
        best = {}
        for (s, v) in deps:
            i = id(s)
            if i not in best or best[i][1] < v:
                best[i] = (s, v)
        out = []
        wd = self.waited[eng]
        for i, (s, v) in best.items():
            if wd.get(i, 0) >= v:
                continue
            wd[i] = v
            out.append((s, v))
        return out

    def _commit(self, tok, rkey, reads, writes):
        for k in writes:
            self.last_w[k] = tok
            self.readers[k] = {}
        for k in reads:
            self.readers.setdefault(k, {})[rkey] = tok

    def op(self, eng, fn, reads=(), writes=()):
        if self.dry:
            return
        waits = self._deps_for(eng, reads, writes)
        if self.eng_cnt[eng] >= SEM_LIMIT:
            self.eng_epoch[eng] += 1
            self.eng_sem[eng] = self._new_sem(f"s_{eng}_{self.eng_epoch[eng]}")
            self.eng_cnt[eng] = 0
        self.eng_cnt[eng] += 1
        s = self.eng_sem[eng]
        v = self.eng_cnt[eng]
        if eng == "tensor":
            self.waited[eng][id(s)] = v
        self.ops[eng].append((fn, waits, (s, 1)))
        self._commit((s, v), eng, reads, writes)

    def dma(self, eng, fn, reads=(), writes=()):
        if self.dry:
            return
        i = self.dma_rr
        self.dma_rr = (self.dma_rr + 1) % N_DMA_SEMS
        s = self.dma_sems[i]
        waits = self._deps_for(eng, reads, writes)
        prev = self.dma_cnt[i]
        if prev > 0 and self.waited[eng].get(id(s), 0) < prev:
            waits.append((s, prev))
            self.waited[eng][id(s)] = prev
        self.dma_cnt[i] += 16
        v = self.dma_cnt[i]
        self.ops[eng].append((fn, waits, (s, 16)))
        self._commit((s, v), ("dma", i, v), reads, writes)

    def wait_all(self, eng):
        waits = []
        for e in ENGINES:
            if self.eng_cnt[e] > 0:
                waits.append((self.eng_sem[e], self.eng_cnt[e]))
        for i, s in enumerate(self.dma_sems):
            if self.dma_cnt[i] > 0:
                waits.append((s, self.dma_cnt[i]))
        self.ops[eng].append((None, waits, None))

    def emit(self):
        nc = self.nc
        ops = self.ops

        def run(e):
            def body(engobj):
                for fn, waits, inc in ops[e]:
                    for (s, v) in waits:
                        engobj.wait_ge(s, v)
                    if fn is not None:
                        fn(engobj).then_inc(inc[0], inc[1])
            return body

        with nc.Block() as block:
            block.tensor(run("tensor"))
            block.vector(run("vector"))
            block.scalar(run("scalar"))
            block.gpsimd(run("gpsimd"))
            block.sync(run("sync"))
        self.stack.close()
        return {e: len(ops[e]) for e in ENGINES}


WEIGHT_SHAPES = {
    "w_in": (D, N_IN), "w_br_a": (A_W, D), "w_br_b": (B_W, D), "w_br_c": (C_IN, D), "w_out": (D, D),
    "wx_q": (D, D), "wx_k": (D, D), "wx_v": (D, D), "wx_o": (D, D), "w_up": (D, 2 * D_FF), "w_down": (D_FF, D),
}
SMALL_SHAPES = {
    "norm_mix": (2, D), "a_rel_bias": (2, 257, 8), "b_conv_w": (2, 4, 1536), "b_a_log": (2, 4), "b_dt_bias": (2, 4),
    "b_norm": (2, 128), "c_conv_w": (2, 4, 1536), "c_conv_b": (2, 1536), "c_dt_bias": (2, 16), "c_a_log": (2, 16),
    "c_d": (2, 16), "c_norm": (2, D), "norm_x": (2, D), "norm_mem": (2, D), "norm_ffn": (2, D),
    "f_conv_w": (2, 3, D_FF), "f_conv_b": (2, D_FF), "norm_final": (D,),
}


NL_MAX = 7


def make_consts():
    c = {}
    eye = np.eye(128, dtype=np.float32)
    p = np.arange(128)[:, None]
    f = np.arange(128)[None, :]
    c["ident"] = eye
    c["anti"] = eye[::-1].copy()
    c["triI"] = (f >= p).astype(np.float32)
    c["triSL"] = (f < p).astype(np.float32)
    c["negU"] = np.where(f >= p, 0.0, -30000.0).astype(np.float32)
    mm = np.zeros((128, NL_MAX, 2, 128), np.float32)
    for l in range(NL_MAX):
        b = 1 << l
        same = (p // (2 * b)) == (f // (2 * b))
        M = same & ((p % (2 * b)) >= b) & ((f % (2 * b)) < b)
        mm[:, l, 0, :] = M
        mm[:, l, 1, :] = M.T
    c["mm"] = mm.reshape(128, NL_MAX * 2 * 128)
    c["ii"] = np.concatenate([eye, eye], axis=1)
    return c


class Builder:
    def __init__(self, NP, SEQ, T=256, sample=True, dbg=None):
        self.NP, self.SEQ, self.T, self.sample = NP, SEQ, T, sample
        self.dbg = dbg or {}
        self.KEEP = min(A_WIN, SEQ)
        self.NSLOT = 4 + T // 128
        nc = self.nc = bass.Bass("TRN2", target_bir_lowering=False)
        self.S = Sched(nc)
        self.inp = {}
        self.out = {}
        self._ptmp = 0
        self._wrr = 0
        self._rr = {}
        self._uid = 0
        self._nb = 8
        self._declare_io()

    def din(self, name, shape):
        self.inp[name] = self.nc.dram_tensor(name, list(shape), F32, kind="ExternalInput").ap()
        return self.inp[name]

    def dout(self, name, shape):
        self.out[name] = self.nc.dram_tensor(name, list(shape), F32, kind="ExternalOutput").ap()
        return self.out[name]

    def _declare_io(self):
        NP, SEQ, KEEP = self.NP, self.SEQ, self.KEEP
        self.din("x_prompt", (NP, SEQ, D))
        self.din("mem_prompt", (NP, N_MEM, D))
        if self.sample:
            self.din("x_sample", (16, D))
            self.din("cache_attn_k", (2, 512, 512))
            self.din("cache_attn_v", (2, 512, 512))
            self.din("state_b_conv", (2, 3, 1536))
            self.din("state_b_rec", (2, 4, 128, 128))
            self.din("state_c_conv", (2, 3, 1536))
            self.din("state_c_ssm", (2, 16, 64, 128))
            self.din("state_ffn_conv", (2, 2, D_FF))
            self.din("cache_mem_k", (2, N_MEM, D))
            self.din("cache_mem_v", (2, N_MEM, D))
        for k, shp in WEIGHT_SHAPES.items():
            self.din(k, (2,) + shp)
        for k, shp in SMALL_SHAPES.items():
            self.din(k, shp)
        for k, v in make_consts().items():
            self.din("c_" + k, v.shape)
        self.dout("y_prompt", (NP, SEQ, D))
        self.dout("attn_k_prompt", (2, NP, KEEP, 512))
        self.dout("attn_v_prompt", (2, NP, KEEP, 512))
        self.dout("b_conv_prompt", (2, NP, 3, 1536))
        self.dout("b_rec_prompt", (2, NP, 4, 128, 128))
        self.dout("c_conv_prompt", (2, NP, 3, 1536))
        self.dout("c_ssm_prompt", (2, NP, 16, 64, 128))
        self.dout("ffn_conv_prompt", (2, NP, 2, D_FF))
        self.dout("mem_k_prompt", (2, NP, N_MEM, D))
        self.dout("mem_v_prompt", (2, NP, N_MEM, D))
        if self.sample:
            self.dout("y_sample", (16, D))
            self.dout("attn_k_sample", (2, 16, 512))
            self.dout("attn_v_sample", (2, 16, 512))
            self.dout("b_conv_sample", (2, 3, 1536))
            self.dout("b_rec_sample", (2, 4, 128, 128))
            self.dout("c_conv_sample", (2, 3, 1536))
            self.dout("c_ssm_sample", (2, 16, 64, 128))
            self.dout("ffn_conv_sample", (2, 2, D_FF))
        self.wscr = {}
        for k, shp in WEIGHT_SHAPES.items():
            self.wscr[k] = self.nc.dram_tensor("scr_" + k, [2, shp[0], shp[1]], BF16).ap()
        self.ext = self.nc.dram_tensor("scr_ext", [2, 8, 384], F32).ap()

    def sb(self, name, shape, dt=F32):
        return self.nc.alloc_sbuf_tensor(name, list(shape), dt)

    def mm(self, out, lhsT, rhs, start=True, stop=True, r=(), w=()):
        self.S.op("tensor", lambda e: e.matmul(out, lhsT=lhsT, rhs=rhs, start=start, stop=stop), reads=r, writes=w)

    def tr(self, out, in_, r=(), w=()):
        k = in_.shape[0]
        ident = self.ident[0:k, 0:k]
        self.S.op("tensor", lambda e: e.transpose(out, in_, ident), reads=list(r) + ["consts"], writes=w)

    def act(self, out, in_, func, r=(), w=(), scale=1.0, bias=0.0):
        self.S.op("scalar", lambda e: e.activation(out=out, in_=in_, func=func, bias=bias, scale=scale), reads=r, writes=w)

    def tt(self, eng, out, in0, in1, op, r=(), w=()):
        self.S.op(eng, lambda e: e.tensor_tensor(out=out, in0=in0, in1=in1, op=op), reads=r, writes=w)

    def ts(self, eng, out, in0, s1, s2, op0, op1=None, r=(), w=()):
        if op1 is None:
            self.S.op(eng, lambda e: e.tensor_scalar(out=out, in0=in0, scalar1=s1, scalar2=None, op0=op0), reads=r, writes=w)
        else:
            self.S.op(eng, lambda e: e.tensor_scalar(out=out, in0=in0, scalar1=s1, scalar2=s2, op0=op0, op1=op1), reads=r, writes=w)

    def stt(self, out, in0, scalar, in1, op0, op1, r=(), w=()):
        self.S.op("vector", lambda e: e.scalar_tensor_tensor(out=out, in0=in0, scalar=scalar, in1=in1, op0=op0, op1=op1), reads=r, writes=w)

    def cp(self, eng, out, in_, r=(), w=()):
        if eng == "scalar":
            self.S.op(eng, lambda e: e.activation(out=out, in_=in_, func=AF.Copy), reads=r, writes=w)
        else:
            self.S.op(eng, lambda e: e.tensor_copy(out=out, in_=in_), reads=r, writes=w)

    def memset(self, eng, ap, val, w=()):
        self.S.op(eng, lambda e: e.memset(ap, val), writes=w)

    def dma(self, q, out, in_, r=(), w=(), slow=False):
        if slow:
            self.S.dma(q, lambda e: e.dma_start(out=out, in_=in_, allow_slow_non_contiguous=True), reads=r, writes=w)
        else:
            self.S.dma(q, lambda e: e.dma_start(out=out, in_=in_), reads=r, writes=w)

    def ptmp(self):
        b = self._ptmp % self._nb
        self._ptmp = (b + 1) % self._nb
        return b

    def rot(self, name, n):
        i = self._rr.get(name, 0)
        self._rr[name] = (i + 1) % n
        return i

    @contextlib.contextmanager
    def scope(self):
        st = contextlib.ExitStack()

        def alloc(name, shape, dt=F32):
            self._uid += 1
            return st.enter_context(self.nc.sbuf_tensor(f"{name}_{self._uid}", list(shape), dt))
        try:
            yield alloc
        finally:
            self.barrier()
            st.close()

    def barrier(self, full=False):
        S = self.S
        if S.dry:
            return
        for e in ENGINES:
            if e == "sync" and not full:
                continue
            waits = []
            for e2 in ENGINES:
                if e2 == "sync" and not full:
                    continue
                if S.eng_cnt[e2] > 0:
                    s, v = S.eng_sem[e2], S.eng_cnt[e2]
                    if S.waited[e].get(id(s), 0) < v:
                        S.waited[e][id(s)] = v
                        waits.append((s, v))
            for i, s in enumerate(S.dma_sems):
                v = S.dma_cnt[i]
                if v > 0 and S.waited[e].get(id(s), 0) < v:
                    S.waited[e][id(s)] = v
                    waits.append((s, v))
            S.ops[e].append((None, waits, None))

    def weights_used(self):
        return self.dbg.get("weights", list(WEIGHT_SHAPES.keys()))

    def prepass(self):
        inp = self.inp
        CW = 2048
        NB = 4
        with contextlib.ExitStack() as st:
            cvf = [st.enter_context(self.nc.sbuf_tensor(f"cvf{i}", [128, CW], F32)) for i in range(NB)]
            cvb = [st.enter_context(self.nc.sbuf_tensor(f"cvb{i}", [128, CW], BF16)) for i in range(NB)]
            engs = ("gpsimd", "scalar", "vector", "gpsimd")
            jobs = []
            for name in self.weights_used():
                rows, cols = WEIGHT_SHAPES[name]
                for l in range(2):
                    for rc in range(rows // 128):
                        for c0 in range(0, cols, CW):
                            jobs.append((name, l, rc, c0, min(CW, cols - c0)))
            LAG = NB - 1
            for i in range(len(jobs) + LAG):
                if i < len(jobs):
                    name, l, rc, c0, n = jobs[i]
                    sl = i % NB
                    self.dma("sync", cvf[sl][:, 0:n], inp[name][l, rc * 128:(rc + 1) * 128, c0:c0 + n], w=[("cvf", sl)])
                    self.cp(engs[sl], cvb[sl][:, 0:n], cvf[sl][:, 0:n], r=[("cvf", sl)], w=[("cvb", sl)])
                if i - LAG >= 0:
                    name, l, rc, c0, n = jobs[i - LAG]
                    sl = (i - LAG) % NB
                    self.dma("sync", self.wscr[name][l, rc * 128:(rc + 1) * 128, c0:c0 + n], cvb[sl][:, 0:n], r=[("cvb", sl)])
            self.barrier(full=True)
        self.barrier(full=True)

    def wload(self, name, l, segs):
        rows = WEIGHT_SHAPES[name][0]
        KC = rows // 128
        ncols = sum(n for _, n in segs)
        assert KC * ncols <= self.WBE, (name, KC, ncols)
        b = self._wrr
        self._wrr = (b + 1) % self.NWB
        view = self.wbuf[b][:, 0:KC * ncols].rearrange("p (c n) -> p c n", c=KC)
        o = 0
        src = self.wscr[name][l].rearrange("(c p) n -> p c n", p=128)
        for (c0, n) in segs:
            self.dma("sync", view[:, :, o:o + n], src[:, :, c0:c0 + n], w=[("wbuf", b)], slow=(n * 2 < 512))
            o += n
        return view, ("wbuf", b)
    def alloc(self):
        T = self.T
        sb = self.sb
        self.ps = self.nc.alloc_psum_tensor("ps", [128, 8, 512], F32)
        self.ident = sb("ident", [128, 128])
        self.anti = sb("anti", [128, 128])
        self.triI = sb("triI", [128, 128])
        self.triSL = sb("triSL", [128, 128])
        self.negU = sb("negU", [128, 128])
        self.zeros = sb("zeros", [128, 128])
        self.identb = sb("identb", [128, 128], BF16)
        self.negUb = sb("negUb", [128, 128], BF16)
        self.mmask = sb("mmask", [128, NL_MAX, 2, 128])
        self.ii = sb("ii", [128, 2, 128])
        self.ones_f = sb("ones_f", [128, 128])
        self.ones_b = sb("ones_b", [128, 128], BF16)
        self.eps_col = sb("eps_col", [128, 1])
        self.g_mix = sb("g_mix", [128, 2, KD])
        self.g_x = sb("g_x", [128, 2, KD])
        self.g_mem = sb("g_mem", [128, 2, KD])
        self.g_ffn = sb("g_ffn", [128, 2, KD])
        self.g_cn = sb("g_cn", [128, 2, KD])
        self.g_fin = sb("g_fin", [128, KD])
        self.g_bn = sb("g_bn", [128, 2])
        self.fcw = sb("fcw", [128, 2, 3, KFF])
        self.fcb = sb("fcb", [128, 2, KFF])
        self.bcw = sb("bcw", [128, 2, 4, 12])
        self.ccw = sb("ccw", [128, 2, 4, 12])
        self.ccb = sb("ccb", [128, 2, 12])
        self.b_dtb = sb("b_dtb", [128, 2, 4])
        self.b_nA = sb("b_nA", [128, 2, 4])
        self.c_dtb = sb("c_dtb", [128, 2, 16])
        self.c_nA = sb("c_nA", [128, 2, 16])
        self.c_dsk = sb("c_dsk", [128, 2, 16])
        self.a_bc = sb("a_bc", [128, 2, 8])
        self.Etab = sb("Etab", [128, 2, 8, 2, 128], BF16)
        NS = self.NSLOT
        self.KT = [sb(f"KT{l}", [64, 8, NS * 128], BF16) for l in range(2)]
        self.VH = [sb(f"VH{l}", [128, NS, 8, 65], BF16) for l in range(2)]
        self.Sst = [sb(f"Sst{l}", [128, 4, 128]) for l in range(2)]
        self.Hst = [sb(f"Hst{l}", [128, 16, 64]) for l in range(2)]
        self.bhist = sb("bhist", [128, 2, 12, 3])
        self.chist = sb("chist", [128, 2, 12, 3])
        self.fhist = sb("fhist", [128, 2, KFF, 2])
        self.MKT = [sb(f"MKT{l}", [128, 8, N_MEM], BF16) for l in range(2)]
        self.MV = [sb(f"MV{l}", [128, 2, D], BF16) for l in range(2)]
        self.xT = sb("xT", [128, KD, T])
        self.hT = sb("hT", [128, KD, T], BF16)
        self.mT = sb("mT", [128, KD, T])
        self.sq = sb("sq", [128, KD, T])
        self.rstd = sb("rstd", [128, T])
        self.sqb = sb("sqb", [128, KD, T], BF16)
        self.NWB = 3
        self.WBE = 4096
        self.wbuf = [sb(f"wbuf{i}", [128, self.WBE], BF16) for i in range(self.NWB)]
        self.gsig = [sb(f"gsig{i}", [128, T]) for i in range(2)]
        self.brT = sb("brT", [128, KD, T], BF16)
        self.io = [sb(f"io{i}", [128, D]) for i in range(2)]
        self.HG = 2
        self.bd = sb("bd", [128, 2, 8])
        self.bet = sb("bet", [128, 2, 4])
        self.nbet = sb("nbet", [128, 2, 4])
        self.gg = sb("gg", [128, 2, 4])
        self.dtr = sb("dtr", [128, 2, 16])
        self.dtt = sb("dtt", [128, 2, 16])
        self.aa = sb("aa", [128, 2, 16])

    def setup(self):
        inp = self.inp
        q = "sync"
        P = ["params"]
        self.dma(q, self.ident[:], inp["c_ident"], w=["consts"])
        self.dma(q, self.anti[:], inp["c_anti"], w=["consts"])
        self.dma(q, self.triI[:], inp["c_triI"], w=["consts"])
        self.dma(q, self.triSL[:], inp["c_triSL"], w=["consts"])
        self.dma(q, self.negU[:], inp["c_negU"], w=["consts"])
        self.memset("vector", self.zeros[:], 0.0, w=["ones"])
        self.cp("vector", self.identb[:], self.ident[:], r=["consts"], w=["constsb"])
        self.cp("vector", self.negUb[:], self.negU[:], r=["consts"], w=["constsb"])
        self.dma(q, self.mmask[:].rearrange("p a b c -> p (a b c)"), inp["c_mm"], w=["consts"])
        self.dma(q, self.ii[:].rearrange("p a b -> p (a b)"), inp["c_ii"], w=["consts"])
        self.memset("vector", self.ones_f[:], 1.0, w=["ones"])
        self.memset("vector", self.ones_b[:], 1.0, w=["ones"])
        self.memset("vector", self.eps_col[:], EPS, w=["ones"])
        for nm, t in (("norm_mix", self.g_mix), ("norm_x", self.g_x), ("norm_mem", self.g_mem),
                      ("norm_ffn", self.g_ffn), ("c_norm", self.g_cn)):
            self.dma(q, t[:], inp[nm].rearrange("l (c p) -> p l c", p=128), w=P, slow=True)
        self.dma(q, self.g_fin[:], inp["norm_final"].rearrange("(c p) -> p c", p=128), w=P, slow=True)
        self.dma(q, self.g_bn[:], inp["b_norm"].rearrange("l p -> p l"), w=P, slow=True)
        for l in range(2):
            for k in range(3):
                self.dma(q, self.fcw[:, l, k, :], inp["f_conv_w"][l, k].rearrange("(c p) -> p c", p=128), w=P, slow=True)
            for k in range(4):
                self.dma(q, self.bcw[:, l, k, :], inp["b_conv_w"][l, k].rearrange("(c p) -> p c", p=128), w=P, slow=True)
                self.dma(q, self.ccw[:, l, k, :], inp["c_conv_w"][l, k].rearrange("(c p) -> p c", p=128), w=P, slow=True)
        self.dma(q, self.fcb[:], inp["f_conv_b"].rearrange("l (c p) -> p l c", p=128), w=P, slow=True)
        self.dma(q, self.ccb[:], inp["c_conv_b"].rearrange("l (c p) -> p l c", p=128), w=P, slow=True)

        def bc(dst, src2d):
            (s0, n0), (s1, n1) = src2d.ap
            src = bass.AP(src2d.tensor, src2d.offset, [[0, 128], [s0, n0], [s1, n1]])
            self.dma(q, dst[:], src, w=P, slow=True)
        if self.dbg.get("nobc"):
            return
        bc(self.b_dtb, inp["b_dt_bias"])
        bc(self.b_nA, inp["b_a_log"])
        bc(self.c_dtb, inp["c_dt_bias"])
        bc(self.c_nA, inp["c_a_log"])
        bc(self.c_dsk, inp["c_d"])
        bc(self.a_bc, inp["a_rel_bias"][:, 256, :])
        for t in (self.b_nA, self.c_nA):
            self.act(t[:], t[:], AF.Exp, r=P, w=P)
            self.ts("vector", t[:], t[:], -1.0, None, ALU.mult, r=P, w=P)
        if self.dbg.get("noE"):
            return
        for l in range(2):
            i = self.rot("io", 2)
            xi, xk = self.io[i], ("io", i)
            self.dma(q, xi[:, 0:8], inp["a_rel_bias"][l, 1:129, :], w=[xk])
            self.dma(q, xi[:, 8:16], inp["a_rel_bias"][l, 129:257, :], w=[xk])
            b = self.ptmp()
            self.tr(self.ps[0:8, b, 0:128], xi[:, 0:8], r=[xk], w=[("ps", b)])
            self.tr(self.ps[0:8, b, 128:256], xi[:, 8:16], r=[xk], w=[("ps", b)])
            self.cp("vector", xi[0:8, 512:768], self.ps[0:8, b, 0:256], r=[("ps", b)], w=[xk])
            self.cp("vector", xi[0:8, 768:896], xi[0:8, 767:768].broadcast_to([8, 128]), r=[xk], w=[xk])
            self.dma(q, self.ext[l], xi[0:8, 512:896], r=[xk], w=[("ext", l)])
            hk = self.sq[:, :, :].rearrange("p c t -> p (c t)")[:, 0:2048].rearrange("p (h j q) -> p h j q", h=8, j=2)
            for jj in range(2):
                off = 128 if jj == 0 else 0
                e = self.ext[l]
                src = bass.AP(e.tensor, e.offset + off, [[1, 128], [384, 8], [1, 128]])
                self.dma(q, hk[:, :, jj, :], src, r=[("ext", l)], w=["sq"])
            hkf = self.sq[:, :, :].rearrange("p c t -> p (c t)")
            ef = self.Etab[:, l].rearrange("p h j q -> p (h j q)")
            for i in range(4):
                b = self.ptmp()
                self.mm(self.ps[:, b, :], self.anti[:], hkf[:, i * 512:(i + 1) * 512], r=["consts", "sq"], w=[("ps", b)])
                self.act(ef[:, i * 512:(i + 1) * 512], self.ps[:, b, :], AF.Exp, r=[("ps", b)], w=[("E", l)])
            self.memset("gpsimd", self.Etab[64:128, l, :, 1, 0:64], 0.0, w=[("E", l)])

    def rmsnorm(self, src, skey, gain, L, out, okey, nk=KD, dim=D):
        rk = [(skey, c) for c in range(nk)]
        hh = nk // 2 if nk >= 4 else nk
        self.act(self.sqb[:, 0:hh, 0:L], src[:, 0:hh, 0:L], AF.Square, r=rk[0:hh], w=["sqb0"])
        if hh < nk:
            self.act(self.sqb[:, hh:nk, 0:L], src[:, hh:nk, 0:L], AF.Square, r=rk[hh:nk], w=["sqb1"])
        b = self.ptmp()
        pst = self.ps[:, b, 0:L]
        for c in range(nk):
            self.mm(pst, self.ones_b[:], self.sqb[:, c, 0:L], start=(c == 0), stop=(c == nk - 1),
                    r=["ones", "sqb0" if c < hh else "sqb1"], w=[("ps", b)])
        self.act(self.rstd[:, 0:L], pst, AF.Ln, r=[("ps", b), "ones"], w=["rstd"], scale=1.0 / dim, bias=self.eps_col[:, 0:1])
        self.act(self.rstd[:, 0:L], self.rstd[:, 0:L], AF.Exp, r=["rstd"], w=["rstd"], scale=-0.5)
        for c in range(nk):
            self.stt(out[:, c, 0:L], src[:, c, 0:L], gain[:, c:c + 1], self.rstd[:, 0:L],
                     ALU.mult, ALU.mult, r=[(skey, c), "rstd", "params"], w=[(okey, c)])

    def hkeys(self):
        return [("hT", c) for c in range(KD)]

    def proj_fm(self, wv, wk, col0, M, L, src=None, skeys=None, KC=KD):
        src = self.hT if src is None else src
        skeys = self.hkeys() if skeys is None else skeys
        b = self.ptmp()
        for c in range(KC):
            self.mm(self.ps[0:M, b, 0:L], wv[:, c, col0:col0 + M], src[:, c, 0:L], start=(c == 0), stop=(c == KC - 1),
                    r=[wk, skeys[c]], w=[("ps", b)])
        return b

    def proj_tm(self, wv, wk, col0, N, t0, n, b=None):
        b = self.ptmp() if b is None else b
        for c in range(KD):
            self.mm(self.ps[0:n, b, 0:N], self.hT[:, c, t0:t0 + n], wv[:, c, col0:col0 + N], start=(c == 0), stop=(c == KD - 1),
                    r=[wk, ("hT", c)], w=[("ps", b)])
        return b

    def merge_branch(self, l, L, wname, KC, gate_off, first):
        bkeys = [("brT", c) for c in range(KC)]
        wb, wbk = self.wload(wname, l, [(0, D)]) if KC == 4 else (None, None)
        for n in range(KD):
            if n % 4 == 0:
                wg, wgk = self.wload("w_in", l, [(gate_off + n * 128, 512)])
                if KC == 8:
                    wb, wbk = self.wload(wname, l, [(n * 128, 512)])
            bg = self.proj_fm(wg, wgk, (n % 4) * 128, 128, L)
            i = self.rot("gsig", 2)
            gs = self.gsig[i]
            self.act(gs[:, 0:L], self.ps[:, bg, 0:L], AF.Sigmoid, r=[("ps", bg)], w=[("gsig", i)])
            coff = n * 128 if KC == 4 else (n % 4) * 128
            bp = self.proj_fm(wb, wbk, coff, 128, L, src=self.brT, skeys=bkeys, KC=KC)
            if first:
                self.tt("vector", self.mT[:, n, 0:L], self.ps[:, bp, 0:L], gs[:, 0:L], ALU.mult,
                        r=[("ps", bp), ("gsig", i)], w=[("mT", n)])
            else:
                self.tt("vector", gs[:, 0:L], self.ps[:, bp, 0:L], gs[:, 0:L], ALU.mult,
                        r=[("ps", bp), ("gsig", i)], w=[("gsig", i)])
                self.tt("vector", self.mT[:, n, 0:L], self.mT[:, n, 0:L], gs[:, 0:L], ALU.add,
                        r=[("mT", n), ("gsig", i)], w=[("mT", n)])

    def resid_proj(self, l, L, wname, src, skeys, KC, nblk):
        for n in range(KD):
            if KC == KFF:
                wv, wk = self.wload(wname, l, [(n * 128, 128)])
                coff = 0
            else:
                if n % 4 == 0:
                    wv, wk = self.wload(wname, l, [(n * 128, 512)])
                coff = (n % 4) * 128
            b = self.proj_fm(wv, wk, coff, 128, L, src=src, skeys=skeys, KC=KC)
            self.tt("vector", self.xT[:, n, 0:L], self.xT[:, n, 0:L], self.ps[:, b, 0:L], ALU.add,
                    r=[("xT", n), ("ps", b)], w=[("xT", n)])
    def load_x_tile(self, src, L):
        nsub = (L + 127) // 128
        for j in range(nsub):
            n = min(128, L - j * 128)
            i = self.rot("io", 2)
            xi, xk = self.io[i], ("io", i)
            self.dma("sync", xi[0:n, :], src[j * 128:j * 128 + n, :], w=[xk])
            for half in range(2):
                b = self.ptmp()
                for cc in range(4):
                    c = half * 4 + cc
                    self.tr(self.ps[:, b, cc * 128:cc * 128 + n], xi[0:n, c * 128:(c + 1) * 128], r=[xk], w=[("ps", b)])
                dst = self.xT[:, half * 4:half * 4 + 4, j * 128:j * 128 + n]
                srcp = self.ps[:, b, :].rearrange("p (c t) -> p c t", c=4)[:, :, 0:n]
                self.cp("vector" if half else "scalar", dst, srcp, r=[("ps", b)], w=[("xT", half * 4 + cc) for cc in range(4)])

    def store_tm(self, dst, src, skey, nch, L, t_lo=0):
        skeys = [(skey, c) for c in range(nch)]
        for j in range(t_lo // 128, (L + 127) // 128):
            n = min(128, L - j * 128)
            i = self.rot("io", 2)
            xi, xk = self.io[i], ("io", i)
            for g in range(nch // 4):
                b = self.ptmp()
                for cc in range(4):
                    c = g * 4 + cc
                    self.tr(self.ps[0:n, b, cc * 128:(cc + 1) * 128], src[:, c, j * 128:j * 128 + n], r=skeys, w=[("ps", b)])
                self.cp("vector" if g % 2 else "scalar", xi[0:n, g * 512:(g + 1) * 512], self.ps[0:n, b, :], r=[("ps", b)], w=[xk])
            self.dma("sync", dst[j * 128 - t_lo:j * 128 - t_lo + n, :], xi[0:n, 0:nch * 128], r=[xk])

    def init_stream(self, st):
        kind = st["kind"]
        inp = self.inp
        for l in range(2):
            self.memset("gpsimd", self.VH[l][:, :, :, 64:65], 1.0, w=[("VH", l)])
        if kind == "p":
            self.memset("gpsimd", self.bhist[:], 0.0, w=["bhist"])
            self.memset("gpsimd", self.chist[:], 0.0, w=["chist"])
            self.memset("gpsimd", self.fhist[:], 0.0, w=["fhist"])
            for l in range(2):
                self.memset("gpsimd", self.Sst[l][:], 0.0, w=[("S", l, h) for h in range(4)])
                self.memset("gpsimd", self.Hst[l][:], 0.0, w=[("H", l)])
            if not self.dbg.get("nomem"):
                self.memory_kv_prompt(st["idx"])
        else:
            for l in range(2):
                for c in range(12):
                    self.dma("sync", self.bhist[:, l, c, :], inp["state_b_conv"][l, :, c * 128:(c + 1) * 128].rearrange("k p -> p k"),
                             w=["bhist"], slow=True)
                    self.dma("sync", self.chist[:, l, c, :], inp["state_c_conv"][l, :, c * 128:(c + 1) * 128].rearrange("k p -> p k"),
                             w=["chist"], slow=True)
                for c in range(KFF):
                    self.dma("sync", self.fhist[:, l, c, :], inp["state_ffn_conv"][l, :, c * 128:(c + 1) * 128].rearrange("k p -> p k"),
                             w=["fhist"], slow=True)
                self.dma("sync", self.Sst[l][:], inp["state_b_rec"][l].rearrange("h k v -> k h v"), w=[("S", l, h) for h in range(4)])
                for g in range(2):
                    i = self.rot("io", 2)
                    xi, xk = self.io[i], ("io", i)
                    src = inp["state_c_ssm"][l].rearrange("h p n -> (h p) n").rearrange("(c q) n -> q c n", q=128)
                    self.dma("sync", xi[:, 0:512].rearrange("q (c n) -> q c n", c=4), src[:, g * 4:g * 4 + 4, :], w=[xk])
                    b = self.ptmp()
                    for cc in range(4):
                        self.tr(self.ps[:, b, cc * 128:(cc + 1) * 128], xi[:, cc * 128:(cc + 1) * 128], r=[xk], w=[("ps", b)])
                    self.cp("vector", self.Hst[l][:, g * 8:g * 8 + 8, :].rearrange("n h p -> n (h p)"), self.ps[:, b, :],
                            r=[("ps", b)], w=[("H", l)])
                for tt_ in range(4):
                    i = self.rot("io", 2)
                    xi, xk = self.io[i], ("io", i)
                    self.dma("sync", xi[:, 0:512], inp["cache_attn_k"][l, tt_ * 128:(tt_ + 1) * 128, :], w=[xk])
                    self.dma("sync", xi[:, 512:1024], inp["cache_attn_v"][l, tt_ * 128:(tt_ + 1) * 128, :], w=[xk])
                    for g in range(2):
                        b = self.ptmp()
                        for hh in range(4):
                            h = g * 4 + hh
                            self.tr(self.ps[0:64, b, hh * 128:(hh + 1) * 128], xi[:, h * 64:(h + 1) * 64], r=[xk], w=[("ps", b)])
                        self.cp("scalar", self.KT[l][:, g * 4:g * 4 + 4, tt_ * 128:(tt_ + 1) * 128],
                                self.ps[0:64, b, :].rearrange("p (h t) -> p h t", h=4), r=[("ps", b)], w=[("KT", l, tt_)])
                    self.cp("vector", self.VH[l][:, tt_, :, 0:64], xi[:, 512:1024].rearrange("p (h d) -> p h d", h=8),
                            r=[xk], w=[("VH", l)])
                for mt in range(2):
                    i = self.rot("io", 2)
                    xi, xk = self.io[i], ("io", i)
                    self.dma("sync", xi[:], inp["cache_mem_k"][l, mt * 128:(mt + 1) * 128, :], w=[xk])
                    for g in range(2):
                        b = self.ptmp()
                        for cc in range(4):
                            c = g * 4 + cc
                            self.tr(self.ps[:, b, cc * 128:(cc + 1) * 128], xi[:, c * 128:(c + 1) * 128], r=[xk], w=[("ps", b)])
                        self.cp("scalar", self.MKT[l][:, g * 4:g * 4 + 4, mt * 128:(mt + 1) * 128],
                                self.ps[:, b, :].rearrange("p (c t) -> p c t", c=4), r=[("ps", b)], w=[("MKT", l)])
                    i = self.rot("io", 2)
                    xi, xk = self.io[i], ("io", i)
                    self.dma("sync", xi[:], inp["cache_mem_v"][l, mt * 128:(mt + 1) * 128, :], w=[xk])
                    self.cp("vector", self.MV[l][:, mt, :], xi[:], r=[xk], w=[("MV", l)])

    def memory_kv_prompt(self, s):
        with self.scope() as sc:
            self.qxT = sc("qxT", [128, KD, self.T], BF16)
            self._memory_kv_prompt(s)

    def _memory_kv_prompt(self, s):
        src = self.inp["mem_prompt"][s]
        L = N_MEM
        for j in range(2):
            i = self.rot("io", 2)
            xi, xk = self.io[i], ("io", i)
            self.dma("sync", xi[:], src[j * 128:(j + 1) * 128, :], w=[xk])
            for half in range(2):
                b = self.ptmp()
                for cc in range(4):
                    c = half * 4 + cc
                    self.tr(self.ps[:, b, cc * 128:(cc + 1) * 128], xi[:, c * 128:(c + 1) * 128], r=[xk], w=[("ps", b)])
                self.cp("vector" if half else "scalar", self.mT[:, half * 4:half * 4 + 4, j * 128:(j + 1) * 128],
                        self.ps[:, b, :].rearrange("p (c t) -> p c t", c=4), r=[("ps", b)], w=[("mT", half * 4 + cc) for cc in range(4)])
        stage = self.dbg.get("memstage", 9)
        for l in range(2):
            if stage < 1:
                continue
            self.rmsnorm(self.mT, "mT", self.g_mem[:, l, :], L, self.qxT, "qxT")
            qk = [("qxT", c) for c in range(KD)]
            if stage < 2:
                continue
            for n in range(KD):
                if n % 4 == 0:
                    wv, wk = self.wload("wx_k", l, [(n * 128, 512)])
                b = self.proj_fm(wv, wk, (n % 4) * 128, 128, L, src=self.qxT, skeys=qk)
                self.cp("scalar" if n % 2 else "vector", self.MKT[l][:, n, :], self.ps[:, b, 0:L], r=[("ps", b)], w=[("MKT", l)])
            if stage < 3:
                continue
            for which, wname, oname in ((0, "wx_k", "mem_k_prompt"), (1, "wx_v", "mem_v_prompt")):
                if which == 1 and stage < 4:
                    continue
                for half in range(2):
                    wv, wk = self.wload(wname, l, [(half * 512, 512)])
                    for mt in range(2):
                        b = self.ptmp()
                        for c in range(KD):
                            self.mm(self.ps[:, b, :], self.qxT[:, c, mt * 128:(mt + 1) * 128], wv[:, c, :], start=(c == 0), stop=(c == KD - 1),
                                    r=[wk] + qk, w=[("ps", b)])
                        i = self.rot("io", 2)
                        ko, kk = self.io[i][:, 0:512], ("io", i)
                        self.cp("scalar", ko[:], self.ps[:, b, :], r=[("ps", b)], w=[kk])
                        if which == 1:
                            self.cp("vector", self.MV[l][:, mt, half * 512:(half + 1) * 512], ko[:], r=[kk], w=[("MV", l)])
                        self.dma("sync", self.out[oname][l, s, mt * 128:(mt + 1) * 128, half * 512:(half + 1) * 512], ko[:], r=[kk])

    def phase_A(self, l, st, t0, L):
        with self.scope() as sc:
            self.QT = sc("QT", [64, 8, self.T], BF16)
            self.PT = sc("PT", [128, 8, 5, 128], BF16)
            self.yA = sc("yA", [128, 512])
            self.rec8 = sc("rec8", [128, 8])
            self._phase_A(l, st, t0, L)

    def _phase_A(self, l, st, t0, L):
        NS = self.NSLOT
        kind = st["kind"]
        vpos0 = t0 if kind == "p" else 512
        nsub = (L + 127) // 128
        hk = self.hkeys()
        wq, wqk = self.wload("w_in", l, [(O_AQ, 512)])
        for h in range(8):
            b = self.proj_fm(wq, wqk, h * 64, 64, L)
            self.act(self.QT[:, h, 0:L], self.ps[0:64, b, 0:L], AF.Copy, r=[("ps", b)], w=[("QT", h)], scale=0.125)
        wkk, wkkk = self.wload("w_in", l, [(O_AK, 512)])
        slot0 = (vpos0 // 128) % NS
        kkeys = [("KT", l, (slot0 + j) % NS) for j in range(nsub)]
        for h in range(8):
            b = self.proj_fm(wkk, wkkk, h * 64, 64, L)
            self.cp("vector" if h % 2 else "scalar", self.KT[l][:, h, slot0 * 128:slot0 * 128 + L], self.ps[0:64, b, 0:L],
                    r=[("ps", b)], w=kkeys)
        keep_lo = st["keep_lo"]
        for j in range(nsub):
            n = min(128, L - j * 128)
            if t0 + j * 128 + n <= keep_lo:
                continue
            b = self.proj_tm(wkk, wkkk, 0, 512, j * 128, n)
            i = self.rot("io", 2)
            ko, kk = self.io[i][:, 0:512], ("io", i)
            self.cp("scalar", ko[0:n, :], self.ps[0:n, b, :], r=[("ps", b)], w=[kk])
            r0 = t0 + j * 128 - keep_lo
            self.dma("sync", st["ak_out"][l][r0:r0 + n, :], ko[0:n, :], r=[kk])
        wvv, wvk = self.wload("w_in", l, [(O_AV, 512)])
        for j in range(nsub):
            n = min(128, L - j * 128)
            slot = (slot0 + j) % NS
            b = self.proj_tm(wvv, wvk, 0, 512, j * 128, n)
            self.cp("vector", self.VH[l][0:n, slot, :, 0:64], self.ps[0:n, b, :].rearrange("p (h d) -> p h d", h=8),
                    r=[("ps", b)], w=[("VH", l)])
            if t0 + j * 128 + n > keep_lo:
                i = self.rot("io", 2)
                ko, kk = self.io[i][:, 0:512], ("io", i)
                self.cp("scalar", ko[0:n, :], self.ps[0:n, b, :], r=[("ps", b)], w=[kk])
                r0 = t0 + j * 128 - keep_lo
                self.dma("sync", st["av_out"][l][r0:r0 + n, :], ko[0:n, :], r=[kk])
        self._nb = 4
        for j in range(nsub):
            n = min(128, L - j * 128)
            q0 = j * 128
            vq = vpos0 + q0
            jmin = max(0, 4 - vq // 128)
            tiles = []
            for jt in range(jmin, 5):
                slot = ((vq - 512 + 128 * jt) // 128) % NS
                tiles.append((jt, slot, n if jt == 4 else 128))
            for hp in range(4):
                bn = self.ptmp()
                bf_ = [None, None]
                for hh in range(2):
                    h = hp * 2 + hh
                    far = [t for t in tiles if t[0] < 3]
                    if far:
                        bf_[hh] = self.ptmp()
                    for (jt, slot, nk) in tiles:
                        if jt < 3:
                            out = self.ps[0:nk, bf_[hh], jt * 128:jt * 128 + n]
                            wkey = ("ps", bf_[hh])
                        else:
                            col = (hh * 2 + (jt - 3)) * 128
                            out = self.ps[0:nk, bn, col:col + n]
                            wkey = ("ps", bn)
                        self.mm(out, self.KT[l][:, h, slot * 128:slot * 128 + nk], self.QT[:, h, q0:q0 + n],
                                r=[("KT", l, slot), ("QT", h)], w=[wkey])
                    if far:
                        j0 = far[0][0]
                        self.act(self.PT[:, h, j0:3, 0:n], self.ps[:, bf_[hh], :].rearrange("p (j q) -> p j q", j=4)[:, j0:3, 0:n],
                                 AF.Exp, r=[("ps", bf_[hh]), "params"], w=[("PT", h)], bias=self.a_bc[:, l, h:h + 1])
                        if j0 == 0 and n > 64:
                            self.memset("gpsimd", self.PT[0:64, h, 0, 64:n], 0.0, w=[("PT", h)])
                near = [t for t in tiles if t[0] >= 3]
                for (jt, slot, nk) in near:
                    jj = jt - 3
                    src = self.ps[0:nk, bn, :].rearrange("p (h j q) -> p h j q", h=2, j=2)[:, :, jj, 0:n]
                    dst = self.PT[0:nk, hp * 2:hp * 2 + 2, jt, 0:n]
                    self.act(dst, src, AF.Exp, r=[("ps", bn)], w=[("PT", hp * 2), ("PT", hp * 2 + 1)])
                    self.tt("vector", dst, dst, self.Etab[0:nk, l, hp * 2:hp * 2 + 2, jj, 0:n], ALU.mult,
                            r=[("PT", hp * 2), ("PT", hp * 2 + 1), ("E", l)], w=[("PT", hp * 2), ("PT", hp * 2 + 1)])
            for h in range(8):
                ob = 4 + h // 4
                oview = self.ps[0:n, ob, :].rearrange("p (h d) -> p h d", d=128)[:, h % 4, 0:65]
                for ti, (jt, slot, nk) in enumerate(tiles):
                    self.mm(oview, self.PT[0:nk, h, jt, 0:n], self.VH[l][0:nk, slot, h, :], start=(ti == 0), stop=(ti == len(tiles) - 1),
                            r=[("PT", h), ("VH", l)], w=[("ps", ob)])
            for g in range(2):
                ob = 4 + g
                ov = self.ps[0:n, ob, :].rearrange("p (h d) -> p h d", d=128)
                self.S.op("vector", lambda e, o=self.rec8[0:n, g * 4:g * 4 + 4], i_=ov[:, :, 64]: e.reciprocal(out=o, in_=i_),
                          reads=[("ps", ob)], writes=["rec8"])
                self.tt("vector", self.yA[0:n, g * 256:(g + 1) * 256].rearrange("p (h d) -> p h d", h=4), ov[:, :, 0:64],
                        self.rec8[0:n, g * 4:g * 4 + 4].unsqueeze(2).broadcast_to([n, 4, 64]), ALU.mult,
                        r=[("ps", ob), "rec8"], w=["yA"])
            b = self.ptmp()
            for c in range(4):
                self.tr(self.ps[:, b, c * 128:c * 128 + n], self.yA[0:n, c * 128:(c + 1) * 128], r=["yA"], w=[("ps", b)])
            self.cp("scalar", self.brT[:, 0:4, q0:q0 + n], self.ps[:, b, :].rearrange("p (c t) -> p c t", c=4)[:, :, 0:n],
                    r=[("ps", b)], w=[("brT", c) for c in range(4)])
        self._nb = 8
        self.merge_branch(l, L, "w_br_a", 4, O_GATES, first=True)
    def raw_proj_conv(self, l, L, col_off, hist, hkey, cw, cb):
        self.cp("gpsimd", self.RAW[:, :, 0:3], hist[:, l, :, :], r=[hkey], w=[("RAW", c) for c in range(12)])
        for c in range(12):
            if c % 4 == 0:
                wv, wk = self.wload("w_in", l, [(col_off + c * 128, 512)])
            b = self.proj_fm(wv, wk, (c % 4) * 128, 128, L)
            rk, ck = ("RAW", c), ("CO", c)
            self.cp("scalar" if c % 2 else "vector", self.RAW[:, c, 3:3 + L], self.ps[:, b, 0:L], r=[("ps", b)], w=[rk])
            if cb is None:
                self.act(self.CO[:, c, 0:L], self.RAW[:, c, 3:3 + L], AF.Copy, r=[rk, "params"], w=[ck], scale=cw[:, l, 3, c:c + 1])
            else:
                self.act(self.CO[:, c, 0:L], self.RAW[:, c, 3:3 + L], AF.Identity, r=[rk, "params"], w=[ck],
                         scale=cw[:, l, 3, c:c + 1], bias=cb[:, l, c:c + 1])
            for k in range(3):
                self.stt(self.CO[:, c, 0:L], self.RAW[:, c, k:k + L], cw[:, l, k, c:c + 1], self.CO[:, c, 0:L], ALU.mult, ALU.add,
                         r=[rk, ck, "params"], w=[ck])
            if c > 0:
                self.act(self.CO[:, c - 1, 0:L], self.CO[:, c - 1, 0:L], AF.Silu, r=[("CO", c - 1)], w=[("CO", c - 1)])
        self.act(self.CO[:, 11, 0:L], self.CO[:, 11, 0:L], AF.Silu, r=[("CO", 11)], w=[("CO", 11)])
        self.cp("gpsimd", hist[:, l, :, :], self.RAW[:, :, L:L + 3], r=[("RAW", c) for c in range(12)], w=[hkey])

    GDN_TMPS = (("KVt", [128, 2, 2, 128]), ("DD", [128, 2, 2, 128]), ("EG", [128, 2, 128]),
                ("QKD", [128, 2, 128]), ("LL", [128, 2, 2, 128]), ("XX", [128, 2, 2, 128]), ("YY", [128, 2, 2, 128]),
                ("Rp", [128, 2, 128]), ("UU", [128, 2, 128]), ("qg", [128, 2, 128]), ("kd", [128, 2, 128]),
                ("gct", [128, 2]), ("egc", [128, 2]), ("wdec", [128, 2]))

    def phase_B(self, l, st, L):
        T = self.T
        with self.scope() as sc:
            self.CO = sc("CO", [128, 12, T])
            self.OB = sc("OB", [128, 4, T])
            with self.scope() as sc1:
                self.RAW = sc1("RAW", [128, 12, T + 3])
                self._phase_B1(l, st, L)
            with self.scope() as sc2:
                self.G = []
                for g in range(2):
                    d = {nm: sc2(f"{nm}{g}", shp) for nm, shp in self.GDN_TMPS}
                    d["tag"] = g
                    self.G.append(d)
                self._phase_B2(l, st, L)

    def _phase_B1(self, l, st, L):
        P = ["params"]
        nsub = (L + 127) // 128
        self.raw_proj_conv(l, L, O_BQKV, self.bhist, "bhist", self.bcw, None)
        wd, wdk = self.wload("w_in", l, [(O_BBETA, 8)])
        for j in range(nsub):
            n = min(128, L - j * 128)
            b = self.proj_tm(wd, wdk, 0, 8, j * 128, n)
            self.cp("vector", self.bd[0:n, j, :], self.ps[0:n, b, 0:8], r=[("ps", b)], w=["bd"])
        nfull = min(128, L)
        J = slice(0, nsub)
        self.act(self.bet[0:nfull, J, :], self.bd[0:nfull, J, 0:4], AF.Exp, r=["bd"], w=["bet"], scale=-1.0)
        self.ts("vector", self.bet[0:nfull, J, :], self.bet[0:nfull, J, :], 1.0, None, ALU.add, r=["bet"], w=["bet"])
        self.S.op("vector", lambda e, o=self.bet[0:nfull, J, :]: e.reciprocal(out=o, in_=o), reads=["bet"], writes=["bet"])
        self.ts("vector", self.nbet[0:nfull, J, :], self.bet[0:nfull, J, :], -1.0, None, ALU.mult, r=["bet"], w=["nbet"])
        self.tt("vector", self.gg[0:nfull, J, :], self.bd[0:nfull, J, 4:8],
                self.b_dtb[0:nfull, l:l + 1, :].broadcast_to([nfull, nsub, 4]), ALU.add, r=["bd"] + P, w=["gg"])
        self.act(self.gg[0:nfull, J, :], self.gg[0:nfull, J, :], AF.Exp, r=["gg"], w=["gg"])
        self.act(self.gg[0:nfull, J, :], self.gg[0:nfull, J, :], AF.Ln, r=["gg"], w=["gg"], bias=1.0)
        self.tt("vector", self.gg[0:nfull, J, :], self.gg[0:nfull, J, :],
                self.b_nA[0:nfull, l:l + 1, :].broadcast_to([nfull, nsub, 4]), ALU.mult, r=["gg"] + P, w=["gg"])
        cok = [("CO", c) for c in range(8)]
        rk8 = [("RAW", c) for c in range(8)]
        self.act(self.sqb[:, 0:8, 0:L], self.CO[:, 0:8, 0:L], AF.Square, r=cok, w=["sqb0", "sqb1"])
        for c2 in range(4):
            b = self.ptmp()
            pv = self.ps[:, b, :].rearrange("p (c t) -> p c t", c=2)
            for cc in range(2):
                self.mm(pv[:, cc, 0:L], self.ones_b[:], self.sqb[:, c2 * 2 + cc, 0:L], r=["ones", "sqb0", "sqb1"], w=[("ps", b)])
            rv = self.RAW[:, c2 * 2:c2 * 2 + 2, 0:L]
            self.act(rv, pv[:, :, 0:L], AF.Ln, r=[("ps", b), "ones"], w=rk8[c2 * 2:c2 * 2 + 2], bias=self.eps_col[:, 0:1])
            self.act(rv, rv, AF.Exp, r=rk8[c2 * 2:c2 * 2 + 2], w=rk8[c2 * 2:c2 * 2 + 2], scale=-0.5)
        self.stt(self.CO[:, 0:4, 0:L], self.CO[:, 0:4, 0:L], float(B_DH ** -0.5), self.RAW[:, 0:4, 0:L], ALU.mult, ALU.mult,
                 r=cok[0:4] + rk8[0:4], w=cok[0:4])
        self.tt("vector", self.CO[:, 4:8, 0:L], self.CO[:, 4:8, 0:L], self.RAW[:, 4:8, 0:L], ALU.mult, r=cok[4:8] + rk8[4:8], w=cok[4:8])

    def _phase_B2(self, l, st, L):
        P = ["params"]
        nsub = (L + 127) // 128
        self._nb = 4
        for j in range(nsub):
            C = min(128, L - j * 128)
            t0 = j * 128
            nl = max(1, (C - 1).bit_length())
            gens = [self.gdn_chunk(l, j, t0, C, nl, [2 * g, 2 * g + 1], self.G[g]) for g in range(2)]
            live = list(gens)
            while live:
                for g in list(live):
                    try:
                        next(g)
                    except StopIteration:
                        live.remove(g)
        self._nb = 8
        obk = [("OB", h) for h in range(4)]
        self.act(self.sqb[:, 0:4, 0:L], self.OB[:, :, 0:L], AF.Square, r=obk, w=["sqb0", "sqb1"])
        for h2 in range(2):
            b = self.ptmp()
            pv = self.ps[:, b, :].rearrange("p (c t) -> p c t", c=2)
            for cc in range(2):
                self.mm(pv[:, cc, 0:L], self.ones_b[:], self.sqb[:, h2 * 2 + cc, 0:L], r=["ones", "sqb0", "sqb1"], w=[("ps", b)])
            rv = self.sq[:, 4 + h2 * 2:6 + h2 * 2, 0:L]
            self.act(rv, pv[:, :, 0:L], AF.Ln, r=[("ps", b), "ones"], w=["sq2"], scale=1.0 / B_DH, bias=self.eps_col[:, 0:1])
            self.act(rv, rv, AF.Exp, r=["sq2"], w=["sq2"], scale=-0.5)
        for h in range(4):
            self.stt(self.OB[:, h, 0:L], self.OB[:, h, 0:L], self.g_bn[:, l:l + 1], self.sq[:, 4 + h, 0:L], ALU.mult, ALU.mult,
                     r=[("OB", h), "sq2"] + P, w=[("OB", h)])
        wg, wgk = self.wload("w_in", l, [(O_BGATE, 512)])
        for h in range(4):
            bg = self.proj_fm(wg, wgk, h * 128, 128, L)
            i2 = self.rot("gsig", 2)
            g2 = self.gsig[i2]
            self.act(g2[:, 0:L], self.ps[:, bg, 0:L], AF.Silu, r=[("ps", bg)], w=[("gsig", i2)])
            self.tt("vector", self.brT[:, h, 0:L], self.OB[:, h, 0:L], g2[:, 0:L], ALU.mult,
                    r=[("OB", h), ("gsig", i2)], w=[("brT", h)])
        self.merge_branch(l, L, "w_br_b", 4, O_GATES + D, first=False)

    def gdn_chunk(self, l, j, t0, C, nl, hs, G):
        HG = len(hs)
        h0 = hs[0]
        tg = G["tag"]
        K = lambda nm: (nm, tg)
        CON = ["consts"]
        KVt, DD, EG, QKD, LL, XX, YY = G["KVt"], G["DD"], G["EG"], G["QKD"], G["LL"], G["XX"], G["YY"]
        Rp, UU, qg, kd, gct, egc, wdec = G["Rp"], G["UU"], G["qg"], G["kd"], G["gct"], G["egc"], G["wdec"]
        b = self.ptmp()
        for i, h in enumerate(hs):
            self.tr(self.ps[0:C, b, i * 128:(i + 1) * 128], self.CO[:, 4 + h, t0:t0 + C], r=[("CO", 4 + h)], w=[("ps", b)])
            self.tr(self.ps[0:C, b, (HG + i) * 128:(HG + i + 1) * 128], self.CO[:, 8 + h, t0:t0 + C], r=[("CO", 8 + h)], w=[("ps", b)])
        self.cp("scalar", KVt[0:C].rearrange("p a h d -> p (a h d)"), self.ps[0:C, b, 0:2 * HG * 128], r=[("ps", b)], w=[K("KVt")])
        gsl = self.gg[0:C, j, h0:h0 + HG]
        yield
        bg = self.ptmp()
        gcv = self.ps[:, bg, 0:HG * 128].rearrange("p (h i) -> p h i", h=HG)
        for i in range(HG):
            self.mm(gcv[:, i, 0:C], gsl[:, i:i + 1].broadcast_to([C, 128]), self.triI[0:C, 0:C], r=["gg"] + CON, w=[("ps", bg)])
        self.mm(self.ps[0:C, bg, 384:384 + HG], self.triI[0:C, 0:C], gsl, r=["gg"] + CON, w=[("ps", bg)])
        self.cp("vector", gct[0:C, :], self.ps[0:C, bg, 384:384 + HG], r=[("ps", bg)], w=[K("gct")])
        for i in range(HG):
            self.stt(DD[0:C, 0, i, 0:C], gcv[0:C, i, 0:C], gct[0:C, i:i + 1], self.negU[0:C, 0:C], ALU.subtract, ALU.add,
                     r=[("ps", bg), K("gct")] + CON, w=[K("DD")])
            self.stt(DD[0:C, 1, i, 0:C], gcv[0:C, i, 0:C], gct[0:C, i:i + 1], self.zeros[0:C, 0:C], ALU.subtract, ALU.max,
                     r=[("ps", bg), K("gct"), "ones"], w=[K("DD")])
        self.tt("vector", wdec[0:C, :], gcv[0:C, :, C - 1], gct[0:C, :], ALU.subtract, r=[("ps", bg), K("gct")], w=[K("wdec")])
        self.act(EG[:, :, 0:C], gcv[:, :, 0:C], AF.Exp, r=[("ps", bg)], w=[K("EG")])
        self.act(egc[0:C, :], gct[0:C, :], AF.Exp, r=[K("gct")], w=[K("egc")])
        self.act(wdec[0:C, :], wdec[0:C, :], AF.Exp, r=[K("wdec")], w=[K("wdec")])
        self.act(DD[0:C, 0, :, 0:C], DD[0:C, 0, :, 0:C], AF.Exp, r=[K("DD")], w=[K("DD")])
        self.act(DD[0:C, 1, :, 0:C], DD[0:C, 1, :, 0:C], AF.Exp, r=[K("DD")], w=[K("DD")], scale=-1.0)
        self.tt("vector", DD[0:C, 1, :, 0:C], DD[0:C, 1, :, 0:C], self.triSL[0:C, 0:C].unsqueeze(1).broadcast_to([C, HG, C]),
                ALU.mult, r=[K("DD")] + CON, w=[K("DD")])
        yield
        bk = self.ptmp()
        kv = self.ps[:, bk, :].rearrange("p (a h i) -> p a h i", a=2, h=2)
        for i, h in enumerate(hs):
            kT = self.CO[:, 4 + h, t0:t0 + C]
            self.mm(kv[0:C, 0, i, 0:C], kT, kT, r=[("CO", 4 + h)], w=[("ps", bk)])
            self.mm(kv[0:C, 1, i, 0:C], kT, self.CO[:, h, t0:t0 + C], r=[("CO", 4 + h), ("CO", h)], w=[("ps", bk)])
        for i, h in enumerate(hs):
            self.stt(LL[0:C, i, 0, 0:C], kv[0:C, 0, i, 0:C], self.bet[0:C, j, h:h + 1], DD[0:C, 1, i, 0:C], ALU.mult, ALU.mult,
                     r=[("ps", bk), "bet", K("DD")], w=[K("LL")])
        self.tt("vector", QKD[0:C, :, 0:C], kv[0:C, 1, 0:HG, 0:C], DD[0:C, 0, :, 0:C], ALU.mult, r=[("ps", bk), K("DD")], w=[K("QKD")])
        yield
        bt = self.ptmp()
        for i in range(HG):
            self.tr(self.ps[0:C, bt, i * 128:i * 128 + C], LL[0:C, i, 0, 0:C], r=[K("LL")], w=[("ps", bt)])
        self.cp("scalar", LL[0:C, :, 1, 0:C], self.ps[0:C, bt, 0:HG * 128].rearrange("p (h i) -> p h i", h=HG)[:, :, 0:C],
                r=[("ps", bt)], w=[K("LL")])
        mm0 = self.mmask[0:C, 0, :, 0:C].unsqueeze(1).broadcast_to([C, HG, 2, C])
        self.tt("vector", XX[0:C, :, :, 0:C], LL[0:C, :, :, 0:C], mm0, ALU.mult, r=[K("LL")] + CON, w=[K("XX")])
        self.tt("vector", XX[0:C, :, :, 0:C], self.ii[0:C, :, 0:C].unsqueeze(1).broadcast_to([C, HG, 2, C]), XX[0:C, :, :, 0:C],
                ALU.subtract, r=[K("XX")] + CON, w=[K("XX")])
        yield
        for lev in range(1, nl):
            by = self.ptmp()
            yv = self.ps[:, by, :].rearrange("p (h a i) -> p h a i", h=2, a=2)
            last = (lev == nl - 1)
            for i in range(HG):
                if not last:
                    self.mm(yv[0:C, i, 0, 0:C], LL[0:C, i, 1, 0:C], XX[0:C, i, 0, 0:C], r=[K("LL"), K("XX")], w=[("ps", by)])
                self.mm(yv[0:C, i, 1, 0:C], LL[0:C, i, 0, 0:C], XX[0:C, i, 1, 0:C], r=[K("LL"), K("XX")], w=[("ps", by)])
            if last:
                mml = self.mmask[0:C, lev, 1, 0:C].unsqueeze(1).broadcast_to([C, HG, C])
                self.tt("vector", YY[0:C, :, 1, 0:C], yv[0:C, 0:HG, 1, 0:C], mml, ALU.mult, r=[("ps", by)] + CON, w=[K("YY")])
            else:
                mml = self.mmask[0:C, lev, :, 0:C].unsqueeze(1).broadcast_to([C, HG, 2, C])
                self.tt("vector", YY[0:C, :, :, 0:C], yv[0:C, 0:HG, :, 0:C], mml, ALU.mult, r=[("ps", by)] + CON, w=[K("YY")])
            yield
            bz = self.ptmp()
            zv = self.ps[:, bz, :].rearrange("p (h a i) -> p h a i", h=2, a=2)
            for i in range(HG):
                if not last:
                    self.mm(zv[0:C, i, 0, 0:C], XX[0:C, i, 1, 0:C], YY[0:C, i, 0, 0:C], r=[K("XX"), K("YY")], w=[("ps", bz)])
                self.mm(zv[0:C, i, 1, 0:C], XX[0:C, i, 0, 0:C], YY[0:C, i, 1, 0:C], r=[K("XX"), K("YY")], w=[("ps", bz)])
            if last:
                self.tt("vector", XX[0:C, :, 1, 0:C], XX[0:C, :, 1, 0:C], zv[0:C, 0:HG, 1, 0:C], ALU.subtract,
                        r=[K("XX"), ("ps", bz)], w=[K("XX")])
            else:
                self.tt("vector", XX[0:C, :, :, 0:C], XX[0:C, :, :, 0:C], zv[0:C, 0:HG, :, 0:C], ALU.subtract,
                        r=[K("XX"), ("ps", bz)], w=[K("XX")])
            yield
        bs = self.ptmp()
        sv = self.ps[:, bs, :].rearrange("p (h d) -> p h d", h=4)
        for i, h in enumerate(hs):
            self.mm(sv[0:C, i, :], self.CO[:, 4 + h, t0:t0 + C], self.Sst[l][:, h, :], r=[("CO", 4 + h), ("S", l, h)], w=[("ps", bs)])
        for i, h in enumerate(hs):
            self.stt(Rp[0:C, i, :], sv[0:C, i, :], egc[0:C, i:i + 1], KVt[0:C, 1, i, :], ALU.mult, ALU.subtract,
                     r=[("ps", bs), K("egc"), K("KVt")], w=[K("Rp")])
            self.act(Rp[0:C, i, :], Rp[0:C, i, :], AF.Copy, r=[K("Rp"), "nbet"], w=[K("Rp")], scale=self.nbet[0:C, j, h:h + 1])
        yield
        bu = self.ptmp()
        uv = self.ps[:, bu, :].rearrange("p (h d) -> p h d", h=4)
        for i in range(HG):
            self.mm(uv[0:C, i, :], XX[0:C, i, 1, 0:C], Rp[0:C, i, :], r=[K("XX"), K("Rp")], w=[("ps", bu)])
        self.cp("scalar", UU[0:C, :, :], uv[0:C, 0:HG, :], r=[("ps", bu)], w=[K("UU")])
        self.tt("gpsimd", qg[:, :, 0:C], self.CO[:, h0:h0 + HG, t0:t0 + C], EG[:, :, 0:C], ALU.mult,
                r=[("CO", h) for h in hs] + [K("EG")], w=[K("qg")])
        for i in range(HG):
            self.act(kd[0:C, i, :], KVt[0:C, 0, i, :], AF.Copy, r=[K("KVt"), K("wdec")], w=[K("kd")], scale=wdec[0:C, i:i + 1])
        yield
        bo = self.ptmp()
        ov = self.ps[:, bo, :].rearrange("p (h i) -> p h i", h=4)
        for i, h in enumerate(hs):
            self.mm(ov[:, i, 0:C], self.Sst[l][:, h, :], qg[:, i, 0:C], start=True, stop=False, r=[("S", l, h), K("qg")], w=[("ps", bo)])
            self.mm(ov[:, i, 0:C], UU[0:C, i, :], QKD[0:C, i, 0:C], start=False, stop=True, r=[K("UU"), K("QKD")], w=[("ps", bo)])
        self.cp("scalar", self.OB[:, h0:h0 + HG, t0:t0 + C], ov[:, 0:HG, 0:C], r=[("ps", bo)], w=[("OB", h) for h in hs])
        bn = self.ptmp()
        nv = self.ps[:, bn, :].rearrange("p (h d) -> p h d", h=4)
        for i in range(HG):
            self.mm(nv[:, i, :], kd[0:C, i, :], UU[0:C, i, :], r=[K("kd"), K("UU")], w=[("ps", bn)])
        for i, h in enumerate(hs):
            self.stt(self.Sst[l][:, h, :], self.Sst[l][:, h, :], EG[:, i, C - 1:C], nv[:, i, :], ALU.mult, ALU.add,
                     r=[("S", l, h), K("EG"), ("ps", bn)], w=[("S", l, h)])
        yield
    def phase_C(self, l, st, L):
        T = self.T
        P = ["params"]
        nsub = (L + 127) // 128
        with self.scope() as sc:
            self.CO = sc("CO", [128, 12, T])
            with self.scope() as sc1:
                self.RAW = sc1("RAW", [128, 12, T + 3])
                self.raw_proj_conv(l, L, O_CXBC, self.chist, "chist", self.ccw, self.ccb)
                wd, wdk = self.wload("w_in", l, [(O_CDT, 16)])
                for j in range(nsub):
                    n = min(128, L - j * 128)
                    b = self.proj_tm(wd, wdk, 0, 16, j * 128, n)
                    self.tt("vector", self.dtr[0:n, j, :], self.ps[0:n, b, 0:16], self.c_dtb[0:n, l, :], ALU.add, r=[("ps", b)] + P, w=["dtr"])
                    self.act(self.dtr[0:n, j, :], self.dtr[0:n, j, :], AF.Exp, r=["dtr"], w=["dtr"])
                    self.act(self.dtt[0:n, j, :], self.dtr[0:n, j, :], AF.Ln, r=["dtr"], w=["dtt"], bias=1.0)
                    self.tt("vector", self.aa[0:n, j, :], self.dtt[0:n, j, :], self.c_nA[0:n, l, :], ALU.mult, r=["dtt"] + P, w=["aa"])
            with self.scope() as sc2:
                for nm, shp in (("xtok", [128, 16, 64]), ("xdt", [128, 16, 64]), ("Btok", [128, 2, 128]), ("CBm", [128, 2, 128]),
                                ("act_", [128, 16]), ("nact", [128, 16]), ("eact", [128, 16]), ("AL", [128, 16]), ("EAL", [128, 16]), ("wst", [128, 16]),
                                ("Yt", [128, 16, 64]), ("xsk", [128, 16, 64]), ("YZ", [128, KD, T])):
                    setattr(self, nm, sc2(nm, shp))
                self.MTm = [sc2(f"MTm{i}", [128, 4, 128], BF16) for i in range(4)]
                self.xdtb = sc2("xdtb", [128, 16, 64], BF16)
                for j in range(nsub):
                    C = min(128, L - j * 128)
                    self.ssd_chunk(l, j, j * 128, C)
                for n in range(KD):
                    if n % 4 == 0:
                        wz, wzk = self.wload("w_in", l, [(O_CZ + n * 128, 512)])
                    b = self.proj_fm(wz, wzk, (n % 4) * 128, 128, L)
                    i = self.rot("gsig", 2)
                    gs = self.gsig[i]
                    self.act(gs[:, 0:L], self.ps[:, b, 0:L], AF.Silu, r=[("ps", b)], w=[("gsig", i)])
                    self.tt("vector", self.YZ[:, n, 0:L], self.YZ[:, n, 0:L], gs[:, 0:L], ALU.mult, r=[("YZ", n), ("gsig", i)], w=[("YZ", n)])
                self.rmsnorm(self.YZ, "YZ", self.g_cn[:, l, :], L, self.brT, "brT")
        self.merge_branch(l, L, "w_br_c", 8, O_GATES + 2 * D, first=False)

    def ssd_chunk(self, l, j, t0, C):
        self._nb = 4
        self._ssd_chunk(l, j, t0, C)
        self._nb = 8

    def _ssd_chunk(self, l, j, t0, C):
        CON = ["consts"]
        xk = [("CO", c) for c in range(8)]
        for g in range(2):
            b = self.ptmp()
            for cc in range(4):
                self.tr(self.ps[0:C, b, cc * 128:(cc + 1) * 128], self.CO[:, g * 4 + cc, t0:t0 + C], r=[("CO", g * 4 + cc)], w=[("ps", b)])
            self.cp("scalar" if g else "vector", self.xtok[0:C, g * 8:g * 8 + 8, :].rearrange("p h d -> p (h d)"), self.ps[0:C, b, :],
                    r=[("ps", b)], w=["xtok"])
        b = self.ptmp()
        for g in range(2):
            self.tr(self.ps[0:C, b, g * 128:(g + 1) * 128], self.CO[:, 8 + g, t0:t0 + C], r=[("CO", 8 + g)], w=[("ps", b)])
        for g in range(2):
            self.mm(self.ps[0:C, b, 256 + g * 128:256 + g * 128 + C], self.CO[:, 8 + g, t0:t0 + C], self.CO[:, 10 + g, t0:t0 + C],
                    r=[("CO", 8 + g), ("CO", 10 + g)], w=[("ps", b)])
        self.cp("scalar", self.Btok[0:C, :, :].rearrange("p g n -> p (g n)"), self.ps[0:C, b, 0:256], r=[("ps", b)], w=["Btok"])
        self.tt("vector", self.CBm[0:C, :, 0:C], self.ps[0:C, b, 256:512].rearrange("p (g i) -> p g i", g=2)[:, :, 0:C],
                self.triI[0:C, 0:C].unsqueeze(1).broadcast_to([C, 2, C]), ALU.mult, r=[("ps", b)] + CON, w=["CBm"])
        asl = self.aa[0:C, j, :]
        ba = self.ptmp()
        self.mm(self.ps[0:C, ba, 0:16], self.triI[0:C, 0:C], asl, r=["aa"] + CON, w=[("ps", ba)])
        self.cp("vector", self.act_[0:C, :], self.ps[0:C, ba, 0:16], r=[("ps", ba)], w=["act_"])
        self.act(self.eact[0:C, :], self.ps[0:C, ba, 0:16], AF.Exp, r=[("ps", ba)], w=["eact"])
        self.tt("gpsimd", self.xsk[0:C], self.xtok[0:C], self.c_dsk[0:C, l, :].unsqueeze(2).broadcast_to([C, 16, 64]), ALU.mult,
                r=["xtok", "params"], w=["xsk"])
        self.tt("vector", self.xdtb[0:C], self.xtok[0:C], self.dtt[0:C, j, :].unsqueeze(2).broadcast_to([C, 16, 64]), ALU.mult,
                r=["xtok", "dtt"], w=["xdtb"])
        for g in range(2):
            self.mm(self.ps[0:C, 6 + g, :], self.CO[:, 10 + g, t0:t0 + C], self.Hst[l][:, g * 8:g * 8 + 8, :].rearrange("n h p -> n (h p)"),
                    r=[("CO", 10 + g), ("H", l)], w=[("ps", 6 + g)])
        self.ts("vector", self.nact[0:C, :], self.act_[0:C, :], -1.0, None, ALU.mult, r=["act_"], w=["nact"])

        def stage1(hq):
            i = self.rot("MTm", 4)
            mt, mk = self.MTm[i], ("MTm", i)
            bb = self.ptmp()
            av = self.ps[:, bb, :].rearrange("p (h i) -> p h i", h=4)
            for hh in range(4):
                h = hq * 4 + hh
                self.mm(av[0:C, hh, 0:C], asl[:, h:h + 1].broadcast_to([C, C]), self.triI[0:C, 0:C], start=True, stop=False,
                        r=["aa"] + CON, w=[("ps", bb)])
                self.mm(av[0:C, hh, 0:C], self.identb[0:C, 0:C], self.negUb[0:C, 0:C], start=False, stop=True, r=["constsb"], w=[("ps", bb)])
            self.cp("scalar", self.AL[0:C, hq * 4:hq * 4 + 4], av[0:C, :, C - 1], r=[("ps", bb)], w=["AL"])
            for hh in range(4):
                h = hq * 4 + hh
                self.act(mt[0:C, hh, 0:C], av[0:C, hh, 0:C], AF.Exp, r=[("ps", bb), "nact"], w=[mk], bias=self.nact[0:C, h:h + 1])
            g = hq // 2
            self.tt("vector", mt[0:C, :, 0:C], mt[0:C, :, 0:C], self.CBm[0:C, g:g + 1, 0:C].broadcast_to([C, 4, C]), ALU.mult,
                    r=[mk, "CBm"], w=[mk])
            return mt, mk

        def stage2(hq, mt, mk):
            for hh in range(4):
                h = hq * 4 + hh
                ob = 4 + h // 8
                self.mm(self.ps[0:C, ob, (h % 8) * 64:(h % 8) * 64 + 64], mt[0:C, hh, 0:C], self.xdtb[0:C, h, :], r=[mk, "xdtb"], w=[("ps", ob)])

        prev = None
        for hq in range(4):
            cur = stage1(hq)
            if prev is not None:
                stage2(hq - 1, *prev)
            prev = cur
        stage2(3, *prev)
        for g in range(2):
            yv = self.Yt[0:C, g * 8:g * 8 + 8, :]
            self.tt("vector", yv, self.ps[0:C, 6 + g, :].rearrange("p (h d) -> p h d", h=8),
                    self.eact[0:C, g * 8:g * 8 + 8].unsqueeze(2).broadcast_to([C, 8, 64]), ALU.mult, r=[("ps", 6 + g), "eact"], w=[("Yt", g)])
            self.tt("vector", yv, yv, self.ps[0:C, 4 + g, :].rearrange("p (h d) -> p h d", h=8), ALU.add, r=[("Yt", g), ("ps", 4 + g)], w=[("Yt", g)])
            self.tt("vector", yv, yv, self.xsk[0:C, g * 8:g * 8 + 8, :], ALU.add, r=[("Yt", g), "xsk"], w=[("Yt", g)])
        self.tt("vector", self.wst[0:C, :], self.AL[0:C, :], self.act_[0:C, :], ALU.subtract, r=["AL", "act_"], w=["wst"])
        self.act(self.wst[0:C, :], self.wst[0:C, :], AF.Exp, r=["wst"], w=["wst"])
        self.tt("vector", self.wst[0:C, :], self.wst[0:C, :], self.dtt[0:C, j, :], ALU.mult, r=["wst", "dtt"], w=["wst"])
        self.tt("vector", self.xdt[0:C], self.xtok[0:C], self.wst[0:C, :].unsqueeze(2).broadcast_to([C, 16, 64]), ALU.mult,
                r=["xtok", "wst", "xdt"], w=["xdt"])
        bl = self.ptmp()
        self.mm(self.ps[:, bl, 0:16], self.ones_f[0:C, :], asl, r=["ones", "aa"], w=[("ps", bl)])
        self.act(self.EAL[:, :], self.ps[:, bl, 0:16], AF.Exp, r=[("ps", bl)], w=["EAL"])
        for g in range(2):
            bh = self.ptmp()
            self.mm(self.ps[:, bh, :], self.Btok[0:C, g, :], self.xdt[0:C, g * 8:g * 8 + 8, :].rearrange("p h d -> p (h d)"),
                    r=["Btok", "xdt"], w=[("ps", bh)])
            hv = self.Hst[l][:, g * 8:g * 8 + 8, :]
            self.tt("vector", hv, hv, self.EAL[:, g * 8:g * 8 + 8].unsqueeze(2).broadcast_to([128, 8, 64]), ALU.mult,
                    r=[("H", l), "EAL"], w=[("H", l)])
            self.tt("vector", hv, hv, self.ps[:, bh, :].rearrange("p (h d) -> p h d", h=8), ALU.add, r=[("H", l), ("ps", bh)], w=[("H", l)])
        for g in range(2):
            b = self.ptmp()
            for cc in range(4):
                c = g * 4 + cc
                self.tr(self.ps[:, b, cc * 128:cc * 128 + C], self.Yt[0:C, c * 2:c * 2 + 2, :].rearrange("p h d -> p (h d)"),
                        r=[("Yt", g)], w=[("ps", b)])
            self.cp("vector", self.YZ[:, g * 4:g * 4 + 4, t0:t0 + C], self.ps[:, b, :].rearrange("p (c t) -> p c t", c=4)[:, :, 0:C],
                    r=[("ps", b)], w=[("YZ", g * 4 + cc) for cc in range(4)])

    def phase_X(self, l, st, L):
        with self.scope() as sc:
            self.qxT = sc("qxT", [128, KD, self.T], BF16)
            self.PX = sc("PX", [128, 8, self.T], BF16)
            self.recx = [sc(f"recx{i}", [128, self.T]) for i in range(2)]
            self._phase_X(l, st, L)

    def _phase_X(self, l, st, L):
        for c in range(KD):
            self.cp("scalar" if c % 2 else "vector", self.qxT[:, c, 0:L], self.mT[:, c, 0:L], r=[("mT", c)], w=[("qxT", c)])
        self.resid_proj(l, L, "w_out", self.qxT, [("qxT", c) for c in range(KD)], KD, 2)
        self.rmsnorm(self.xT, "xT", self.g_x[:, l, :], L, self.hT, "hT")
        for n in range(KD):
            if n % 4 == 0:
                wv, wk = self.wload("wx_q", l, [(n * 128, 512)])
            b = self.proj_fm(wv, wk, (n % 4) * 128, 128, L)
            self.act(self.qxT[:, n, 0:L], self.ps[:, b, 0:L], AF.Copy, r=[("ps", b)], w=[("qxT", n)], scale=float(X_DH ** -0.5))
        for h in range(4):
            for mt in range(2):
                b = self.ptmp()
                for dc in range(2):
                    self.mm(self.ps[:, b, 0:L], self.MKT[l][:, h * 2 + dc, mt * 128:(mt + 1) * 128], self.qxT[:, h * 2 + dc, 0:L],
                            start=(dc == 0), stop=(dc == 1), r=[("MKT", l), ("qxT", h * 2 + dc)], w=[("ps", b)])
                self.act(self.PX[:, h * 2 + mt, 0:L], self.ps[:, b, 0:L], AF.Exp, r=[("ps", b)], w=[("PX", h * 2 + mt)])
            bd = self.ptmp()
            for mt in range(2):
                self.mm(self.ps[:, bd, 0:L], self.ones_b[:], self.PX[:, h * 2 + mt, 0:L], start=(mt == 0), stop=(mt == 1),
                        r=["ones", ("PX", h * 2 + mt)], w=[("ps", bd)])
            i = self.rot("recx", 2)
            rx = self.recx[i]
            self.act(rx[:, 0:L], self.ps[:, bd, 0:L], AF.Ln, r=[("ps", bd)], w=[("recx", i)])
            self.act(rx[:, 0:L], rx[:, 0:L], AF.Exp, r=[("recx", i)], w=[("recx", i)], scale=-1.0)
            for dc in range(2):
                b = self.ptmp()
                for mt in range(2):
                    self.mm(self.ps[:, b, 0:L], self.MV[l][:, mt, (h * 2 + dc) * 128:(h * 2 + dc + 1) * 128], self.PX[:, h * 2 + mt, 0:L],
                            start=(mt == 0), stop=(mt == 1), r=[("MV", l), ("PX", h * 2 + mt)], w=[("ps", b)])
                self.tt("vector", self.brT[:, h * 2 + dc, 0:L], self.ps[:, b, 0:L], rx[:, 0:L], ALU.mult,
                        r=[("ps", b), ("recx", i)], w=[("brT", h * 2 + dc)])
        self.resid_proj(l, L, "wx_o", self.brT, [("brT", c) for c in range(KD)], KD, 2)

    def phase_F(self, l, st, L):
        with self.scope() as sc:
            self.gT = sc("gT", [128, KFF, self.T], BF16)
            self.vtmp = [sc(f"vtmp{i}", [128, self.T + 2]) for i in range(3)]
            self.ctmp = [sc(f"ctmp{i}", [128, self.T]) for i in range(3)]
            self._phase_F(l, st, L)

    def _phase_F(self, l, st, L):
        self.rmsnorm(self.xT, "xT", self.g_ffn[:, l, :], L, self.hT, "hT")
        P = ["params"]
        pend = None
        for blk in range(KFF // 2):
            wv, wk = self.wload("w_up", l, [(blk * 256, 256), (D_FF + blk * 256, 256)])
            for kk in range(2):
                k = blk * 2 + kk
                bu = self.proj_fm(wv, wk, kk * 128, 128, L)
                bv = self.proj_fm(wv, wk, 256 + kk * 128, 128, L)
                i = self.rot("vtmp", 3)
                vt, ct = self.vtmp[i], self.ctmp[i]
                vk, ck = ("vtmp", i), ("ctmp", i)
                self.cp("gpsimd", vt[:, 0:2], self.fhist[:, l, k, :], r=["fhist"], w=[vk])
                self.cp("scalar", vt[:, 2:2 + L], self.ps[:, bv, 0:L], r=[("ps", bv)], w=[vk])
                self.cp("gpsimd", self.fhist[:, l, k, :], vt[:, L:L + 2], r=[vk], w=["fhist"])
                self.act(ct[:, 0:L], self.ps[:, bv, 0:L], AF.Identity, r=[("ps", bv)] + P, w=[ck],
                         scale=self.fcw[:, l, 2, k:k + 1], bias=self.fcb[:, l, k:k + 1])
                self.stt(ct[:, 0:L], vt[:, 1:1 + L], self.fcw[:, l, 1, k:k + 1], ct[:, 0:L], ALU.mult, ALU.add, r=[vk, ck] + P, w=[ck])
                self.stt(ct[:, 0:L], vt[:, 0:L], self.fcw[:, l, 0, k:k + 1], ct[:, 0:L], ALU.mult, ALU.add, r=[vk, ck] + P, w=[ck])
                if pend is not None:
                    self.ffn_tail(*pend, L)
                pend = (k, bu, ct, ck)
        self.ffn_tail(*pend, L)
        self.resid_proj(l, L, "w_down", self.gT, [("gT", k) for k in range(KFF)], KFF, 8)

    def ffn_tail(self, k, bu, ct, ck, L):
        self.act(ct[:, 0:L], ct[:, 0:L], AF.Silu, r=[ck], w=[ck])
        self.tt("vector", self.gT[:, k, 0:L], self.ps[:, bu, 0:L], ct[:, 0:L], ALU.mult, r=[("ps", bu), ck], w=[("gT", k)])

    def store_hist(self, dst, hist, hkey, nch, k):
        for c0 in range(0, nch, 8):
            nc8 = min(8, nch - c0)
            i = self.rot("io", 2)
            xi, xk = self.io[i], ("io", i)
            for g0 in range(0, nc8, 4):
                ng = min(4, nc8 - g0)
                b = self.ptmp()
                for cc in range(ng):
                    self.tr(self.ps[0:k, b, cc * 128:(cc + 1) * 128], hist[:, c0 + g0 + cc, :], r=[hkey], w=[("ps", b)])
                self.cp("vector", xi[0:k, g0 * 128:(g0 + ng) * 128], self.ps[0:k, b, 0:ng * 128], r=[("ps", b)], w=[xk])
            self.dma("sync", dst[:, c0 * 128:(c0 + nc8) * 128], xi[0:k, 0:nc8 * 128], r=[xk])

    def store_states(self, st):
        o = st["outs"]
        for l in range(2):
            self.store_hist(o["b_conv"][l], self.bhist[:, l], "bhist", 12, 3)
            self.store_hist(o["c_conv"][l], self.chist[:, l], "chist", 12, 3)
            self.store_hist(o["ffn_conv"][l], self.fhist[:, l], "fhist", KFF, 2)
            self.dma("sync", o["b_rec"][l].rearrange("h k v -> k h v"), self.Sst[l][:], r=[("S", l, h) for h in range(4)])
            dst = o["c_ssm"][l].rearrange("h p n -> (h p) n").rearrange("(c q) n -> q c n", q=128)
            for g in range(2):
                b = self.ptmp()
                for cc in range(4):
                    c = g * 4 + cc
                    self.tr(self.ps[:, b, cc * 128:(cc + 1) * 128], self.Hst[l][:, c * 2:c * 2 + 2, :].rearrange("n h p -> n (h p)"),
                            r=[("H", l)], w=[("ps", b)])
                i = self.rot("io", 2)
                xi, xk = self.io[i], ("io", i)
                self.cp("vector", xi[:, 0:512], self.ps[:, b, :], r=[("ps", b)], w=[xk])
                self.dma("sync", dst[:, g * 4:g * 4 + 4, :], xi[:, 0:512].rearrange("q (c n) -> q c n", c=4), r=[xk])

    def run_stream(self, st):
        T = self.T
        self.init_stream(st)
        Ls = st["L"]
        for t0 in range(0, Ls, T):
            L = min(T, Ls - t0)
            self.load_x_tile(st["x"][t0:t0 + L, :], L)
            ph = self.dbg.get("phases", "ABCXF")
            for l in range(2):
                self.rmsnorm(self.xT, "xT", self.g_mix[:, l, :], L, self.hT, "hT")
                if "A" in ph:
                    self.phase_A(l, st, t0, L)
                else:
                    self.memset("vector", self.mT[:], 0.0, w=[("mT", c) for c in range(KD)])
                if "B" in ph:
                    self.phase_B(l, st, L)
                if "C" in ph:
                    self.phase_C(l, st, L)
                if "X" in ph:
                    self.phase_X(l, st, L)
                if "F" in ph:
                    self.phase_F(l, st, L)
            self.rmsnorm(self.xT, "xT", self.g_fin, L, self.mT, "mT")
            self.store_tm(st["y"][t0:t0 + L, :], self.mT, "mT", KD, L)
        if self.dbg.get("states", True):
            self.store_states(st)

    def build(self):
        self.prepass()
        self.alloc()
        self.setup()
        out, inp = self.out, self.inp
        for s in range(self.NP):
            st = {"kind": "p", "idx": s, "L": self.SEQ, "x": inp["x_prompt"][s], "y": out["y_prompt"][s],
                  "keep_lo": self.SEQ - self.KEEP,
                  "ak_out": [out["attn_k_prompt"][l, s] for l in range(2)], "av_out": [out["attn_v_prompt"][l, s] for l in range(2)],
                  "outs": {k: [out[k + "_prompt"][l, s] for l in range(2)] for k in ("b_conv", "b_rec", "c_conv", "c_ssm", "ffn_conv")}}
            self.run_stream(st)
        if self.sample:
            st = {"kind": "s", "idx": 0, "L": 16, "x": inp["x_sample"], "y": out["y_sample"], "keep_lo": 0,
                  "ak_out": [out["attn_k_sample"][l] for l in range(2)], "av_out": [out["attn_v_sample"][l] for l in range(2)],
                  "outs": {k: [out[k + "_sample"][l] for l in range(2)] for k in ("b_conv", "b_rec", "c_conv", "c_ssm", "ffn_conv")}}
            self.run_stream(st)
        self.S.wait_all("sync")
        return self.S.emit()


_CACHE = {}

N_CORES = 8
NP_CORE = 4
SEQ_FULL = 2048


def _get_builder():
    if "b" not in _CACHE:
        b = Builder(NP_CORE, SEQ_FULL, T=256, sample=True)
        b.build()
        _CACHE["b"] = b
    return _CACHE["b"]


def kernel(**inputs):
    b = _get_builder()
    f32 = lambda a: np.ascontiguousarray(np.asarray(a, dtype=np.float32))
    consts = {"c_" + k: v for k, v in make_consts().items()}
    shared = {}
    for k in list(WEIGHT_SHAPES.keys()) + list(SMALL_SHAPES.keys()):
        shared[k] = f32(inputs[k])
    in_maps = []
    for i in range(N_CORES):
        m = dict(shared)
        m.update(consts)
        m["x_prompt"] = f32(inputs["x_prompt"][i * NP_CORE:(i + 1) * NP_CORE])
        m["mem_prompt"] = f32(inputs["mem_prompt"][i * NP_CORE:(i + 1) * NP_CORE])
        m["x_sample"] = f32(inputs["x_sample"][i])
        m["cache_attn_k"] = f32(np.asarray(inputs["cache_attn_k"])[:, i]).reshape(2, 512, 512)
        m["cache_attn_v"] = f32(np.asarray(inputs["cache_attn_v"])[:, i]).reshape(2, 512, 512)
        m["state_b_conv"] = f32(np.asarray(inputs["state_b_conv"])[:, i])
        m["state_b_rec"] = f32(np.asarray(inputs["state_b_rec"])[:, i])
        m["state_c_conv"] = f32(np.asarray(inputs["state_c_conv"])[:, i])
        m["state_c_ssm"] = f32(np.asarray(inputs["state_c_ssm"])[:, i])
        m["state_ffn_conv"] = f32(np.asarray(inputs["state_ffn_conv"])[:, i])
        m["cache_mem_k"] = f32(np.asarray(inputs["cache_mem_k"])[:, i]).reshape(2, N_MEM, D)
        m["cache_mem_v"] = f32(np.asarray(inputs["cache_mem_v"])[:, i]).reshape(2, N_MEM, D)
        in_maps.append({k: v for k, v in m.items() if k in b.inp})
    res = run_bass_kernel_spmd(b.nc, in_maps, core_ids=list(range(N_CORES))).results
    B = N_CORES * NP_CORE
    cat0 = lambda nm: np.concatenate([r[nm] for r in res], axis=0)
    cat1 = lambda nm: np.concatenate([r[nm] for r in res], axis=1)
    stk0 = lambda nm: np.stack([r[nm] for r in res], axis=0)
    stk1 = lambda nm: np.stack([r[nm] for r in res], axis=1)
    outs = (
        cat0("y_prompt"),
        stk0("y_sample"),
        cat1("attn_k_prompt").reshape(2, B, 512, A_H, A_DH),
        cat1("attn_v_prompt").reshape(2, B, 512, A_H, A_DH),
        cat1("b_conv_prompt"),
        cat1("b_rec_prompt"),
        cat1("c_conv_prompt"),
        cat1("c_ssm_prompt"),
        cat1("ffn_conv_prompt"),
        cat1("mem_k_prompt").reshape(2, B, N_MEM, X_H, X_DH),
        cat1("mem_v_prompt").reshape(2, B, N_MEM, X_H, X_DH),
        stk1("attn_k_sample").reshape(2, N_CORES, 16, A_H, A_DH),
        stk1("attn_v_sample").reshape(2, N_CORES, 16, A_H, A_DH),
        stk1("b_conv_sample"),
        stk1("b_rec_sample"),
        stk1("c_conv_sample"),
        stk1("c_ssm_sample"),
        stk1("ffn_conv_sample"),
    )
    return tuple(np.ascontiguousarray(o, dtype=np.float32) for o in outs)
```
